# Optimizing a Trainium2 kernel written in Bass

```python
import jax, jax.numpy as jnp
from jax import lax
import numpy as np

D_MODEL = 1024
BATCH = 4
SEQ = 8192
DEPTH = 2

N_AB_LAYERS = (DEPTH + 1) // 2
N_C_LAYERS = DEPTH // 2

DN_HEADS = 4
DN_HEAD_DIM = 128
DN_KEY = DN_HEADS * DN_HEAD_DIM
DN_VAL = DN_HEADS * DN_HEAD_DIM
DN_CONV = 4
DN_CHUNK = 64
SG_GROUPS = 4
SG_GROUP_DIM = 128
SG_WIDTH = SG_GROUPS * SG_GROUP_DIM
SG_CHUNK = 128
POOL_WINDOWS = (2, 4, 8, 16)
POOL_GROUP_DIM = D_MODEL // len(POOL_WINDOWS)
D_FF = 2816
NORM_EPS = 1e-6

MIX_WIDTH = DN_VAL + SG_WIDTH
PROJ_SIZES = (DN_KEY, DN_KEY, DN_VAL, DN_VAL, DN_HEADS, DN_HEADS, SG_WIDTH, SG_WIDTH)
IN_PROJ = int(sum(PROJ_SIZES))
SPLIT_POINTS = tuple(int(s) for s in np.cumsum(PROJ_SIZES)[:-1])
QKV_WIDTH = 2 * DN_KEY + DN_VAL

kernel_name = "hybrid_deltanet_sgmlp_pool_macaron"


def _rmsnorm(x, w):
    xf = x.astype(jnp.float32)
    y = xf * lax.rsqrt(jnp.mean(xf * xf, axis=-1, keepdims=True) + NORM_EPS)
    return (y * w.astype(jnp.float32)).astype(x.dtype)


def _l2norm(x):
    return x * lax.rsqrt(jnp.sum(x * x, axis=-1, keepdims=True) + NORM_EPS)


def _swiglu(h, w_in, w_out):
    gate, up = jnp.split(h @ w_in, 2, axis=-1)
    return (jax.nn.silu(gate) * up) @ w_out


def _causal_dwconv(x, w):
    c = x.shape[-1]
    return lax.conv_general_dilated(
        x, w[:, None, :].astype(x.dtype), window_strides=(1,), padding=[(w.shape[0] - 1, 0)],
        dimension_numbers=("NWC", "WIO", "NWC"), feature_group_count=c)


def _gated_delta_rule(q, k, v, g, beta):
    bsz, t, h, dk = q.shape
    dv = v.shape[-1]
    n, c = t // DN_CHUNK, DN_CHUNK

    def chunk(a):
        return jnp.moveaxis(a.reshape((bsz, n, c, h) + a.shape[3:]), 3, 1)

    q = chunk(_l2norm(q) * dk ** -0.5)
    k = chunk(_l2norm(k))
    v = chunk(v)
    beta = chunk(beta)
    g = jnp.cumsum(chunk(g), axis=-1)
    causal = jnp.tril(jnp.ones((c, c), dtype=bool))
    strict = jnp.tril(jnp.ones((c, c), dtype=bool), k=-1)
    diff = g[..., :, None] - g[..., None, :]
    decay = jnp.where(causal, jnp.exp(jnp.where(causal, diff, 0.0)), 0.0)
    k_beta = k * beta[..., None]
    lower = jnp.where(strict, jnp.einsum('bhncd,bhnsd->bhncs', k_beta, k) * decay, 0.0)
    eye = jnp.eye(c, dtype=q.dtype)
    rhs = jnp.concatenate([v * beta[..., None], k_beta * jnp.exp(g)[..., None]], axis=-1)
    sol = lax.linalg.triangular_solve(eye + lower, rhs, left_side=True, lower=True, unit_diagonal=True)
    u, w = sol[..., :dv], sol[..., dv:]
    attn = jnp.einsum('bhncd,bhnsd->bhncs', q, k) * decay
    q_dec = q * jnp.exp(g)[..., None]
    g_last = g[..., -1]
    k_tail = k * jnp.exp(g_last[..., None] - g)[..., None]

    def step(state, inp):
        q_i, w_i, u_i, a_i, k_i, gl_i = inp
        v_new = u_i - jnp.einsum('bhck,bhkv->bhcv', w_i, state)
        o_i = jnp.einsum('bhck,bhkv->bhcv', q_i, state) + jnp.einsum('bhcs,bhsv->bhcv', a_i, v_new)
        state = state * jnp.exp(gl_i)[..., None, None] + jnp.einsum('bhck,bhcv->bhkv', k_i, v_new)
        return state, o_i

    xs = tuple(jnp.moveaxis(a, 2, 0) for a in (q_dec, w, u, attn, k_tail, g_last))
    state0 = jnp.zeros((bsz, h, dk, dv), q.dtype)
    _, o = lax.scan(step, state0, xs)
    return jnp.transpose(o, (1, 0, 3, 2, 4)).reshape(bsz, t, h, dv)


def _hybrid_ab_mixer(h, w_in, conv_w, a_log, dt_bias, dn_norm, sg_norm, sg_w, sg_b, w_out):
    bsz, t, _ = h.shape
    q, k, v, z, b, a, su, sv = jnp.split(h @ w_in, SPLIT_POINTS, axis=-1)
    qkv = jax.nn.silu(_causal_dwconv(jnp.concatenate([q, k, v], axis=-1), conv_w)).astype(jnp.float32)
    q, k, v = jnp.split(qkv, (DN_KEY, 2 * DN_KEY), axis=-1)
    q = q.reshape(bsz, t, DN_HEADS, DN_HEAD_DIM)
    k = k.reshape(bsz, t, DN_HEADS, DN_HEAD_DIM)
    v = v.reshape(bsz, t, DN_HEADS, DN_HEAD_DIM)
    beta = jax.nn.sigmoid(b.astype(jnp.float32))
    g = -jnp.exp(a_log.astype(jnp.float32)) * jax.nn.softplus(a.astype(jnp.float32) + dt_bias.astype(jnp.float32))
    o = _gated_delta_rule(q, k, v, g, beta)
    zf = z.astype(jnp.float32).reshape(bsz, t, DN_HEADS, DN_HEAD_DIM)
    o = _rmsnorm(o, dn_norm) * jax.nn.silu(zf)
    o_a = o.reshape(bsz, t, DN_VAL).astype(h.dtype)
    su = jax.nn.gelu(su, approximate=False).reshape(bsz, t, SG_GROUPS, SG_GROUP_DIM)
    sv = _rmsnorm(jax.nn.gelu(sv, approximate=False).reshape(bsz, t, SG_GROUPS, SG_GROUP_DIM), sg_norm)
    sv = sv.reshape(bsz, t // SG_CHUNK, SG_CHUNK, SG_GROUPS, SG_GROUP_DIM)
    tri = jnp.tril(jnp.ones((SG_CHUNK, SG_CHUNK), dtype=bool))
    w_s = jnp.where(tri, sg_w, 0.0).astype(sv.dtype)
    mixed = jnp.einsum('gts,bnsgc->bntgc', w_s, sv) + jnp.transpose(sg_b)[:, :, None].astype(sv.dtype)
    o_b = (su * mixed.reshape(bsz, t, SG_GROUPS, SG_GROUP_DIM)).reshape(bsz, t, SG_WIDTH)
    return jnp.concatenate([o_a, o_b], axis=-1) @ w_out


def _pool_mixer(h, pool_w, pool_scale):
    t = h.shape[1]
    hf = h.astype(jnp.float32)
    csum = jnp.cumsum(hf, axis=1)
    pos = jnp.arange(1, t + 1)
    outs = []
    for gi, win in enumerate(POOL_WINDOWS):
        sl = slice(gi * POOL_GROUP_DIM, (gi + 1) * POOL_GROUP_DIM)
        cs = csum[..., sl]
        lag = jnp.pad(cs[:, :-win], ((0, 0), (win, 0), (0, 0)))
        count = jnp.minimum(pos, win).astype(jnp.float32)[None, :, None]
        pooled = ((cs - lag) / count - hf[..., sl]).astype(h.dtype)
        outs.append(pooled @ pool_w[gi])
    return jnp.concatenate(outs, axis=-1) * pool_scale


def setup_inputs(seed: int = 0) -> dict:
    key = jax.random.key(seed)
    ks = jax.random.split(key, 24)
    f32 = jnp.float32
    nrm = lambda k, s, sc: jax.random.normal(k, s, f32) * sc
    gain = lambda k, s: 1.0 + 0.02 * jax.random.normal(k, s, f32)
    x = jax.random.normal(ks[0], (BATCH, SEQ, D_MODEL), f32)
    ffn_norm1 = gain(ks[1], (DEPTH, D_MODEL))
    ffn1_w_in = nrm(ks[2], (DEPTH, D_MODEL, 2 * D_FF), D_MODEL ** -0.5)
    ffn1_w_out = nrm(ks[3], (DEPTH, D_FF, D_MODEL), D_FF ** -0.5)
    mix_norm = gain(ks[4], (DEPTH, D_MODEL))
    ffn_norm2 = gain(ks[5], (DEPTH, D_MODEL))
    ffn2_w_in = nrm(ks[6], (DEPTH, D_MODEL, 2 * D_FF), D_MODEL ** -0.5)
    ffn2_w_out = nrm(ks[7], (DEPTH, D_FF, D_MODEL), D_FF ** -0.5)
    ab_w_in = nrm(ks[8], (N_AB_LAYERS, D_MODEL, IN_PROJ), D_MODEL ** -0.5)
    dn_conv_w = nrm(ks[9], (N_AB_LAYERS, DN_CONV, QKV_WIDTH), DN_CONV ** -0.5)
    dn_a_log = jnp.log(jax.random.uniform(ks[10], (N_AB_LAYERS, DN_HEADS), f32, 1.0, 16.0))
    dt = jnp.exp(jax.random.uniform(ks[11], (N_AB_LAYERS, DN_HEADS), f32, np.log(1e-3), np.log(1e-1)))
    dn_dt_bias = dt + jnp.log(-jnp.expm1(-dt))
    dn_out_norm = gain(ks[12], (N_AB_LAYERS, DN_HEAD_DIM))
    sg_norm = gain(ks[13], (N_AB_LAYERS, SG_GROUPS, SG_GROUP_DIM))
    sg_w = nrm(ks[14], (N_AB_LAYERS, SG_GROUPS, SG_CHUNK, SG_CHUNK), SG_CHUNK ** -0.5)
    sg_b = 1.0 + 0.1 * jax.random.normal(ks[15], (N_AB_LAYERS, SG_GROUPS, SG_CHUNK), f32)
    ab_w_out = nrm(ks[16], (N_AB_LAYERS, MIX_WIDTH, D_MODEL), MIX_WIDTH ** -0.5)
    pool_w = nrm(ks[17], (N_C_LAYERS, len(POOL_WINDOWS), POOL_GROUP_DIM, POOL_GROUP_DIM), POOL_GROUP_DIM ** -0.5)
    pool_scale = gain(ks[18], (N_C_LAYERS, D_MODEL))
    final_norm = gain(ks[19], (D_MODEL,))
    return {"x": x, "ffn_norm1": ffn_norm1, "ffn1_w_in": ffn1_w_in, "ffn1_w_out": ffn1_w_out,
            "mix_norm": mix_norm, "ffn_norm2": ffn_norm2, "ffn2_w_in": ffn2_w_in, "ffn2_w_out": ffn2_w_out,
            "ab_w_in": ab_w_in, "dn_conv_w": dn_conv_w, "dn_a_log": dn_a_log, "dn_dt_bias": dn_dt_bias,
            "dn_out_norm": dn_out_norm, "sg_norm": sg_norm, "sg_w": sg_w, "sg_b": sg_b, "ab_w_out": ab_w_out,
            "pool_w": pool_w, "pool_scale": pool_scale, "final_norm": final_norm}


def reference(x, ffn_norm1, ffn1_w_in, ffn1_w_out, mix_norm, ffn_norm2, ffn2_w_in, ffn2_w_out,
              ab_w_in, dn_conv_w, dn_a_log, dn_dt_bias, dn_out_norm, sg_norm, sg_w, sg_b, ab_w_out,
              pool_w, pool_scale, final_norm):
    for l in range(DEPTH):
        x = x + 0.5 * _swiglu(_rmsnorm(x, ffn_norm1[l]), ffn1_w_in[l], ffn1_w_out[l])
        h = _rmsnorm(x, mix_norm[l])
        i = l // 2
        if l % 2 == 0:
            x = x + _hybrid_ab_mixer(h, ab_w_in[i], dn_conv_w[i], dn_a_log[i], dn_dt_bias[i], dn_out_norm[i],
                                     sg_norm[i], sg_w[i], sg_b[i], ab_w_out[i])
        else:
            x = x + _pool_mixer(h, pool_w[i], pool_scale[i])
        x = x + 0.5 * _swiglu(_rmsnorm(x, ffn_norm2[l]), ffn2_w_in[l], ffn2_w_out[l])
    return _rmsnorm(x, final_norm)
```

```python
import numpy as np
import concourse.bass as bass
import concourse.mybir as mybir
from concourse.bass_utils import run_bass_kernel_spmd
from contextlib import ExitStack

F32 = mybir.dt.float32
BF16 = mybir.dt.bfloat16
AF = mybir.ActivationFunctionType
ALU = mybir.AluOpType

import os as _os0
SAME_ENG_SYNC = _os0.environ.get("SES", "1") == "1"
EPOCH = 12000


class Prog:
    CENGS = ("pe", "act", "dve", "pool", "sp")

    def __init__(self):
        self.ops = []
        self.lastw = {}
        self.readers = {}
        self.dma_cnt = {}
        self.pending = {}
        self.read_hook = None

    def fence(self, dma=True):
        last = {}
        for i, op in enumerate(self.ops):
            if op["fn"] is None:
                continue
            if op["dsem"] is not None and not dma:
                continue
            k = ("d", op["dsem"]) if op["dsem"] is not None else ("e", op["eng"])
            last[k] = i
        self.pending = {e: set(last.values()) for e in self.CENGS}

    def add(self, eng, fn, r=(), w=(), dsem=None):
        i = len(self.ops)
        deps = set()
        if eng != "pe" and self.read_hook is not None:
            for k in r:
                if isinstance(k, str) and k.startswith("ps"):
                    self.read_hook(k)
        if eng != "pe":
            w = list(w) + [k for k in r if isinstance(k, str) and k.startswith("ps") and k not in w]
        if eng in self.pending:
            deps |= self.pending.pop(eng)
        for k in r:
            d = self.lastw.get(k)
            if d is not None:
                deps.add(d)
        for k in w:
            d = self.lastw.get(k)
            if d is not None:
                deps.add(d)
            for d in self.readers.get(k, ()):
                deps.add(d)
        for k in r:
            self.readers.setdefault(k, []).append(i)
        for k in w:
            self.lastw[k] = i
            self.readers[k] = []
        op = dict(eng=eng, fn=fn, deps=deps, dsem=dsem, signal=False, val=None, key=None)
        if dsem is not None:
            self.dma_cnt[dsem] = self.dma_cnt.get(dsem, 0) + 1
            op["val"] = 16 * self.dma_cnt[dsem]
            op["key"] = ("d", dsem)
        self.ops.append(op)
        return i

    def finalize(self):
        ops = self.ops
        for op in ops:
            keep = set()
            for d in op["deps"]:
                dop = ops[d]
                if dop["dsem"] is None:
                    if dop["eng"] == op["eng"] and op["dsem"] is None:
                        if op["eng"] == "pe" or not SAME_ENG_SYNC:
                            continue
                    dop["signal"] = True
                keep.add(d)
            op["deps"] = keep
        cnt = {e: 0 for e in self.CENGS}
        self.ekeys = set()
        for op in ops:
            if op["dsem"] is None and op["signal"]:
                c = cnt[op["eng"]]
                cnt[op["eng"]] += 1
                op["key"] = ("e", op["eng"], c // EPOCH)
                op["val"] = c % EPOCH + 1
                self.ekeys.add(op["key"])
        wm = {e: {} for e in self.CENGS}
        for op in ops:
            need = {}
            for d in op["deps"]:
                dop = ops[d]
                need[dop["key"]] = max(need.get(dop["key"], 0), dop["val"])
            waits = []
            for key, v in need.items():
                if wm[op["eng"]].get(key, 0) >= v:
                    continue
                wm[op["eng"]][key] = v
                waits.append((key, v))
            op["waits"] = waits

    def emit(self, nc):
        self.finalize()
        with ExitStack() as es:
            sems = {}
            for k in sorted(self.ekeys):
                sems[k] = es.enter_context(nc.semaphore("s_%s_%d" % (k[1], k[2])))
            for k in self.dma_cnt:
                sems[("d", k)] = es.enter_context(nc.semaphore("d_%s" % k))
            block = es.enter_context(nc.Block())

            def run(ename):
                def f(eng):
                    for op in self.ops:
                        if op["eng"] != ename:
                            continue
                        for key, v in op["waits"]:
                            eng.wait_ge(sems[key], v)
                        if op["fn"] is None:
                            continue
                        ins = op["fn"](eng)
                        if op["dsem"] is not None:
                            ins.then_inc(sems[op["key"]], 16)
                        elif op["signal"]:
                            ins.then_inc(sems[op["key"]], 1)
                return f

            block.tensor(run("pe"))
            block.scalar(run("act"))
            block.vector(run("dve"))
            block.gpsimd(run("pool"))
            block.sync(run("sp"))


D = 1024
KC = 8
DFF = 2816
NSL = 11
SEG = 2048
GT = 512
NG = SEG // GT
EPS = 1e-6
ARENA_WORDS = 53200


def build(nseg, dbg_names=(), only=None, npre=0):
    nc = bass.Bass("TRN2", target_bir_lowering=False)
    NTOK = nseg * SEG

    def din(name, shape):
        return nc.dram_tensor(name, list(shape), F32, kind="ExternalInput").ap()

    xT_d = din("xT", [D, NTOK])
    xp_d = din("xp", [D, max(npre, 1) * SEG])
    w1h_d = din("w1h", [4, NSL, 128, 4096])
    w2h_d = din("w2h", [4, DFF, D])
    wmix_d = din("wmix", [8, 128, 4096])
    wab_d = din("wab", [128, 64])
    norms_d = din("norms", [128, 56])
    convw_d = din("convw", [128, 48])
    small_d = din("small32", [128, 32])
    dnn_d = din("dnn", [128, 1])
    sgnb_d = din("sgnb", [128, 512])
    sgwT_d = din("sgwT", [128, 512])
    sgb_d = din("sgb", [128, 512])
    pw_d = din("pw", [128, 2048])
    pscale_d = din("pscale", [128, 8])
    consts_d = din("consts", [128, 4 * 128 + 128])
    outT_d = nc.dram_tensor("outT", [D, NTOK], F32, kind="ExternalOutput").ap()

    big = nc.alloc_sbuf_tensor("arena", [128, ARENA_WORDS], F32)
    off = [0]

    def a32(n):
        n = (n + 7) // 8 * 8
        v = big[:, off[0]:off[0] + n]
        off[0] += n
        assert off[0] <= ARENA_WORDS, ("arena overflow", off[0])
        return v

    def a16(n):
        w = (n // 2 + 7) // 8 * 8
        v = big[:, off[0]:off[0] + w].bitcast(BF16)[:, 0:n]
        off[0] += w
        assert off[0] <= ARENA_WORDS, ("arena overflow", off[0])
        return v

    psb = [nc.alloc_psum_tensor("ps%d" % i, [128, 512], F32) for i in range(8)]
    P = Prog()
    dbg_outs = {}

    def mm(out, lhsT, rhs, start, stop, r, w):
        P.add("pe", lambda e: e.matmul(out, lhsT=lhsT, rhs=rhs, start=start, stop=stop), r=r, w=w)

    F32R = mybir.dt.float32r
    USE_R = _os0.environ.get("FP32R", "0") == "1"

    def mmr(out, lhsT, rhs, start, stop, r, w):
        if USE_R:
            lhsT = lhsT.bitcast(F32R)
            rhs = rhs.bitcast(F32R)
        P.add("pe", lambda e: e.matmul(out, lhsT=lhsT, rhs=rhs, start=start, stop=stop), r=r, w=w)

    def act(out, in_, func, r, w, bias=None, scale=1.0):
        if bias is None:
            P.add("act", lambda e: e.activation(out=out, in_=in_, func=func, scale=scale), r=r, w=w)
        else:
            P.add("act", lambda e: e.activation(out=out, in_=in_, func=func, bias=bias, scale=scale), r=r, w=w)

    def tt(eng, out, in0, in1, op, r, w):
        P.add(eng, lambda e: e.tensor_tensor(out=out, in0=in0, in1=in1, op=op), r=r, w=w)

    def ts(eng, out, in0, s1, op0, r, w, s2=None, op1=None):
        if op1 is None:
            P.add(eng, lambda e: e.tensor_scalar(out=out, in0=in0, scalar1=s1, scalar2=None, op0=op0), r=r, w=w)
        else:
            P.add(eng, lambda e: e.tensor_scalar(out=out, in0=in0, scalar1=s1, scalar2=s2, op0=op0, op1=op1), r=r, w=w)

    def stt(out, in0, scalar, in1, op0, op1, r, w):
        P.add("dve", lambda e: e.scalar_tensor_tensor(out=out, in0=in0, scalar=scalar, in1=in1, op0=op0, op1=op1), r=r, w=w)

    def cp(eng, out, in_, r, w):
        if eng == "act":
            P.add("act", lambda e: e.copy(out=out, in_=in_), r=r, w=w)
        else:
            P.add(eng, lambda e: e.tensor_copy(out=out, in_=in_), r=r, w=w)

    def recip(out, in_, r, w):
        P.add("dve", lambda e: e.reciprocal(out=out, in_=in_), r=r, w=w)

    def dma(q, out, in_, r, w, dsem):
        P.add(q, lambda e: e.dma_start(out=out, in_=in_), r=r, w=w, dsem=dsem)

    def memset(eng, ap, val, w):
        P.add(eng, lambda e: e.memset(ap, val), w=w)

    dbg_cnt = [0]

    def dbg(name, ap, r):
        if name not in dbg_names or name in dbg_outs:
            return
        shp = list(ap.shape)
        fl = 1
        for s in shp[1:]:
            fl *= s
        t = nc.dram_tensor("dbg_" + name, [shp[0], fl], ap.dtype, kind="ExternalOutput").ap()
        dbg_outs[name] = t
        view = t
        if len(shp) == 3:
            view = t.rearrange("p (a b) -> p a b", a=shp[1])
        elif len(shp) == 4:
            view = t.rearrange("p (a b c) -> p a b c", a=shp[1], b=shp[2])
        dma("sp", view, ap, r, [], "dbg%d" % dbg_cnt[0])
        dbg_cnt[0] += 1
        P.add("sp", None, r=[], w=[])
        P.ops[-1]["deps"] = {len(P.ops) - 2}

    rr = [0]

    hold = {}

    def nb(n=1):
        for _ in range(8):
            i = rr[0] % 8
            rr[0] += 1
            if hold.get("ps%d" % i, 0) == 0:
                hold["ps%d" % i] = n
                return psb[i], "ps%d" % i
        raise RuntimeError("no free PSUM bank")

    def _rh(k):
        if hold.get(k, 0) > 0:
            hold[k] -= 1

    P.read_hook = _rh

    xT = a32(KC * SEG).rearrange("p (c t) -> p c t", c=KC)

    def xk(g):
        return [("x", g, m) for m in range(KC)]

    c_ident = a32(128)
    c_triu = a32(128)
    c_trils = a32(128)
    c_triu128 = a32(128)
    c_invc = a32(128).rearrange("p (c t) -> p c t", c=8)
    ones32 = a32(128)
    onesb = a16(128)
    identb = a16(128)
    epst = a32(8)
    normw = a32(56)
    convw = a32(48).rearrange("p (c j) -> p c j", j=4)
    small = a32(32)
    nealog = a32(16)
    dnn = a32(8)
    sgnb = a32(512)
    sgb = a32(512).rearrange("p (g t) -> p g t", g=4)
    sgwTb = a16(512).rearrange("p (g t) -> p g t", g=4)
    pwb = a16(2048).rearrange("p (g i o) -> p g i o", g=4, i=2)
    pscale = a32(8)
    wab = a16(64).rearrange("p (k n) -> p k n", k=8)
    S32 = a32(512).rearrange("p (h v) -> p h v", h=4)
    Sb = a16(512).rearrange("p (h v) -> p h v", h=4)
    chalo = a32(48).rearrange("p (c j) -> p c j", j=4)
    phalo = a32(128).rearrange("p (c t) -> p c t", c=8)
    base_off = off[0]

    cst = a32(5 * 128)
    dma("sp", cst, consts_d, [], ["cst"], "setup1")
    dma("sp", normw, norms_d, [], ["normw"], "setup2")
    dma("sp", convw.rearrange("p c j -> p (c j)"), convw_d, [], ["convw"], "setup3")
    dma("sp", small, small_d, [], ["small"], "setup4")
    dma("sp", dnn[:, 0:1], dnn_d, [], ["dnn"], "setup5")
    dma("sp", sgnb, sgnb_d, [], ["sgnb"], "setup6")
    dma("sp", sgb.rearrange("p g t -> p (g t)"), sgb_d, [], ["sgb"], "setup7")
    dma("sp", pscale, pscale_d, [], ["pscale"], "setup8")
    dma("pool", wab.rearrange("p k n -> p (k n)"), wab_d, [], ["wab"], "setup9")
    dma("pool", pwb.rearrange("p g i o -> p (g i o)"), pw_d, [], ["pwb"], "setup10")
    sgw_stage = a32(512)
    dma("sp", sgw_stage, sgwT_d, [], ["sgwst"], "setup11")
    cp("dve", c_ident, cst[:, 0:128], ["cst"], ["c_ident"])
    cp("dve", c_triu, cst[:, 128:256], ["cst"], ["c_triu"])
    cp("dve", c_trils, cst[:, 256:384], ["cst"], ["c_trils"])
    cp("dve", c_triu128, cst[:, 384:512], ["cst"], ["c_triu128"])
    cp("dve", c_invc.rearrange("p c t -> p (c t)"), cst[:, 512:640], ["cst"], ["c_invc"])
    cp("dve", identb, cst[:, 0:128], ["cst"], ["identb"])
    memset("dve", ones32, 1.0, ["ones32"])
    memset("dve", onesb, 1.0, ["onesb"])
    memset("dve", epst, EPS, ["epst"])
    memset("dve", S32.rearrange("p h v -> p (h v)"), 0.0, ["S32"])
    memset("dve", Sb.rearrange("p h v -> p (h v)"), 0.0, ["Sb"])
    memset("dve", chalo.rearrange("p c j -> p (c j)"), 0.0, ["chalo"])
    memset("dve", phalo.rearrange("p c t -> p (c t)"), 0.0, ["phalo"])
    act(nealog, small[:, 16:32], AF.Exp, ["small"], ["nealog"])
    ts("dve", nealog, nealog, -1.0, ALU.mult, ["nealog"], ["nealog"])
    tt("dve", sgwTb, sgw_stage.rearrange("p (g t) -> p g t", g=4),
       c_triu128.unsqueeze(1).broadcast_to([128, 4, 128]), ALU.mult, ["sgwst", "c_triu128"], ["sgwTb"])
    P.fence()
    off[0] = base_off
    phase_base = base_off

    def phase_reset():
        off[0] = phase_base

    def norm_group(g, nidx, out_ap, out_keys, sq, rstd, bank, bankk):
        gs = slice(g * GT, (g + 1) * GT)
        act(sq, xT[:, :, gs], AF.Square, xk(g), ["sq"])
        for c in range(KC):
            mm(bank[:], onesb, sq[:, c, :], c == 0, c == KC - 1, ["sq", "onesb"], [bankk])
        act(rstd, bank[:], AF.Ln, [bankk, "epst"], ["rstd"], bias=epst[:, 0:1], scale=1.0 / D)
        act(rstd, rstd, AF.Exp, ["rstd"], ["rstd"], scale=-0.5)
        for c in range(KC):
            stt(out_ap[:, c, :], xT[:, c, gs], normw[:, nidx * 8 + c:nidx * 8 + c + 1], rstd, ALU.mult, ALU.mult,
                [("x", g, c), "normw", "rstd"], [out_keys[c]])

    def ffn(fidx, nidx, groups=None):
        groups = list(range(NG)) if groups is None else list(groups)
        phase_reset()
        xn = a16(KC * SEG).rearrange("p (c t) -> p c t", c=KC)
        sq = a16(KC * GT).rearrange("p (c t) -> p c t", c=KC)
        rstd = a32(GT)
        w1 = [a16(4096).rearrange("p (k n) -> p k n", k=8) for _ in range(2)]
        w2 = [a16(2048).rearrange("p (j n) -> p j n", j=2) for _ in range(2)]
        sg = [a32(GT) for _ in range(2)]
        hT = [a16(2 * GT).rearrange("p (j t) -> p j t", j=2) for _ in range(2)]
        P.fence()

        def load(s):
            sl = s % 2
            dma("pool", w1[sl].rearrange("p k n -> p (k n)"), w1h_d[fidx, s], [], ["w1_%d" % sl], "w1_%d" % sl)
            dma("pool", w2[sl], w2h_d[fidx, s * 256:(s + 1) * 256, :].rearrange("(j p) n -> p j n", p=128),
                [], ["w2_%d" % sl], "w2_%d" % sl)

        load(0)
        load(1)
        for g in groups:
            norm_group(g, nidx, xn[:, :, g * GT:(g + 1) * GT], [("xn", g, c) for c in range(KC)], sq, rstd, psb[4], "ps4")
        its = [(s, g) for s in range(NSL) for g in groups]

        def p1(i, j):
            s, g = its[i]
            sl = s % 2
            hb = i % 2
            gs = slice(g * GT, (g + 1) * GT)
            for which in range(2):
                bi = 2 * j + which
                for k in range(KC):
                    mm(psb[bi][:], w1[sl][:, k, which * 256 + j * 128: which * 256 + (j + 1) * 128],
                       xn[:, k, gs], k == 0, k == KC - 1, ["w1_%d" % sl, ("xn", g, k)], ["ps%d" % bi])
            act(sg[j], psb[2 * j][:], AF.Silu, ["ps%d" % (2 * j)], ["sg%d" % j])
            tt("dve", hT[hb][:, j, :], sg[j], psb[2 * j + 1][:], ALU.mult, ["sg%d" % j, "ps%d" % (2 * j + 1)],
               ["hT%d_%d" % (hb, j)])

        def p2(i, half):
            s, g = its[i]
            sl = s % 2
            hb = i % 2
            gs = slice(g * GT, (g + 1) * GT)
            for m in range(4 * half, 4 * half + 4):
                bi = 4 + (m % 4)
                for j in range(2):
                    mm(psb[bi][:], w2[sl][:, j, m * 128:(m + 1) * 128], hT[hb][:, j, :], j == 0, j == 1,
                       ["w2_%d" % sl, "hT%d_%d" % (hb, j)], ["ps%d" % bi])
                stt(xT[:, m, gs], psb[bi][:], 0.5, xT[:, m, gs], ALU.mult, ALU.add, ["ps%d" % bi, ("x", g, m)], [("x", g, m)])

        def scale_w2(s):
            sl = s % 2
            ts("pool", w2[sl].rearrange("p j n -> p (j n)"), w2[sl].rearrange("p j n -> p (j n)"), 0.5, ALU.mult,
               ["w2_%d" % sl], ["w2_%d" % sl])

        p1(0, 0)
        p1(0, 1)
        for i in range(len(its)):
            s = its[i][0]
            nxt_new = i + 1 < len(its) and its[i + 1][0] != s
            if _os0.environ.get("FSPLIT", "1") == "1":
                if i + 1 < len(its):
                    p1(i + 1, 0)
                p2(i, 0)
                if i + 1 < len(its):
                    p1(i + 1, 1)
                p2(i, 1)
            else:
                if i + 1 < len(its):
                    p1(i + 1, 0)
                    p1(i + 1, 1)
                p2(i, 0)
                p2(i, 1)
            if (i + 1 == len(its) or its[i + 1][0] != s) and s + 2 < NSL:
                load(s + 2)
        P.fence()

    def interleave(*gens):
        gens = [g_ for g_ in gens if g_ is not None]
        while gens:
            for g_ in list(gens):
                try:
                    next(g_)
                except StopIteration:
                    gens.remove(g_)

    def run(gen):
        for _ in gen:
            pass

    def pipeline(items, sets, width):
        free = list(sets)
        active = []
        idx = 0
        while idx < len(items) or active:
            while idx < len(items) and len(active) < width:
                fac, needs, on_start = items[idx]
                if needs and not free:
                    break
                if on_start is not None:
                    on_start()
                st = free.pop(0) if needs else None
                g_ = fac(st)
                idx += 1
                try:
                    next(g_)
                    active.append((g_, st))
                except StopIteration:
                    if st is not None:
                        free.append(st)
            for ent in list(active):
                try:
                    next(ent[0])
                except StopIteration:
                    active.remove(ent)
                    if ent[1] is not None:
                        free.append(ent[1])

    def mixer_ab(full_groups=(0, 1, 2, 3), state_groups=()):
        phase_reset()
        h4 = lambda ap: ap.rearrange("p (h c) -> p h c", h=4)
        wsl = [a16(4096).rearrange("p (k n) -> p k n", k=8) for _ in range(2)]
        qnT = h4(a16(4 * GT))
        kT = h4(a16(4 * GT))
        qdT = h4(a16(4 * GT))
        vT = h4(a16(4 * GT))
        zs = h4(a16(4 * GT))
        gsu = h4(a16(4 * GT))
        svt = a16(4 * GT).rearrange("p (a n) -> p a n", a=4)
        abx = a32(32)
        beta = a32(16)
        gT = a32(16)
        gam = a32(16)
        bg = a32(16)
        kts = a32(16)
        egl = a32(32).rearrange("p (a h e) -> p a h e", a=4, h=4)
        AT4 = [h4(a16(512)) for _ in range(4)]
        ktail = [h4(a16(512)) for _ in range(4)]
        u_sb = [h4(a32(512)) for _ in range(4)]
        wT = [h4(a16(512)) for _ in range(4)]
        vnew2 = [h4(a16(512)) for _ in range(2)]

        def prep_set(tag):
            T_ = [h4(a32(512)) for _ in range(7)]
            d_ = dict(tag=tag, gtri=T_[0], Gb=T_[1], EU=T_[2], EL=T_[3], L4=T_[4], U4=T_[5], X4=T_[6],
                      PP1=T_[0], PT1=T_[1], PP0=T_[2], PT0=T_[3])
            d_["kn"] = dict(gtri="T0", Gb="T1", EU="T2", EL="T3", L4="T4", U4="T5", X4="T6",
                            PP1="T0", PT1="T1", PP0="T2", PT0="T3", TT4="TT4", KbG="KbG", Vb="Vb")
            for nm in ("TT4", "KbG", "Vb"):
                d_[nm] = h4(a16(512))
            return d_

        setA = prep_set("A")
        R0 = off[0]
        hn = a16(KC * GT).rearrange("p (c t) -> p c t", c=KC)
        sq = a16(KC * GT).rearrange("p (c t) -> p c t", c=KC)
        rstd = a32(GT)
        SA = []
        for i_ in range(4):
            pre_ = a32(520)
            cv_ = a32(GT)
            SA.append(dict(i=i_, pre=pre_, sqb=pre_.bitcast(BF16)[:, 0:GT], cv=cv_, rs=cv_, tmpf=cv_, sv_=a32(GT), ss4=a32(8)))
        RA_end = off[0]
        off[0] = R0
        sB0 = off[0]
        setB = prep_set("B")
        setC = prep_set("C")
        sB1 = off[0]
        o_sb = h4(a32(4 * GT))
        RB_end = off[0]
        off[0] = sB0
        SC = [dict(i=i_, sqb=a16(GT), rs=a32(GT), tmpf=a32(GT)) for i_ in range(4)]
        assert off[0] <= sB1
        off[0] = max(RB_end, RA_end)
        P.fence()
        for hf in range(2):
            memset("pool", vnew2[hf].rearrange("p h c -> p (h c)"), 0.0, ["vnew%d" % hf])

        ucnt = [0]

        def load_unit(u):
            sl = ucnt[0] % 2
            ucnt[0] += 1
            dma("pool", wsl[sl].rearrange("p k n -> p (k n)"), wmix_d[u], [], ["wsl%d" % sl], "wsl%d" % sl)
            return sl

        bc4 = lambda ap16, t4: ap16[:, t4 * 4:(t4 + 1) * 4].unsqueeze(2).broadcast_to([128, 4, 128])
        m_u = c_triu.unsqueeze(1).broadcast_to([128, 4, 128])
        m_ls = c_trils.unsqueeze(1).broadcast_to([128, 4, 128])
        i4 = c_ident.unsqueeze(1).broadcast_to([128, 4, 128])
        v4 = lambda bank: bank[:].rearrange("p (h c) -> p h c", h=4)

        def gates_chain():
            bk, bkk = nb(2)
            for t4 in range(4):
                for k in range(KC):
                    mm(bk[:, t4 * 8:(t4 + 1) * 8], hn[:, k, t4 * 128:(t4 + 1) * 128], wab[:, k, :], k == 0, k == KC - 1,
                       [("hn", k), "wab"], [bkk])
            yield
            abv = bk[:, 0:32].rearrange("p (a n) -> p a n", a=4)
            act(beta.rearrange("p (a h) -> p a h", a=4), abv[:, :, 0:4], AF.Sigmoid, [bkk], ["beta"])
            xx = abx[:, 0:16]
            ax = abx[:, 16:32]
            tt("dve", xx.rearrange("p (a h) -> p a h", a=4), abv[:, :, 4:8], small[:, 0:16].rearrange("p (a h) -> p a h", a=4),
               ALU.add, [bkk, "small"], ["abx"])
            yield
            stt(ax, xx, -1.0, xx, ALU.mult, ALU.max, ["abx"], ["abx2"])
            yield
            act(ax, ax, AF.Exp, ["abx2"], ["abx2"], scale=-1.0)
            yield
            act(ax, ax, AF.Ln, ["abx2", "ones32"], ["abx2"], bias=ones32[:, 0:1])
            yield
            stt(gT, xx, 0.0, ax, ALU.max, ALU.add, ["abx", "abx2"], ["gT"])
            yield
            tt("dve", gT, gT, nealog, ALU.mult, ["gT", "nealog"], ["gT"])
            yield
            bk, bkk = nb()
            for t4 in range(4):
                mm(bk[:, t4 * 4:(t4 + 1) * 4], c_triu, gT[:, t4 * 4:(t4 + 1) * 4], True, True, ["c_triu", "gT"], [bkk])
            yield
            cp("dve", gam, bk[:, 0:16], [bkk], ["gam"])
            yield
            act(bg, gam, AF.Exp, ["gam"], ["bg"])
            yield
            tt("dve", bg, bg, beta, ALU.mult, ["bg", "beta"], ["bg"])

        def chain_qkv(u, h, sl, T_):
            i_ = T_["i"]
            pre, cv, sv_, sqb, rs = T_["pre"], T_["cv"], T_["sv_"], T_["sqb"], T_["rs"]
            kp, kcv, ksv = ["%s%d" % (n_, i_) for n_ in ("pre", "cv", "sv_")]
            kph, ksq, krs = kp, kp, kcv
            ch = u * 4 + h
            bk, bkk = nb()
            for k in range(KC):
                mm(bk[:], wsl[sl][:, k, h * 128:(h + 1) * 128], hn[:, k, :], k == 0, k == KC - 1,
                   ["wsl%d" % sl, ("hn", k)], [bkk])
            yield
            cp("act", pre[:, 3:515], bk[:], [bkk], [kp])
            cp("act", pre[:, 0:3], chalo[:, ch, 0:3], ["chalo%d" % ch], [kph])
            yield
            ts("dve", cv, pre[:, 3:515], convw[:, ch, 3:4], ALU.mult, [kp, "convw"], [kcv])
            yield
            for j in range(3):
                stt(cv, pre[:, j:j + 512], convw[:, ch, j:j + 1], cv, ALU.mult, ALU.add, [kp, kph, "convw", kcv], [kcv])
                yield
            cp("pool", chalo[:, ch, 0:3], pre[:, 512:515], [kp], ["chalo%d" % ch])
            if u == 2:
                act(vT[:, h, :], cv, AF.Silu, [kcv], [("vT", h)])
                return
            act(sv_, cv, AF.Silu, [kcv], [ksv])
            yield
            tt("pool", sqb, sv_, sv_, ALU.mult, [ksv], [ksq])
            yield
            b2, b2k = nb()
            mm(b2[:], onesb, sqb, True, True, ["onesb", ksq], [b2k])
            yield
            act(rs, b2[:], AF.Ln, [b2k, "epst"], [krs], bias=epst[:, 0:1])
            yield
            act(rs, rs, AF.Exp, [krs], [krs], scale=-0.5)
            yield
            if u == 0:
                stt(qnT[:, h, :], sv_, float(128 ** -0.5), rs, ALU.mult, ALU.mult, [ksv, krs], [("qnT", h)])
            else:
                tt("dve", kT[:, h, :], sv_, rs, ALU.mult, [ksv, krs], [("kT", h)])

        def chain_zsu(u, h, sl):
            bk, bkk = nb()
            for k in range(KC):
                mm(bk[:], wsl[sl][:, k, h * 128:(h + 1) * 128], hn[:, k, :], k == 0, k == KC - 1,
                   ["wsl%d" % sl, ("hn", k)], [bkk])
            yield
            if u == 3:
                act(zs[:, h, :], bk[:], AF.Silu, [bkk], [("zs", h)])
            else:
                act(gsu[:, h, :], bk[:], AF.Gelu, [bkk], [("gsu", h)])

        def chain_sv(t4, sl, T_):
            i_ = T_["i"]
            sv_, tmpf, ss4 = T_["sv_"], T_["tmpf"], T_["ss4"]
            ksv, ktm, kss = ["%s%d" % (n_, i_) for n_ in ("sv_", "cv", "ss4")]
            bk, bkk = nb()
            for k in range(KC):
                mm(bk[:], hn[:, k, t4 * 128:(t4 + 1) * 128], wsl[sl][:, k, :], k == 0, k == KC - 1,
                   [("hn", k), "wsl%d" % sl], [bkk])
            yield
            act(sv_, bk[:], AF.Gelu, [bkk], [ksv])
            yield
            tt("pool", tmpf, sv_, sv_, ALU.mult, [ksv], [ktm])
            yield
            P.add("dve", lambda e: e.tensor_reduce(out=ss4[:, 0:4], in_=tmpf.rearrange("p (a c) -> p a c", a=4),
                                                  axis=mybir.AxisListType.X, op=ALU.add), r=[ktm], w=[kss])
            yield
            act(ss4[:, 0:4], ss4[:, 0:4], AF.Sqrt, [kss, "epst"], [kss], bias=epst[:, 0:1], scale=1.0 / 128)
            yield
            recip(ss4[:, 0:4], ss4[:, 0:4], [kss], [kss])
            yield
            tt("dve", tmpf.rearrange("p (a c) -> p a c", a=4), sv_.rearrange("p (a c) -> p a c", a=4),
               ss4[:, 0:4].unsqueeze(2).broadcast_to([128, 4, 128]), ALU.mult, [ksv, kss], [ktm])
            yield
            tt("pool", svt[:, t4, :], tmpf, sgnb, ALU.mult, [ktm, "sgnb"], [("svt", t4)])

        def prep(t4, S_, full):
            tg = S_["tag"]
            K_ = lambda nm: S_["kn"][nm] + tg
            ob = t4
            ts_ = slice(t4 * 128, (t4 + 1) * 128)
            gtri, Gb, EU, EL, L4, U4, X4 = S_["gtri"], S_["Gb"], S_["EU"], S_["EL"], S_["L4"], S_["U4"], S_["X4"]
            TT4, KbG, Vb = S_["TT4"], S_["KbG"], S_["Vb"]
            for h in range(4):
                ts("pool", gtri[:, h, :], c_triu, gT[:, t4 * 4 + h:t4 * 4 + h + 1], ALU.mult, ["c_triu", "gT"], [K_("gtri")])
            yield
            bB, bBk = nb(3)
            mm(bB[:], ones32, gtri.rearrange("p h c -> p (h c)"), True, True, ["ones32", K_("gtri")], [bBk])
            yield
            B3 = v4(bB)
            tt("dve", EU, B3, bc4(gam, t4), ALU.subtract, [bBk, "gam"], [K_("EU")])
            yield
            tt("dve", EL, bc4(gam, t4), B3, ALU.subtract, [bBk, "gam"], [K_("EL")])
            yield
            act(Gb, B3, AF.Exp, [bBk], [K_("Gb")])
            yield
            act(EU, EU, AF.Exp, [K_("EU")], [K_("EU")])
            yield
            act(EL, EL, AF.Exp, [K_("EL")], [K_("EL")])
            yield
            stt(EU, EU, 1.0, m_u, ALU.min, ALU.mult, [K_("EU"), "c_triu"], [K_("EU")])
            yield
            stt(EL, EL, 1.0, m_ls, ALU.min, ALU.mult, [K_("EL"), "c_trils"], [K_("EL")])
            cp("pool", egl[:, t4, :, :], Gb[:, :, 63:128:64], [K_("Gb")], [("egl", t4)])
            yield
            if full:
                tt("pool", qdT[:, :, ts_], qnT[:, :, ts_], Gb, ALU.mult, [("qnT", h) for h in range(4)] + [K_("Gb")],
                   [("qdT", t4)])
            cp("pool", kts[0:64, t4 * 4:(t4 + 1) * 4], EU[0:64, :, 63], [K_("EU")], [("kts_a", t4)])
            cp("pool", kts[64:128, t4 * 4:(t4 + 1) * 4], EU[64:128, :, 127], [K_("EU")], [("kts_b", t4)])
            yield
            bK, bKk = nb(2)
            for h in range(4):
                mm(bK[:, h * 128:(h + 1) * 128], kT[:, h, ts_], identb, True, True, [("kT", h), "identb"], [bKk])
            yield
            K3 = v4(bK)
            tt("dve", ktail[ob], K3, bc4(kts, t4), ALU.mult, [bKk, ("kts_a", t4), ("kts_b", t4)], [("ktail", ob)])
            yield
            tt("dve", KbG, K3, bc4(bg, t4), ALU.mult, [bKk, "bg"], [K_("KbG")])
            yield
            bV, bVk = nb()
            for h in range(4):
                mm(bV[:, h * 128:(h + 1) * 128], vT[:, h, ts_], identb, True, True, [("vT", h), "identb"], [bVk])
            yield
            tt("dve", Vb, v4(bV), bc4(beta, t4), ALU.mult, [bVk, "beta"], [K_("Vb")])
            yield
            bKK, bKKk = nb()
            for h in range(4):
                mm(bKK[:, h * 128:(h + 1) * 128], kT[:, h, ts_], kT[:, h, ts_], True, True, [("kT", h)], [bKKk])
            if full:
                bKQ, bKQk = nb()
                for h in range(4):
                    mm(bKQ[:, h * 128:(h + 1) * 128], kT[:, h, ts_], qnT[:, h, ts_], True, True, [("kT", h), ("qnT", h)], [bKQk])
            yield
            tt("dve", L4, v4(bKK), EL, ALU.mult, [bKKk, K_("EL")], [K_("L4")])
            yield
            tt("pool", L4, L4, bc4(beta, t4), ALU.mult, [K_("L4"), "beta"], [K_("L4")])
            if full:
                tt("dve", AT4[ob], v4(bKQ), EU, ALU.mult, [bKQk, K_("EU")], [("AT4", ob)])
            yield
            bU, bUk = nb()
            for h in range(4):
                mmr(bU[:, h * 128:(h + 1) * 128], L4[:, h, :], c_ident, True, True, [K_("L4"), "c_ident"], [bUk])
            yield
            cp("act", U4, v4(bU), [bUk], [K_("U4")])
            yield
            tt("pool", X4, i4, U4, ALU.subtract, ["c_ident", K_("U4")], [K_("X4")])
            yield
            PPb = [S_["PP0"], S_["PP1"]]
            PTb = [S_["PT0"], S_["PT1"]]
            K_ = lambda nm: S_["kn"].get(nm, nm) + tg
            cur = {0: (U4, L4, K_("U4"), K_("L4"))}

            def sqr(i):
                Pc, Ptc, Pck, Ptck = cur[i - 1]
                pn, ptn = PPb[i % 2], PTb[i % 2]
                pnk, ptnk = K_("PP%d" % (i % 2)), K_("PT%d" % (i % 2))
                if i < 5:
                    b1, b1k = nb()
                    for h in range(4):
                        mmr(b1[:, h * 128:(h + 1) * 128], Ptc[:, h, :], Pc[:, h, :], True, True, [Pck, Ptck], [b1k])
                b2, b2k = nb()
                for h in range(4):
                    mmr(b2[:, h * 128:(h + 1) * 128], Pc[:, h, :], Ptc[:, h, :], True, True, [Pck, Ptck], [b2k])
                if i < 5:
                    cp("act", pn, v4(b1), [b1k], [pnk])
                cp("act", ptn, v4(b2), [b2k], [ptnk])
                cur[i] = (pn, ptn, pnk, ptnk)

            def xpm(i):
                ptn, ptnk = cur[i][1], cur[i][3]
                b3, b3k = nb()
                for h in range(4):
                    mmr(b3[:, h * 128:(h + 1) * 128], ptn[:, h, :], X4[:, h, :], True, True, [ptnk, K_("X4")], [b3k])
                return b3, b3k

            def xadd(i, b3, b3k):
                if i < 5:
                    tt("dve", X4, X4, v4(b3), ALU.add, [K_("X4"), b3k], [K_("X4")])
                else:
                    tt("dve", TT4, X4, v4(b3), ALU.add, [K_("X4"), b3k], [K_("TT4")])

            sqr(1)
            yield
            for i in range(1, 6):
                if i + 1 <= 5:
                    sqr(i + 1)
                    yield
                b3, b3k = xpm(i)
                yield
                xadd(i, b3, b3k)
                yield
            bu, buk = nb()
            for h in range(4):
                mm(bu[:, h * 128:(h + 1) * 128], TT4[:, h, :], Vb[:, h, :], True, True, [K_("TT4"), K_("Vb")], [buk])
            bw, bwk = nb()
            for h in range(4):
                mm(bw[:, h * 128:(h + 1) * 128], KbG[:, h, :], TT4[:, h, :], True, True, [K_("KbG"), K_("TT4")], [bwk])
            yield
            cp("act", u_sb[ob], v4(bu), [buk], [("u_sb", ob)])
            cp("act", wT[ob], v4(bw), [bwk], [("wT", ob)])

        def scan(t4, full):
            ob = t4
            for half in range(2):
                r0 = half * 64
                rs_ = slice(r0, r0 + 64)
                n = t4 * 2 + half
                vnew = vnew2[half]
                vk = "vnew%d" % half
                bpw, bpwk = nb()
                for h in range(4):
                    mm(bpw[:, h * 128:(h + 1) * 128], wT[ob][:, h, :], Sb[:, h, :], True, True, [("wT", ob), "Sb"], [bpwk])
                tt("dve", vnew[rs_], u_sb[ob][rs_], bpw[rs_, :].rearrange("p (h c) -> p h c", h=4), ALU.subtract,
                   [("u_sb", ob), bpwk], [vk])
                if full:
                    bo, bok = nb()
                    for h in range(4):
                        mm(bo[:, h * 64:(h + 1) * 64], Sb[:, h, :], qdT[:, h, n * 64:(n + 1) * 64], True, False,
                           ["Sb", ("qdT", t4)], [bok])
                        mm(bo[:, h * 64:(h + 1) * 64], vnew[:, h, :], AT4[ob][:, h, r0:r0 + 64], False, True,
                           [vk, ("AT4", ob)], [bok])
                    cp("act", o_sb[:, :, n * 64:(n + 1) * 64], bo[:, 0:256].rearrange("p (h c) -> p h c", h=4), [bok],
                       [("o_sb", n)])
                bs, bsk = nb()
                for h in range(4):
                    mm(bs[:, h * 128:(h + 1) * 128], ktail[ob][:, h, :], vnew[:, h, :], True, True, [("ktail", ob), vk], [bsk])
                tt("pool", S32, S32, egl[:, t4, :, half:half + 1].broadcast_to([128, 4, 128]), ALU.mult,
                   ["S32", ("egl", t4)], ["S32"])
                tt("dve", S32, S32, v4(bs), ALU.add, ["S32", bsk], ["S32"])
                cp("act", Sb, S32, ["S32"], ["Sb"])
                yield

        osk = [("o_sb", n) for n in range(8)]

        def chain_onorm(h, T_):
            i_ = T_["i"]
            sqb, rs, tmpf = T_["sqb"], T_["rs"], T_["tmpf"]
            ksq, krs, ktm = ["%s%d" % (n_, i_) for n_ in ("csqb", "crs", "ctmpf")]
            tt("pool", sqb, o_sb[:, h, :], o_sb[:, h, :], ALU.mult, osk, [ksq])
            yield
            b2, b2k = nb()
            mm(b2[:], onesb, sqb, True, True, ["onesb", ksq], [b2k])
            yield
            act(rs, b2[:], AF.Ln, [b2k, "epst"], [krs], bias=epst[:, 0:1], scale=1.0 / 128)
            yield
            act(rs, rs, AF.Exp, [krs], [krs], scale=-0.5)
            yield
            stt(tmpf, o_sb[:, h, :], dnn[:, 0:1], rs, ALU.mult, ALU.mult, osk + ["dnn", krs], [ktm])
            yield
            tt("pool", zs[:, h, :], tmpf, zs[:, h, :], ALU.mult, [ktm, ("zs", h)], [("zs", h)])

        def chain_sgmix(gi, T_):
            i_ = T_["i"]
            tmpf = T_["tmpf"]
            ktm = "ctmpf%d" % i_
            bk, bkk = nb()
            for t4 in range(4):
                mm(bk[:, t4 * 128:(t4 + 1) * 128], svt[:, t4, gi * 128:(gi + 1) * 128], sgwTb[:, gi, :], True, True,
                   [("svt", t4), "sgwTb"], [bkk])
            yield
            tt("dve", tmpf.rearrange("p (a t) -> p a t", a=4), bk[:].rearrange("p (a t) -> p a t", a=4),
               sgb[:, gi, :].unsqueeze(1).broadcast_to([128, 4, 128]), ALU.add, [bkk, "sgb"], [ktm])
            yield
            tt("pool", gsu[:, gi, :], tmpf, gsu[:, gi, :], ALU.mult, [ktm, ("gsu", gi)], [("gsu", gi)])

        groups = sorted(set(full_groups) | set(state_groups))
        MIXCUT = int(_os0.environ.get("MIXCUT", "0"))

        def seqg(*gens):
            for g_ in gens:
                yield from g_

        for g in groups:
            full = g in full_groups
            gs = slice(g * GT, (g + 1) * GT)
            ulist = [0, 1, 2, 3, 4, 5, 6, 7] if full else [1, 2]
            norm_group(g, 1, hn, [("hn", c) for c in range(KC)], sq, rstd, psb[0], "ps0")
            slots = {}
            slots[ulist[0]] = load_unit(ulist[0])
            items = [(lambda st: gates_chain(), False, None)]
            for ui, u in enumerate(ulist):
                if u >= 6:
                    break

                def on_start(ui=ui):
                    if ui + 1 < len(ulist):
                        slots[ulist[ui + 1]] = load_unit(ulist[ui + 1])

                for h in range(4):
                    osf = on_start if h == 0 else None
                    if u <= 2:
                        items.append((lambda st, u=u, h=h: chain_qkv(u, h, slots[u], st), True, osf))
                    elif u <= 4:
                        items.append((lambda st, u=u, h=h: chain_zsu(u, h, slots[u]), False, osf))
                    else:
                        items.append((lambda st, u=u, h=h: chain_sv(h, slots[u], st), True, osf))
            pipeline(items, SA, int(_os0.environ.get("PW", "5")))
            P.fence(dma=False)
            if MIXCUT == 1:
                break
            if _os0.environ.get("SEQB", "0") == "1":
                for t4_ in range(4):
                    run(prep(t4_, [setA, setB, setC, setA][t4_], full))
                    run(scan(t4_, full))
            else:
                interleave(prep(0, setA, full), prep(1, setB, full), prep(2, setC, full))
                interleave(seqg(scan(0, full), scan(1, full), scan(2, full)), prep(3, setA, full))
                run(scan(3, full))
            P.fence(dma=False)
            if MIXCUT == 2:
                break
            if full:
                interleave(*[chain_onorm(h, SC[h]) for h in range(4)])
                interleave(*[chain_sgmix(gi, SC[gi]) for gi in range(4)])
                for uo in range(2):
                    u = 6 + uo
                    if uo == 0:
                        slots[7] = load_unit(7)
                    sl = slots[u]
                    for mq in range(4):
                        m = uo * 4 + mq
                        bk, bkk = nb()
                        for k in range(8):
                            rhs = zs[:, k, :] if k < 4 else gsu[:, k - 4, :]
                            rk = ("zs", k) if k < 4 else ("gsu", k - 4)
                            mm(bk[:], wsl[sl][:, k, mq * 128:(mq + 1) * 128], rhs, k == 0, k == 7, ["wsl%d" % sl, rk], [bkk])
                        tt("dve", xT[:, m, gs], xT[:, m, gs], bk[:], ALU.add, [("x", g, m), bkk], [("x", g, m)])
            P.fence(dma=False)
        P.fence()

    def mixer_pool(first_seg):
        phase_reset()
        HW = 528
        hp = a32(KC * HW).rearrange("p (c t) -> p c t", c=KC)
        bufA = a32(KC * HW).rearrange("p (c t) -> p c t", c=KC)
        bufB = a32(KC * HW).rearrange("p (c t) -> p c t", c=KC)
        pooled = a16(KC * GT).rearrange("p (c t) -> p c t", c=KC)
        sq = a16(KC * GT).rearrange("p (c t) -> p c t", c=KC)
        rstd = a32(GT)
        tmp16 = a32(KC * 16).rearrange("p (c t) -> p c t", c=KC)
        tmpo = a32(GT)
        P.fence()
        for g in range(NG):
            gs = slice(g * GT, (g + 1) * GT)
            cp("pool", hp[:, :, 0:16], phalo, ["phalo"], ["hp_h"])
            norm_group(g, 4, hp[:, :, 16:HW], [("hp", c) for c in range(KC)], sq, rstd, psb[0], "ps0")
            hpk = [("hp", c) for c in range(KC)] + ["hp_h"]
            cp("pool", phalo, hp[:, :, GT:HW], hpk, ["phalo"])
            import os
            CUT = int(os.environ.get("POOLCUT", "9"))
            if CUT <= 1:
                continue
            tt("dve", bufA[:, :, 1:HW], hp[:, :, 1:HW], hp[:, :, 0:HW - 1], ALU.add, hpk, ["bufA", "bufA2"])
            tt("pool", bufB[:, 2:8, 3:HW], bufA[:, 2:8, 3:HW], bufA[:, 2:8, 1:HW - 2], ALU.add, ["bufA"], ["bufB", "bufB2"])
            tt("dve", bufA[:, 4:8, 7:HW], bufB[:, 4:8, 7:HW], bufB[:, 4:8, 3:HW - 4], ALU.add, ["bufB", "bufA"], ["bufA2"])
            tt("pool", bufB[:, 6:8, 15:HW], bufA[:, 6:8, 15:HW], bufA[:, 6:8, 7:HW - 8], ALU.add, ["bufA2", "bufB"], ["bufB2"])
            srcs = [(bufA, ["bufA"]), (bufB, ["bufB"]), (bufA, ["bufA2"]), (bufB, ["bufB2"])]
            if CUT <= 2:
                continue
            for gi in range(4):
                win = 2 ** (gi + 1)
                src, sk = srcs[gi]
                cs = slice(2 * gi, 2 * gi + 2)
                stt(pooled[:, cs, :], src[:, cs, 16:HW], 1.0 / win, hp[:, cs, 16:HW], ALU.mult, ALU.subtract,
                    sk + hpk, [("pooled", gi)])
                if first_seg and g == 0:
                    tt("dve", tmp16[:, cs, :], src[:, cs, 16:32], c_invc[:, cs, :], ALU.mult, sk + ["c_invc"], ["tmp16"])
                    tt("dve", pooled[:, cs, 0:16], tmp16[:, cs, :], hp[:, cs, 16:32], ALU.subtract, ["tmp16"] + hpk,
                       [("pooled", gi)])
            if CUT <= 3:
                continue
            for gi in range(4):
                for oc in range(2):
                    m = 2 * gi + oc
                    bk, bkk = nb()
                    for ic in range(2):
                        mm(bk[:], pwb[:, gi, ic, oc * 128:(oc + 1) * 128], pooled[:, 2 * gi + ic, :], ic == 0, ic == 1,
                           ["pwb", ("pooled", gi)], [bkk])
                    if CUT == 10:
                        stt(xT[:, m, gs], bk[:], pscale[:, m:m + 1], xT[:, m, gs], ALU.mult, ALU.add, [bkk, "pscale", ("x", g, m)], [("x", g, m)])
                    elif CUT == 11:
                        stt(xT[:, m, gs], bk[:], pscale[:, m:m + 1], xT[:, m, gs], ALU.mult, ALU.add, [bkk, "normw", ("x", g, m)], [("x", g, m)])
                    elif CUT == 6:
                        stt(xT[:, m, gs], bk[:], 0.5, xT[:, m, gs], ALU.mult, ALU.add, [bkk, ("x", g, m)], [("x", g, m)])
                    elif CUT == 7:
                        stt(xT[:, m, gs], bk[:], normw[:, m:m + 1], xT[:, m, gs], ALU.mult, ALU.add, [bkk, "normw", ("x", g, m)], [("x", g, m)])
                    elif CUT == 8:
                        act(tmpo, bk[:], AF.Copy, [bkk], ["tmpo"])
                        stt(xT[:, m, gs], tmpo, pscale[:, m:m + 1], xT[:, m, gs], ALU.mult, ALU.add, ["tmpo", "pscale", ("x", g, m)], [("x", g, m)])
                    elif CUT == 4:
                        tt("dve", xT[:, m, gs], xT[:, m, gs], bk[:], ALU.add, [bkk, ("x", g, m)], [("x", g, m)])
                    elif CUT == 5:
                        pass
                    else:
                        act(tmpo, bk[:], AF.Copy, [bkk, "pscale"], ["tmpo"], scale=pscale[:, m:m + 1])
                        tt("dve", xT[:, m, gs], xT[:, m, gs], tmpo, ALU.add, ["tmpo", ("x", g, m)], [("x", g, m)])
        P.fence()

    def final_out(seg):
        phase_reset()
        ob = [a32(KC * GT).rearrange("p (c t) -> p c t", c=KC) for _ in range(2)]
        sq = a16(KC * GT).rearrange("p (c t) -> p c t", c=KC)
        rstd = a32(GT)
        P.fence()
        for g in range(NG):
            b = g % 2
            norm_group(g, 6, ob[b], [("ob", b, c) for c in range(KC)], sq, rstd, psb[0], "ps0")
            t0 = seg * SEG + g * GT
            dma("sp", outT_d[:, t0:t0 + GT].rearrange("(c p) t -> p c t", p=128), ob[b],
                [("ob", b, c) for c in range(KC)], [], "out%d" % b)
        P.fence()

    def pool_halo_only(g):
        phase_reset()
        HW = 528
        hp = a32(KC * HW).rearrange("p (c t) -> p c t", c=KC)
        sq = a16(KC * GT).rearrange("p (c t) -> p c t", c=KC)
        rstd = a32(GT)
        P.fence()
        norm_group(g, 4, hp[:, :, 16:HW], [("hp", c) for c in range(KC)], sq, rstd, psb[0], "ps0")
        cp("pool", phalo, hp[:, :, GT:HW], [("hp", c) for c in range(KC)], ["phalo"])
        P.fence()

    allx = [k for g in range(NG) for k in xk(g)]
    inc = lambda nm: only is None or nm in only
    for ps in range(npre):
        last = ps == npre - 1
        dma("sp", xT, xp_d[:, ps * SEG:(ps + 1) * SEG].rearrange("(c p) t -> p c t", p=128), [], allx, "xload")
        ffn(0, 0)
        if not last:
            mixer_ab(full_groups=(), state_groups=(0, 1, 2, 3))
        else:
            mixer_ab(full_groups=(NG - 1,), state_groups=tuple(range(NG - 1)))
            ffn(1, 2, groups=[NG - 1])
            ffn(2, 3, groups=[NG - 1])
            pool_halo_only(NG - 1)
    for seg in range(nseg):
        dma("sp", xT, xT_d[:, seg * SEG:(seg + 1) * SEG].rearrange("(c p) t -> p c t", p=128), [], allx, "xload")
        if inc("ffn0"):
            ffn(0, 0)
        if inc("mix0"):
            mixer_ab()
        if inc("ffn1"):
            ffn(1, 2)
        if inc("ffn2"):
            ffn(2, 3)
        if inc("pool"):
            mixer_pool(seg == 0)
        if inc("ffn3"):
            ffn(3, 5)
        final_out(seg)
    P.fence()
    P.add("sp", None)
    P.emit(nc)
    return nc, dbg_outs


def host_prep(inp):
    f = lambda a: np.ascontiguousarray(np.asarray(a, dtype=np.float32))
    w_in = [inp["ffn1_w_in"][0], inp["ffn2_w_in"][0], inp["ffn1_w_in"][1], inp["ffn2_w_in"][1]]
    w_out = [inp["ffn1_w_out"][0], inp["ffn2_w_out"][0], inp["ffn1_w_out"][1], inp["ffn2_w_out"][1]]
    w1h = np.empty((4, NSL, 128, 8, 512), np.float32)
    for i, w in enumerate(w_in):
        w = np.asarray(w, np.float32)
        gate = w[:, :DFF].reshape(8, 128, NSL, 256)
        up = w[:, DFF:].reshape(8, 128, NSL, 256)
        w1h[i, :, :, :, :256] = gate.transpose(2, 1, 0, 3)
        w1h[i, :, :, :, 256:] = up.transpose(2, 1, 0, 3)
    w1h = w1h.reshape(4, NSL, 128, 4096)
    w2h = np.stack([np.asarray(w, np.float32) for w in w_out])
    wi = np.asarray(inp["ab_w_in"][0], np.float32)
    wo = np.asarray(inp["ab_w_out"][0], np.float32)
    bases = [0, 512, 1024, 1536, 2056, 2568]
    wmix = np.empty((8, 128, 8, 512), np.float32)
    for u, b in enumerate(bases):
        wmix[u] = wi[:, b:b + 512].reshape(8, 128, 512).transpose(1, 0, 2)
    for u in range(2):
        wmix[6 + u] = wo[:, u * 512:(u + 1) * 512].reshape(8, 128, 512).transpose(1, 0, 2)
    wmix = wmix.reshape(8, 128, 4096)
    wab = wi[:, 2048:2056].reshape(8, 128, 8).transpose(1, 0, 2).reshape(128, 64)
    nl = [inp["ffn_norm1"][0], inp["mix_norm"][0], inp["ffn_norm2"][0],
          inp["ffn_norm1"][1], inp["mix_norm"][1], inp["ffn_norm2"][1], inp["final_norm"]]
    norms = np.concatenate([np.asarray(v, np.float32).reshape(8, 128).T for v in nl], axis=1)
    convw = np.asarray(inp["dn_conv_w"][0], np.float32).reshape(4, 12, 128).transpose(2, 1, 0).reshape(128, 48)
    small = np.empty((128, 32), np.float32)
    small[:, 0:16] = np.tile(np.asarray(inp["dn_dt_bias"][0], np.float32), 4)[None, :]
    small[:, 16:32] = np.tile(np.asarray(inp["dn_a_log"][0], np.float32), 4)[None, :]
    dnn = np.asarray(inp["dn_out_norm"][0], np.float32).reshape(128, 1)
    sgnb = np.broadcast_to(np.asarray(inp["sg_norm"][0], np.float32).reshape(1, 512), (128, 512))
    sgwT = np.asarray(inp["sg_w"][0], np.float32).transpose(2, 0, 1).reshape(128, 512)
    sgb = np.broadcast_to(np.asarray(inp["sg_b"][0], np.float32).reshape(1, 512), (128, 512))
    pw = np.asarray(inp["pool_w"][0], np.float32).reshape(4, 2, 128, 256).transpose(2, 0, 1, 3).reshape(128, 2048)
    pscale = np.asarray(inp["pool_scale"][0], np.float32).reshape(8, 128).T
    idx = np.arange(128)
    same = (idx[:, None] // 64) == (idx[None, :] // 64)
    ident = np.eye(128, dtype=np.float32)
    triu = ((idx[:, None] <= idx[None, :]) & same).astype(np.float32)
    trils = ((idx[None, :] < idx[:, None]) & same).astype(np.float32)
    triu128 = (idx[:, None] <= idx[None, :]).astype(np.float32)
    invc = np.empty((128, 8, 16), np.float32)
    pos = np.arange(1, 17, dtype=np.float32)
    for c in range(8):
        win = 2 ** (c // 2 + 1)
        invc[:, c, :] = (1.0 / np.minimum(pos, win))[None, :]
    consts = np.concatenate([ident, triu, trils, triu128, invc.reshape(128, 128)], axis=1)
    return dict(w1h=f(w1h), w2h=f(w2h), wmix=f(wmix), wab=f(wab), norms=f(norms), convw=f(convw), small32=f(small),
                dnn=f(dnn), sgnb=f(sgnb), sgwT=f(sgwT), sgb=f(sgb), pw=f(pw), pscale=f(pscale), consts=f(consts))


_CACHE = {}
NPRE = 2
NOWN = 2


def kernel(**inputs):
    x = np.asarray(inputs["x"], np.float32)
    B, T, _ = x.shape
    half_t = T // 2
    assert half_t == NOWN * SEG
    shared = host_prep(inputs)
    key = (NPRE, NOWN)
    if key not in _CACHE:
        _CACHE[key] = build(NOWN, npre=NPRE)[0]
    nc = _CACHE[key]
    consts_a = shared["consts"]
    consts_b = consts_a.copy()
    invc_b = np.empty((128, 8, 16), np.float32)
    for c in range(8):
        invc_b[:, c, :] = 1.0 / (2 ** (c // 2 + 1))
    consts_b[:, 512:640] = invc_b.reshape(128, 128)
    in_maps = []
    for b in range(B):
        for h in range(2):
            m = dict(shared)
            m["xT"] = np.ascontiguousarray(x[b, h * half_t:(h + 1) * half_t].T)
            if h == 0:
                m["xp"] = np.zeros((D, NPRE * SEG), np.float32)
                m["consts"] = consts_a
            else:
                m["xp"] = np.ascontiguousarray(x[b, 0:half_t].T)
                m["consts"] = consts_b
            in_maps.append(m)
    res = run_bass_kernel_spmd(nc, in_maps, core_ids=list(range(2 * B)))
    out = np.empty((B, T, D), np.float32)
    for b in range(B):
        for h in range(2):
            out[b, h * half_t:(h + 1) * half_t] = res.results[2 * b + h]["outT"].T
    return out
```

```python
import numpy as np
import concourse.bass as bass
import concourse.mybir as mybir
from concourse.bass_utils import run_bass_kernel_spmd
from contextlib import ExitStack

F32 = mybir.dt.float32
BF16 = mybir.dt.bfloat16
AF = mybir.ActivationFunctionType
ALU = mybir.AluOpType

import os as _os0
SAME_ENG_SYNC = _os0.environ.get("SES", "1") == "1"
EPOCH = 12000


class Prog:
    CENGS = ("pe", "act", "dve", "pool", "sp")

    def __init__(self):
        self.ops = []
        self.lastw = {}
        self.readers = {}
        self.dma_cnt = {}
        self.pending = {}
        self.read_hook = None

    def fence(self, dma=True):
        last = {}
        for i, op in enumerate(self.ops):
            if op["fn"] is None:
                continue
            if op["dsem"] is not None and (not dma or str(op["dsem"]).startswith("xload")):
                continue
            k = ("d", op["dsem"]) if op["dsem"] is not None else ("e", op["eng"])
            last[k] = i
        self.pending = {e: set(last.values()) for e in self.CENGS}

    def add(self, eng, fn, r=(), w=(), dsem=None):
        i = len(self.ops)
        deps = set()
        if eng != "pe" and self.read_hook is not None:
            for k in r:
                if isinstance(k, str) and k.startswith("ps"):
                    self.read_hook(k)
        if eng != "pe":
            w = list(w) + [k for k in r if isinstance(k, str) and k.startswith("ps") and k not in w]
        if eng in self.pending:
            deps |= self.pending.pop(eng)
        for k in r:
            d = self.lastw.get(k)
            if d is not None:
                deps.add(d)
        for k in w:
            d = self.lastw.get(k)
            if d is not None:
                deps.add(d)
            for d in self.readers.get(k, ()):
                deps.add(d)
        for k in r:
            self.readers.setdefault(k, []).append(i)
        for k in w:
            self.lastw[k] = i
            self.readers[k] = []
        op = dict(eng=eng, fn=fn, deps=deps, dsem=dsem, signal=False, val=None, key=None)
        if dsem is not None:
            self.dma_cnt[dsem] = self.dma_cnt.get(dsem, 0) + 1
            op["val"] = 16 * self.dma_cnt[dsem]
            op["key"] = ("d", dsem)
        self.ops.append(op)
        return i

    def finalize(self):
        ops = self.ops
        for op in ops:
            keep = set()
            for d in op["deps"]:
                dop = ops[d]
                if dop["dsem"] is None:
                    if dop["eng"] == op["eng"] and op["dsem"] is None:
                        if op["eng"] == "pe" or not SAME_ENG_SYNC:
                            continue
                    dop["signal"] = True
                keep.add(d)
            op["deps"] = keep
        cnt = {e: 0 for e in self.CENGS}
        self.ekeys = set()
        for op in ops:
            if op["dsem"] is None and op["signal"]:
                c = cnt[op["eng"]]
                cnt[op["eng"]] += 1
                op["key"] = ("e", op["eng"], c // EPOCH)
                op["val"] = c % EPOCH + 1
                self.ekeys.add(op["key"])
        wm = {e: {} for e in self.CENGS}
        for op in ops:
            need = {}
            for d in op["deps"]:
                dop = ops[d]
                need[dop["key"]] = max(need.get(dop["key"], 0), dop["val"])
            waits = []
            for key, v in need.items():
                if wm[op["eng"]].get(key, 0) >= v:
                    continue
                wm[op["eng"]][key] = v
                waits.append((key, v))
            op["waits"] = waits

    def emit(self, nc):
        self.finalize()
        with ExitStack() as es:
            sems = {}
            for k in sorted(self.ekeys):
                sems[k] = es.enter_context(nc.semaphore("s_%s_%d" % (k[1], k[2])))
            for k in self.dma_cnt:
                sems[("d", k)] = es.enter_context(nc.semaphore("d_%s" % k))
            block = es.enter_context(nc.Block())

            def run(ename):
                def f(eng):
                    for op in self.ops:
                        if op["eng"] != ename:
                            continue
                        for key, v in op["waits"]:
                            eng.wait_ge(sems[key], v)
                        if op["fn"] is None:
                            continue
                        ins = op["fn"](eng)
                        if op["dsem"] is not None:
                            ins.then_inc(sems[op["key"]], 16)
                        elif op["signal"]:
                            ins.then_inc(sems[op["key"]], 1)
                return f

            block.tensor(run("pe"))
            block.scalar(run("act"))
            block.vector(run("dve"))
            block.gpsimd(run("pool"))
            block.sync(run("sp"))


D = 1024
KC = 8
DFF = 2816
NSL = 11
SEG = 2048
GT = 512
NG = SEG // GT
EPS = 1e-6
ARENA_WORDS = 53200


def build(nseg, dbg_names=(), only=None, npre=0):
    nc = bass.Bass("TRN2", target_bir_lowering=False)
    NTOK = nseg * SEG

    def din(name, shape):
        return nc.dram_tensor(name, list(shape), F32, kind="ExternalInput").ap()

    xT_d = din("xT", [D, NTOK])
    xp_d = din("xp", [D, max(npre, 1) * SEG])
    w1h_d = din("w1h", [4, NSL, 128, 4096])
    w2h_d = din("w2h", [4, DFF, D])
    wmix_d = din("wmix", [8, 128, 4096])
    wab_d = din("wab", [128, 64])
    norms_d = din("norms", [128, 56])
    convw_d = din("convw", [128, 48])
    small_d = din("small32", [128, 32])
    dnn_d = din("dnn", [128, 1])
    sgnb_d = din("sgnb", [128, 512])
    sgwT_d = din("sgwT", [128, 512])
    sgb_d = din("sgb", [128, 512])
    pw_d = din("pw", [128, 2048])
    pscale_d = din("pscale", [128, 8])
    consts_d = din("consts", [128, 4 * 128 + 128])
    outT_d = nc.dram_tensor("outT", [D, NTOK], F32, kind="ExternalOutput").ap()

    big = nc.alloc_sbuf_tensor("arena", [128, ARENA_WORDS], F32)
    off = [0]

    def a32(n):
        n = (n + 7) // 8 * 8
        v = big[:, off[0]:off[0] + n]
        off[0] += n
        assert off[0] <= ARENA_WORDS, ("arena overflow", off[0])
        return v

    def a16(n):
        w = (n // 2 + 7) // 8 * 8
        v = big[:, off[0]:off[0] + w].bitcast(BF16)[:, 0:n]
        off[0] += w
        assert off[0] <= ARENA_WORDS, ("arena overflow", off[0])
        return v

    psb = [nc.alloc_psum_tensor("ps%d" % i, [128, 512], F32) for i in range(8)]
    P = Prog()
    dbg_outs = {}

    def mm(out, lhsT, rhs, start, stop, r, w):
        P.add("pe", lambda e: e.matmul(out, lhsT=lhsT, rhs=rhs, start=start, stop=stop), r=r, w=w)

    TRP = _os0.environ.get("TRP", "0") == "1"

    def mmT(out, in_, ident, r, w):
        P.add("pe", lambda e: e.transpose(out, in_, ident), r=r, w=w)

    F32R = mybir.dt.float32r
    USE_R = _os0.environ.get("FP32R", "0") == "1"

    def mmr(out, lhsT, rhs, start, stop, r, w):
        if USE_R:
            lhsT = lhsT.bitcast(F32R)
            rhs = rhs.bitcast(F32R)
        P.add("pe", lambda e: e.matmul(out, lhsT=lhsT, rhs=rhs, start=start, stop=stop), r=r, w=w)

    def act(out, in_, func, r, w, bias=None, scale=1.0):
        if bias is None:
            P.add("act", lambda e: e.activation(out=out, in_=in_, func=func, scale=scale), r=r, w=w)
        else:
            P.add("act", lambda e: e.activation(out=out, in_=in_, func=func, bias=bias, scale=scale), r=r, w=w)

    def tt(eng, out, in0, in1, op, r, w):
        P.add(eng, lambda e: e.tensor_tensor(out=out, in0=in0, in1=in1, op=op), r=r, w=w)

    def ts(eng, out, in0, s1, op0, r, w, s2=None, op1=None):
        if op1 is None:
            P.add(eng, lambda e: e.tensor_scalar(out=out, in0=in0, scalar1=s1, scalar2=None, op0=op0), r=r, w=w)
        else:
            P.add(eng, lambda e: e.tensor_scalar(out=out, in0=in0, scalar1=s1, scalar2=s2, op0=op0, op1=op1), r=r, w=w)

    def stt(out, in0, scalar, in1, op0, op1, r, w):
        P.add("dve", lambda e: e.scalar_tensor_tensor(out=out, in0=in0, scalar=scalar, in1=in1, op0=op0, op1=op1), r=r, w=w)

    def cp(eng, out, in_, r, w):
        if eng == "act":
            P.add("act", lambda e: e.copy(out=out, in_=in_), r=r, w=w)
        else:
            P.add(eng, lambda e: e.tensor_copy(out=out, in_=in_), r=r, w=w)

    def recip(out, in_, r, w):
        P.add("dve", lambda e: e.reciprocal(out=out, in_=in_), r=r, w=w)

    def dma(q, out, in_, r, w, dsem):
        P.add(q, lambda e: e.dma_start(out=out, in_=in_), r=r, w=w, dsem=dsem)

    def memset(eng, ap, val, w):
        P.add(eng, lambda e: e.memset(ap, val), w=w)

    dbg_cnt = [0]

    def dbg(name, ap, r):
        if name not in dbg_names or name in dbg_outs:
            return
        shp = list(ap.shape)
        fl = 1
        for s in shp[1:]:
            fl *= s
        t = nc.dram_tensor("dbg_" + name, [shp[0], fl], ap.dtype, kind="ExternalOutput").ap()
        dbg_outs[name] = t
        view = t
        if len(shp) == 3:
            view = t.rearrange("p (a b) -> p a b", a=shp[1])
        elif len(shp) == 4:
            view = t.rearrange("p (a b c) -> p a b c", a=shp[1], b=shp[2])
        dma("sp", view, ap, r, [], "dbg%d" % dbg_cnt[0])
        dbg_cnt[0] += 1
        P.add("sp", None, r=[], w=[])
        P.ops[-1]["deps"] = {len(P.ops) - 2}

    rr = [0]

    hold = {}

    def nb(n=1):
        for _ in range(8):
            i = rr[0] % 8
            rr[0] += 1
            if hold.get("ps%d" % i, 0) == 0:
                hold["ps%d" % i] = n
                return psb[i], "ps%d" % i
        raise RuntimeError("no free PSUM bank")

    def _rh(k):
        if hold.get(k, 0) > 0:
            hold[k] -= 1

    P.read_hook = _rh

    xT = a32(KC * SEG).rearrange("p (c t) -> p c t", c=KC)

    def xk(g):
        return [("x", g, m) for m in range(KC)]

    c_ident = a32(128)
    c_triu = a32(128)
    c_trils = a32(128)
    c_triu128 = a32(128)
    c_invc = a32(128).rearrange("p (c t) -> p c t", c=8)
    ones32 = a32(128)
    onesb = a16(128)
    identb = a16(128)
    epst = a32(8)
    normw = a32(56)
    convw = a32(48).rearrange("p (c j) -> p c j", j=4)
    small = a32(32)
    nealog = a32(16)
    dnn = a32(8)
    sgnb = a32(512)
    sgb = a32(512).rearrange("p (g t) -> p g t", g=4)
    sgwTb = a16(512).rearrange("p (g t) -> p g t", g=4)
    pwb = a16(2048).rearrange("p (g i o) -> p g i o", g=4, i=2)
    pscale = a32(8)
    wab = a16(64).rearrange("p (k n) -> p k n", k=8)
    S32 = a32(512).rearrange("p (h v) -> p h v", h=4)
    Sb = a16(512).rearrange("p (h v) -> p h v", h=4)
    chalo = a32(48).rearrange("p (c j) -> p c j", j=4)
    phalo = a32(128).rearrange("p (c t) -> p c t", c=8)
    base_off = off[0]

    cst = a32(5 * 128)
    dma("sp", cst, consts_d, [], ["cst"], "setup1")
    dma("sp", normw, norms_d, [], ["normw"], "setup2")
    dma("sp", convw.rearrange("p c j -> p (c j)"), convw_d, [], ["convw"], "setup3")
    dma("sp", small, small_d, [], ["small"], "setup4")
    dma("sp", dnn[:, 0:1], dnn_d, [], ["dnn"], "setup5")
    dma("sp", sgnb, sgnb_d, [], ["sgnb"], "setup6")
    dma("sp", sgb.rearrange("p g t -> p (g t)"), sgb_d, [], ["sgb"], "setup7")
    dma("sp", pscale, pscale_d, [], ["pscale"], "setup8")
    dma("pool", wab.rearrange("p k n -> p (k n)"), wab_d, [], ["wab"], "setup9")
    dma("pool", pwb.rearrange("p g i o -> p (g i o)"), pw_d, [], ["pwb"], "setup10")
    sgw_stage = a32(512)
    dma("sp", sgw_stage, sgwT_d, [], ["sgwst"], "setup11")
    cp("dve", c_ident, cst[:, 0:128], ["cst"], ["c_ident"])
    cp("dve", c_triu, cst[:, 128:256], ["cst"], ["c_triu"])
    cp("dve", c_trils, cst[:, 256:384], ["cst"], ["c_trils"])
    cp("dve", c_triu128, cst[:, 384:512], ["cst"], ["c_triu128"])
    cp("dve", c_invc.rearrange("p c t -> p (c t)"), cst[:, 512:640], ["cst"], ["c_invc"])
    cp("dve", identb, cst[:, 0:128], ["cst"], ["identb"])
    memset("dve", ones32, 1.0, ["ones32"])
    memset("dve", onesb, 1.0, ["onesb"])
    memset("dve", epst, EPS, ["epst"])
    memset("dve", S32.rearrange("p h v -> p (h v)"), 0.0, ["S32"])
    memset("dve", Sb.rearrange("p h v -> p (h v)"), 0.0, ["Sb"])
    memset("dve", chalo.rearrange("p c j -> p (c j)"), 0.0, ["chalo"])
    memset("dve", phalo.rearrange("p c t -> p (c t)"), 0.0, ["phalo"])
    act(nealog, small[:, 16:32], AF.Exp, ["small"], ["nealog"])
    ts("dve", nealog, nealog, -1.0, ALU.mult, ["nealog"], ["nealog"])
    tt("dve", sgwTb, sgw_stage.rearrange("p (g t) -> p g t", g=4),
       c_triu128.unsqueeze(1).broadcast_to([128, 4, 128]), ALU.mult, ["sgwst", "c_triu128"], ["sgwTb"])
    P.fence()
    off[0] = base_off
    phase_base = base_off

    def phase_reset():
        off[0] = phase_base

    def norm_group(g, nidx, out_ap, out_keys, sq, rstd, bank, bankk):
        gs = slice(g * GT, (g + 1) * GT)
        act(sq, xT[:, :, gs], AF.Square, xk(g), ["sq"])
        for c in range(KC):
            mm(bank[:], onesb, sq[:, c, :], c == 0, c == KC - 1, ["sq", "onesb"], [bankk])
        act(rstd, bank[:], AF.Ln, [bankk, "epst"], ["rstd"], bias=epst[:, 0:1], scale=1.0 / D)
        act(rstd, rstd, AF.Exp, ["rstd"], ["rstd"], scale=-0.5)
        for c in range(KC):
            stt(out_ap[:, c, :], xT[:, c, gs], normw[:, nidx * 8 + c:nidx * 8 + c + 1], rstd, ALU.mult, ALU.mult,
                [("x", g, c), "normw", "rstd"], [out_keys[c]])

    def ffn(fidx, nidx, groups=None):
        groups = list(range(NG)) if groups is None else list(groups)
        phase_reset()
        xn = a16(KC * SEG).rearrange("p (c t) -> p c t", c=KC)
        sq = a16(KC * GT).rearrange("p (c t) -> p c t", c=KC)
        rstd = a32(GT)
        w1 = [a16(4096).rearrange("p (k n) -> p k n", k=8) for _ in range(2)]
        w2 = [a16(2048).rearrange("p (j n) -> p j n", j=2) for _ in range(2)]
        sg = [a32(GT) for _ in range(2)]
        hT = [a16(2 * GT).rearrange("p (j t) -> p j t", j=2) for _ in range(2)]
        P.fence()

        def load(s):
            sl = s % 2
            dma("pool", w1[sl].rearrange("p k n -> p (k n)"), w1h_d[fidx, s], [], ["w1_%d" % sl], "w1_%d" % sl)
            dma("pool", w2[sl], w2h_d[fidx, s * 256:(s + 1) * 256, :].rearrange("(j p) n -> p j n", p=128),
                [], ["w2_%d" % sl], "w2_%d" % sl)

        load(0)
        load(1)
        for g in groups:
            norm_group(g, nidx, xn[:, :, g * GT:(g + 1) * GT], [("xn", g, c) for c in range(KC)], sq, rstd, psb[4], "ps4")
        its = [(s, g) for s in range(NSL) for g in groups]

        def p1(i, j):
            s, g = its[i]
            sl = s % 2
            hb = i % 2
            gs = slice(g * GT, (g + 1) * GT)
            for which in range(2):
                bi = 2 * j + which
                for k in range(KC):
                    mm(psb[bi][:], w1[sl][:, k, which * 256 + j * 128: which * 256 + (j + 1) * 128],
                       xn[:, k, gs], k == 0, k == KC - 1, ["w1_%d" % sl, ("xn", g, k)], ["ps%d" % bi])
            act(sg[j], psb[2 * j][:], AF.Silu, ["ps%d" % (2 * j)], ["sg%d" % j])
            tt("dve", hT[hb][:, j, :], sg[j], psb[2 * j + 1][:], ALU.mult, ["sg%d" % j, "ps%d" % (2 * j + 1)],
               ["hT%d_%d" % (hb, j)])

        def p2(i, half):
            s, g = its[i]
            sl = s % 2
            hb = i % 2
            gs = slice(g * GT, (g + 1) * GT)
            for m in range(4 * half, 4 * half + 4):
                bi = 4 + (m % 4)
                for j in range(2):
                    mm(psb[bi][:], w2[sl][:, j, m * 128:(m + 1) * 128], hT[hb][:, j, :], j == 0, j == 1,
                       ["w2_%d" % sl, "hT%d_%d" % (hb, j)], ["ps%d" % bi])
                stt(xT[:, m, gs], psb[bi][:], 0.5, xT[:, m, gs], ALU.mult, ALU.add, ["ps%d" % bi, ("x", g, m)], [("x", g, m)])

        def scale_w2(s):
            sl = s % 2
            ts("pool", w2[sl].rearrange("p j n -> p (j n)"), w2[sl].rearrange("p j n -> p (j n)"), 0.5, ALU.mult,
               ["w2_%d" % sl], ["w2_%d" % sl])

        p1(0, 0)
        p1(0, 1)
        for i in range(len(its)):
            s = its[i][0]
            nxt_new = i + 1 < len(its) and its[i + 1][0] != s
            if _os0.environ.get("FSPLIT", "1") == "1":
                if i + 1 < len(its):
                    p1(i + 1, 0)
                p2(i, 0)
                if i + 1 < len(its):
                    p1(i + 1, 1)
                p2(i, 1)
            else:
                if i + 1 < len(its):
                    p1(i + 1, 0)
                    p1(i + 1, 1)
                p2(i, 0)
                p2(i, 1)
            if (i + 1 == len(its) or its[i + 1][0] != s) and s + 2 < NSL:
                load(s + 2)
        P.fence()

    def interleave(*gens):
        gens = [g_ for g_ in gens if g_ is not None]
        while gens:
            for g_ in list(gens):
                try:
                    next(g_)
                except StopIteration:
                    gens.remove(g_)

    def run(gen):
        for _ in gen:
            pass

    def pipeline(items, sets, width):
        free = list(sets)
        active = []
        idx = 0
        while idx < len(items) or active:
            while idx < len(items) and len(active) < width:
                fac, needs, on_start = items[idx]
                if needs and not free:
                    break
                if on_start is not None:
                    on_start()
                st = free.pop(0) if needs else None
                g_ = fac(st)
                idx += 1
                try:
                    next(g_)
                    active.append((g_, st))
                except StopIteration:
                    if st is not None:
                        free.append(st)
            for ent in list(active):
                try:
                    next(ent[0])
                except StopIteration:
                    active.remove(ent)
                    if ent[1] is not None:
                        free.append(ent[1])

    def mixer_ab(full_groups=(0, 1, 2, 3), state_groups=()):
        phase_reset()
        h4 = lambda ap: ap.rearrange("p (h c) -> p h c", h=4)
        wsl = [a16(4096).rearrange("p (k n) -> p k n", k=8) for _ in range(2)]
        qnT = h4(a16(4 * GT))
        kT = h4(a16(4 * GT))
        qdT = h4(a16(4 * GT))
        vT = h4(a16(4 * GT))
        zs = h4(a16(4 * GT))
        gsu = h4(a16(4 * GT))
        svt = a16(4 * GT).rearrange("p (a n) -> p a n", a=4)
        abx = a32(32)
        beta = a32(16)
        gT = a32(16)
        gam = a32(16)
        bg = a32(16)
        kts = a32(16)
        egl = a32(32).rearrange("p (a h e) -> p a h e", a=4, h=4)
        AT4 = [h4(a16(512)) for _ in range(4)]
        ktail = [h4(a16(512)) for _ in range(4)]
        u_sb = [h4(a32(512)) for _ in range(4)]
        wT = [h4(a16(512)) for _ in range(4)]
        vnew2 = [h4(a16(512)) for _ in range(2)]

        def prep_set(tag):
            T_ = [h4(a32(512)) for _ in range(7)]
            d_ = dict(tag=tag, gtri=T_[0], Gb=T_[1], EU=T_[2], EL=T_[3], L4=T_[4], U4=T_[5], X4=T_[6],
                      PP1=T_[0], PT1=T_[1], PP0=T_[2], PT0=T_[3])
            d_["kn"] = dict(gtri="T0", Gb="T1", EU="T2", EL="T3", L4="T4", U4="T5", X4="T6",
                            PP1="T0", PT1="T1", PP0="T2", PT0="T3", TT4="TT4", KbG="KbG", Vb="Vb")
            for nm in ("TT4", "KbG", "Vb"):
                d_[nm] = h4(a16(512))
            return d_

        setA = prep_set("A")
        R0 = off[0]
        hn = a16(KC * GT).rearrange("p (c t) -> p c t", c=KC)
        sq = a16(KC * GT).rearrange("p (c t) -> p c t", c=KC)
        rstd = a32(GT)
        SA = []
        for i_ in range(4):
            pre_ = a32(520)
            cv_ = a32(GT)
            SA.append(dict(i=i_, pre=pre_, sqb=pre_.bitcast(BF16)[:, 0:GT], cv=cv_, rs=cv_, tmpf=cv_, sv_=a32(GT), ss4=a32(8)))
        RA_end = off[0]
        off[0] = R0
        sB0 = off[0]
        setB = prep_set("B")
        setC = prep_set("C")
        sB1 = off[0]
        o_sb = h4(a32(4 * GT))
        RB_end = off[0]
        off[0] = sB0
        SC = [dict(i=i_, sqb=a16(GT), rs=a32(GT), tmpf=a32(GT)) for i_ in range(4)]
        assert off[0] <= sB1
        off[0] = max(RB_end, RA_end)
        P.fence()
        for hf in range(2):
            memset("pool", vnew2[hf].rearrange("p h c -> p (h c)"), 0.0, ["vnew%d" % hf])

        ucnt = [0]

        def load_unit(u):
            sl = ucnt[0] % 2
            ucnt[0] += 1
            dma("pool", wsl[sl].rearrange("p k n -> p (k n)"), wmix_d[u], [], ["wsl%d" % sl], "wsl%d" % sl)
            return sl

        bc4 = lambda ap16, t4: ap16[:, t4 * 4:(t4 + 1) * 4].unsqueeze(2).broadcast_to([128, 4, 128])
        m_u = c_triu.unsqueeze(1).broadcast_to([128, 4, 128])
        m_ls = c_trils.unsqueeze(1).broadcast_to([128, 4, 128])
        i4 = c_ident.unsqueeze(1).broadcast_to([128, 4, 128])
        v4 = lambda bank: bank[:].rearrange("p (h c) -> p h c", h=4)

        def gates_chain():
            bk, bkk = nb(2)
            for t4 in range(4):
                for k in range(KC):
                    mm(bk[:, t4 * 8:(t4 + 1) * 8], hn[:, k, t4 * 128:(t4 + 1) * 128], wab[:, k, :], k == 0, k == KC - 1,
                       [("hn", k), "wab"], [bkk])
            yield
            abv = bk[:, 0:32].rearrange("p (a n) -> p a n", a=4)
            act(beta.rearrange("p (a h) -> p a h", a=4), abv[:, :, 0:4], AF.Sigmoid, [bkk], ["beta"])
            xx = abx[:, 0:16]
            ax = abx[:, 16:32]
            tt("dve", xx.rearrange("p (a h) -> p a h", a=4), abv[:, :, 4:8], small[:, 0:16].rearrange("p (a h) -> p a h", a=4),
               ALU.add, [bkk, "small"], ["abx"])
            yield
            stt(ax, xx, -1.0, xx, ALU.mult, ALU.max, ["abx"], ["abx2"])
            yield
            act(ax, ax, AF.Exp, ["abx2"], ["abx2"], scale=-1.0)
            yield
            act(ax, ax, AF.Ln, ["abx2", "ones32"], ["abx2"], bias=ones32[:, 0:1])
            yield
            stt(gT, xx, 0.0, ax, ALU.max, ALU.add, ["abx", "abx2"], ["gT"])
            yield
            tt("dve", gT, gT, nealog, ALU.mult, ["gT", "nealog"], ["gT"])
            yield
            bk, bkk = nb()
            for t4 in range(4):
                mm(bk[:, t4 * 4:(t4 + 1) * 4], c_triu, gT[:, t4 * 4:(t4 + 1) * 4], True, True, ["c_triu", "gT"], [bkk])
            yield
            cp("dve", gam, bk[:, 0:16], [bkk], ["gam"])
            yield
            act(bg, gam, AF.Exp, ["gam"], ["bg"])
            yield
            tt("dve", bg, bg, beta, ALU.mult, ["bg", "beta"], ["bg"])

        def chain_qkv(u, h, sl, T_):
            i_ = T_["i"]
            pre, cv, sv_, sqb, rs = T_["pre"], T_["cv"], T_["sv_"], T_["sqb"], T_["rs"]
            kp, kcv, ksv = ["%s%d" % (n_, i_) for n_ in ("pre", "cv", "sv_")]
            kph, ksq, krs = kp, kp, kcv
            ch = u * 4 + h
            bk, bkk = nb()
            for k in range(KC):
                mm(bk[:], wsl[sl][:, k, h * 128:(h + 1) * 128], hn[:, k, :], k == 0, k == KC - 1,
                   ["wsl%d" % sl, ("hn", k)], [bkk])
            yield
            cp("act", pre[:, 3:515], bk[:], [bkk], [kp])
            cp("act", pre[:, 0:3], chalo[:, ch, 0:3], ["chalo%d" % ch], [kph])
            yield
            ts("dve", cv, pre[:, 3:515], convw[:, ch, 3:4], ALU.mult, [kp, "convw"], [kcv])
            yield
            for j in range(3):
                stt(cv, pre[:, j:j + 512], convw[:, ch, j:j + 1], cv, ALU.mult, ALU.add, [kp, kph, "convw", kcv], [kcv])
                yield
            cp("pool", chalo[:, ch, 0:3], pre[:, 512:515], [kp], ["chalo%d" % ch])
            if u == 2:
                act(vT[:, h, :], cv, AF.Silu, [kcv], [("vT", h)])
                return
            act(sv_, cv, AF.Silu, [kcv], [ksv])
            yield
            tt("dve", sqb, sv_, sv_, ALU.mult, [ksv], [ksq])
            yield
            b2, b2k = nb()
            mm(b2[:], onesb, sqb, True, True, ["onesb", ksq], [b2k])
            yield
            act(rs, b2[:], AF.Ln, [b2k, "epst"], [krs], bias=epst[:, 0:1])
            yield
            act(rs, rs, AF.Exp, [krs], [krs], scale=-0.5)
            yield
            if u == 0:
                stt(qnT[:, h, :], sv_, float(128 ** -0.5), rs, ALU.mult, ALU.mult, [ksv, krs], [("qnT", h)])
            else:
                tt("dve", kT[:, h, :], sv_, rs, ALU.mult, [ksv, krs], [("kT", h)])

        def chain_zsu(u, h, sl):
            bk, bkk = nb()
            for k in range(KC):
                mm(bk[:], wsl[sl][:, k, h * 128:(h + 1) * 128], hn[:, k, :], k == 0, k == KC - 1,
                   ["wsl%d" % sl, ("hn", k)], [bkk])
            yield
            if u == 3:
                act(zs[:, h, :], bk[:], AF.Silu, [bkk], [("zs", h)])
            else:
                act(gsu[:, h, :], bk[:], AF.Gelu, [bkk], [("gsu", h)])

        def chain_sv(t4, sl, T_):
            i_ = T_["i"]
            sv_, tmpf, ss4 = T_["sv_"], T_["tmpf"], T_["ss4"]
            ksv, ktm, kss = ["%s%d" % (n_, i_) for n_ in ("sv_", "cv", "ss4")]
            bk, bkk = nb()
            for k in range(KC):
                mm(bk[:], hn[:, k, t4 * 128:(t4 + 1) * 128], wsl[sl][:, k, :], k == 0, k == KC - 1,
                   [("hn", k), "wsl%d" % sl], [bkk])
            yield
            act(sv_, bk[:], AF.Gelu, [bkk], [ksv])
            yield
            tt("dve", tmpf, sv_, sv_, ALU.mult, [ksv], [ktm])
            yield
            P.add("dve", lambda e: e.tensor_reduce(out=ss4[:, 0:4], in_=tmpf.rearrange("p (a c) -> p a c", a=4),
                                                  axis=mybir.AxisListType.X, op=ALU.add), r=[ktm], w=[kss])
            yield
            act(ss4[:, 0:4], ss4[:, 0:4], AF.Sqrt, [kss, "epst"], [kss], bias=epst[:, 0:1], scale=1.0 / 128)
            yield
            recip(ss4[:, 0:4], ss4[:, 0:4], [kss], [kss])
            yield
            tt("dve", tmpf.rearrange("p (a c) -> p a c", a=4), sv_.rearrange("p (a c) -> p a c", a=4),
               ss4[:, 0:4].unsqueeze(2).broadcast_to([128, 4, 128]), ALU.mult, [ksv, kss], [ktm])
            yield
            tt("pool", svt[:, t4, :], tmpf, sgnb, ALU.mult, [ktm, "sgnb"], [("svt", t4)])

        def prep(t4, S_, full):
            tg = S_["tag"]
            K_ = lambda nm: S_["kn"][nm] + tg
            ob = t4
            ts_ = slice(t4 * 128, (t4 + 1) * 128)
            gtri, Gb, EU, EL, L4, U4, X4 = S_["gtri"], S_["Gb"], S_["EU"], S_["EL"], S_["L4"], S_["U4"], S_["X4"]
            TT4, KbG, Vb = S_["TT4"], S_["KbG"], S_["Vb"]
            tt("dve", gtri, m_u, bc4(gT, t4), ALU.mult, ["c_triu", "gT"], [K_("gtri")])
            yield
            bB, bBk = nb(3)
            mm(bB[:], ones32, gtri.rearrange("p h c -> p (h c)"), True, True, ["ones32", K_("gtri")], [bBk])
            yield
            B3 = v4(bB)
            tt("dve", EU, B3, bc4(gam, t4), ALU.subtract, [bBk, "gam"], [K_("EU")])
            yield
            tt("dve", EL, bc4(gam, t4), B3, ALU.subtract, [bBk, "gam"], [K_("EL")])
            yield
            act(Gb, B3, AF.Exp, [bBk], [K_("Gb")])
            yield
            act(EU, EU, AF.Exp, [K_("EU")], [K_("EU")])
            yield
            act(EL, EL, AF.Exp, [K_("EL")], [K_("EL")])
            yield
            stt(EU, EU, 1.0, m_u, ALU.min, ALU.mult, [K_("EU"), "c_triu"], [K_("EU")])
            yield
            stt(EL, EL, 1.0, m_ls, ALU.min, ALU.mult, [K_("EL"), "c_trils"], [K_("EL")])
            tt("pool", EL, EL, bc4(beta, t4), ALU.mult, [K_("EL"), "beta"], [K_("EL")])
            cp("pool", egl[:, t4, :, :], Gb[:, :, 63:128:64], [K_("Gb")], [("egl", t4)])
            yield
            if full:
                tt("pool", qdT[:, :, ts_], qnT[:, :, ts_], Gb, ALU.mult, [("qnT", h) for h in range(4)] + [K_("Gb")],
                   [("qdT", t4)])
            cp("pool", kts[0:64, t4 * 4:(t4 + 1) * 4], EU[0:64, :, 63], [K_("EU")], [("kts_a", t4)])
            cp("pool", kts[64:128, t4 * 4:(t4 + 1) * 4], EU[64:128, :, 127], [K_("EU")], [("kts_b", t4)])
            yield
            bK, bKk = nb(2)
            for h in range(4):
                mm(bK[:, h * 128:(h + 1) * 128], kT[:, h, ts_], identb, True, True, [("kT", h), "identb"], [bKk])
            yield
            K3 = v4(bK)
            tt("dve", ktail[ob], K3, bc4(kts, t4), ALU.mult, [bKk, ("kts_a", t4), ("kts_b", t4)], [("ktail", ob)])
            yield
            tt("dve", KbG, K3, bc4(bg, t4), ALU.mult, [bKk, "bg"], [K_("KbG")])
            yield
            bV, bVk = nb()
            for h in range(4):
                mm(bV[:, h * 128:(h + 1) * 128], vT[:, h, ts_], identb, True, True, [("vT", h), "identb"], [bVk])
            yield
            tt("dve", Vb, v4(bV), bc4(beta, t4), ALU.mult, [bVk, "beta"], [K_("Vb")])
            yield
            bKK, bKKk = nb()
            for h in range(4):
                mm(bKK[:, h * 128:(h + 1) * 128], kT[:, h, ts_], kT[:, h, ts_], True, True, [("kT", h)], [bKKk])
            if full:
                bKQ, bKQk = nb()
                for h in range(4):
                    mm(bKQ[:, h * 128:(h + 1) * 128], kT[:, h, ts_], qnT[:, h, ts_], True, True, [("kT", h), ("qnT", h)], [bKQk])
            yield
            tt("dve", L4, v4(bKK), EL, ALU.mult, [bKKk, K_("EL")], [K_("L4")])
            yield
            if full:
                tt("dve", AT4[ob], v4(bKQ), EU, ALU.mult, [bKQk, K_("EU")], [("AT4", ob)])
            yield
            bU, bUk = nb()
            for h in range(4):
                if TRP:
                    mmT(bU[:, h * 128:(h + 1) * 128], L4[:, h, :], c_ident, [K_("L4"), "c_ident"], [bUk])
                else:
                    mmr(bU[:, h * 128:(h + 1) * 128], L4[:, h, :], c_ident, True, True, [K_("L4"), "c_ident"], [bUk])
            yield
            cp("act", U4, v4(bU), [bUk], [K_("U4")])
            yield
            tt("pool", X4, i4, U4, ALU.subtract, ["c_ident", K_("U4")], [K_("X4")])
            yield
            PPb = [S_["PP0"], S_["PP1"]]
            PTb = [S_["PT0"], S_["PT1"]]
            K_ = lambda nm: S_["kn"].get(nm, nm) + tg
            cur = {0: (U4, L4, K_("U4"), K_("L4"))}

            def sqr(i):
                Pc, Ptc, Pck, Ptck = cur[i - 1]
                pn, ptn = PPb[i % 2], PTb[i % 2]
                pnk, ptnk = K_("PP%d" % (i % 2)), K_("PT%d" % (i % 2))
                if TRP:
                    b2, b2k = nb()
                    for h in range(4):
                        mmr(b2[:, h * 128:(h + 1) * 128], Pc[:, h, :], Ptc[:, h, :], True, True, [Pck, Ptck], [b2k])
                    cp("act", ptn, v4(b2), [b2k], [ptnk])
                    if i < 5:
                        b1, b1k = nb()
                        for h in range(4):
                            mmT(b1[:, h * 128:(h + 1) * 128], ptn[:, h, :], c_ident, [ptnk, "c_ident"], [b1k])
                        cp("act", pn, v4(b1), [b1k], [pnk])
                else:
                    if i < 5:
                        b1, b1k = nb()
                        for h in range(4):
                            mmr(b1[:, h * 128:(h + 1) * 128], Ptc[:, h, :], Pc[:, h, :], True, True, [Pck, Ptck], [b1k])
                    b2, b2k = nb()
                    for h in range(4):
                        mmr(b2[:, h * 128:(h + 1) * 128], Pc[:, h, :], Ptc[:, h, :], True, True, [Pck, Ptck], [b2k])
                    if i < 5:
                        cp("act", pn, v4(b1), [b1k], [pnk])
                    cp("act", ptn, v4(b2), [b2k], [ptnk])
                cur[i] = (pn, ptn, pnk, ptnk)

            def xpm(i):
                ptn, ptnk = cur[i][1], cur[i][3]
                b3, b3k = nb()
                for h in range(4):
                    mmr(b3[:, h * 128:(h + 1) * 128], ptn[:, h, :], X4[:, h, :], True, True, [ptnk, K_("X4")], [b3k])
                return b3, b3k

            def xadd(i, b3, b3k):
                if i < 5:
                    tt("dve", X4, X4, v4(b3), ALU.add, [K_("X4"), b3k], [K_("X4")])
                else:
                    tt("dve", TT4, X4, v4(b3), ALU.add, [K_("X4"), b3k], [K_("TT4")])

            sqr(1)
            yield
            for i in range(1, 6):
                if i + 1 <= 5:
                    sqr(i + 1)
                    yield
                b3, b3k = xpm(i)
                yield
                xadd(i, b3, b3k)
                yield
            bu, buk = nb()
            for h in range(4):
                mm(bu[:, h * 128:(h + 1) * 128], TT4[:, h, :], Vb[:, h, :], True, True, [K_("TT4"), K_("Vb")], [buk])
            bw, bwk = nb()
            for h in range(4):
                mm(bw[:, h * 128:(h + 1) * 128], KbG[:, h, :], TT4[:, h, :], True, True, [K_("KbG"), K_("TT4")], [bwk])
            yield
            cp("act", u_sb[ob], v4(bu), [buk], [("u_sb", ob)])
            cp("act", wT[ob], v4(bw), [bwk], [("wT", ob)])

        def scan(t4, full):
            ob = t4
            for half in range(2):
                r0 = half * 64
                rs_ = slice(r0, r0 + 64)
                n = t4 * 2 + half
                vnew = vnew2[half]
                vk = "vnew%d" % half
                bpw, bpwk = nb()
                for h in range(4):
                    mm(bpw[:, h * 128:(h + 1) * 128], wT[ob][:, h, :], Sb[:, h, :], True, True, [("wT", ob), "Sb"], [bpwk])
                tt("dve", vnew[rs_], u_sb[ob][rs_], bpw[rs_, :].rearrange("p (h c) -> p h c", h=4), ALU.subtract,
                   [("u_sb", ob), bpwk], [vk])
                if full:
                    bo, bok = nb()
                    for h in range(4):
                        mm(bo[:, h * 64:(h + 1) * 64], Sb[:, h, :], qdT[:, h, n * 64:(n + 1) * 64], True, False,
                           ["Sb", ("qdT", t4)], [bok])
                        mm(bo[:, h * 64:(h + 1) * 64], vnew[:, h, :], AT4[ob][:, h, r0:r0 + 64], False, True,
                           [vk, ("AT4", ob)], [bok])
                    cp("act", o_sb[:, :, n * 64:(n + 1) * 64], bo[:, 0:256].rearrange("p (h c) -> p h c", h=4), [bok],
                       [("o_sb", n)])
                bs, bsk = nb()
                for h in range(4):
                    mm(bs[:, h * 128:(h + 1) * 128], ktail[ob][:, h, :], vnew[:, h, :], True, True, [("ktail", ob), vk], [bsk])
                tt("pool", S32, S32, egl[:, t4, :, half:half + 1].broadcast_to([128, 4, 128]), ALU.mult,
                   ["S32", ("egl", t4)], ["S32"])
                tt("dve", S32, S32, v4(bs), ALU.add, ["S32", bsk], ["S32"])
                cp("act", Sb, S32, ["S32"], ["Sb"])
                yield

        osk = [("o_sb", n) for n in range(8)]

        def chain_onorm(h, T_):
            i_ = T_["i"]
            sqb, rs, tmpf = T_["sqb"], T_["rs"], T_["tmpf"]
            ksq, krs, ktm = ["%s%d" % (n_, i_) for n_ in ("csqb", "crs", "ctmpf")]
            tt("dve", sqb, o_sb[:, h, :], o_sb[:, h, :], ALU.mult, osk, [ksq])
            yield
            b2, b2k = nb()
            mm(b2[:], onesb, sqb, True, True, ["onesb", ksq], [b2k])
            yield
            act(rs, b2[:], AF.Ln, [b2k, "epst"], [krs], bias=epst[:, 0:1], scale=1.0 / 128)
            yield
            act(rs, rs, AF.Exp, [krs], [krs], scale=-0.5)
            yield
            stt(tmpf, o_sb[:, h, :], dnn[:, 0:1], rs, ALU.mult, ALU.mult, osk + ["dnn", krs], [ktm])
            yield
            tt("pool", zs[:, h, :], tmpf, zs[:, h, :], ALU.mult, [ktm, ("zs", h)], [("zs", h)])

        def chain_sgmix(gi, T_):
            i_ = T_["i"]
            tmpf = T_["tmpf"]
            ktm = "ctmpf%d" % i_
            bk, bkk = nb()
            for t4 in range(4):
                mm(bk[:, t4 * 128:(t4 + 1) * 128], svt[:, t4, gi * 128:(gi + 1) * 128], sgwTb[:, gi, :], True, True,
                   [("svt", t4), "sgwTb"], [bkk])
            yield
            tt("dve", tmpf.rearrange("p (a t) -> p a t", a=4), bk[:].rearrange("p (a t) -> p a t", a=4),
               sgb[:, gi, :].unsqueeze(1).broadcast_to([128, 4, 128]), ALU.add, [bkk, "sgb"], [ktm])
            yield
            tt("pool", gsu[:, gi, :], tmpf, gsu[:, gi, :], ALU.mult, [ktm, ("gsu", gi)], [("gsu", gi)])

        groups = sorted(set(full_groups) | set(state_groups))
        MIXCUT = int(_os0.environ.get("MIXCUT", "0"))

        def seqg(*gens):
            for g_ in gens:
                yield from g_

        for g in groups:
            full = g in full_groups
            gs = slice(g * GT, (g + 1) * GT)
            ulist = [0, 1, 2, 3, 4, 5, 6, 7] if full else [1, 2]
            norm_group(g, 1, hn, [("hn", c) for c in range(KC)], sq, rstd, psb[0], "ps0")
            slots = {}
            slots[ulist[0]] = load_unit(ulist[0])
            items = [(lambda st: gates_chain(), False, None)]
            for ui, u in enumerate(ulist):
                if u >= 6:
                    break

                def on_start(ui=ui):
                    if ui + 1 < len(ulist):
                        slots[ulist[ui + 1]] = load_unit(ulist[ui + 1])

                for h in range(4):
                    osf = on_start if h == 0 else None
                    if u <= 2:
                        items.append((lambda st, u=u, h=h: chain_qkv(u, h, slots[u], st), True, osf))
                    elif u <= 4:
                        items.append((lambda st, u=u, h=h: chain_zsu(u, h, slots[u]), False, osf))
                    else:
                        items.append((lambda st, u=u, h=h: chain_sv(h, slots[u], st), True, osf))
            pipeline(items, SA, int(_os0.environ.get("PW", "5")))
            P.fence(dma=False)
            if MIXCUT == 1:
                break
            if _os0.environ.get("SEQB", "0") == "1":
                for t4_ in range(4):
                    run(prep(t4_, [setA, setB, setC, setA][t4_], full))
                    run(scan(t4_, full))
            else:
                interleave(prep(0, setA, full), prep(1, setB, full), prep(2, setC, full))
                interleave(seqg(scan(0, full), scan(1, full), scan(2, full)), prep(3, setA, full))
                run(scan(3, full))
            P.fence(dma=False)
            if MIXCUT == 2:
                break
            if full:
                interleave(*[chain_onorm(h, SC[h]) for h in range(4)])
                interleave(*[chain_sgmix(gi, SC[gi]) for gi in range(4)])
                for uo in range(2):
                    u = 6 + uo
                    if uo == 0:
                        slots[7] = load_unit(7)
                    sl = slots[u]
                    for mq in range(4):
                        m = uo * 4 + mq
                        bk, bkk = nb()
                        for k in range(8):
                            rhs = zs[:, k, :] if k < 4 else gsu[:, k - 4, :]
                            rk = ("zs", k) if k < 4 else ("gsu", k - 4)
                            mm(bk[:], wsl[sl][:, k, mq * 128:(mq + 1) * 128], rhs, k == 0, k == 7, ["wsl%d" % sl, rk], [bkk])
                        tt("dve", xT[:, m, gs], xT[:, m, gs], bk[:], ALU.add, [("x", g, m), bkk], [("x", g, m)])
            P.fence(dma=False)
        P.fence()

    def mixer_pool(first_seg):
        phase_reset()
        HW = 528
        hp = a32(KC * HW).rearrange("p (c t) -> p c t", c=KC)
        bufA = a32(KC * HW).rearrange("p (c t) -> p c t", c=KC)
        bufB = a32(KC * HW).rearrange("p (c t) -> p c t", c=KC)
        pooled = a16(KC * GT).rearrange("p (c t) -> p c t", c=KC)
        sq = a16(KC * GT).rearrange("p (c t) -> p c t", c=KC)
        rstd = a32(GT)
        tmp16 = a32(KC * 16).rearrange("p (c t) -> p c t", c=KC)
        tmpo = a32(GT)
        P.fence()
        for g in range(NG):
            gs = slice(g * GT, (g + 1) * GT)
            cp("pool", hp[:, :, 0:16], phalo, ["phalo"], ["hp_h"])
            norm_group(g, 4, hp[:, :, 16:HW], [("hp", c) for c in range(KC)], sq, rstd, psb[0], "ps0")
            hpk = [("hp", c) for c in range(KC)] + ["hp_h"]
            cp("pool", phalo, hp[:, :, GT:HW], hpk, ["phalo"])
            import os
            CUT = int(os.environ.get("POOLCUT", "9"))
            if CUT <= 1:
                continue
            tt("dve", bufA[:, :, 1:HW], hp[:, :, 1:HW], hp[:, :, 0:HW - 1], ALU.add, hpk, ["bufA", "bufA2"])
            tt("pool", bufB[:, 2:8, 3:HW], bufA[:, 2:8, 3:HW], bufA[:, 2:8, 1:HW - 2], ALU.add, ["bufA"], ["bufB", "bufB2"])
            tt("dve", bufA[:, 4:8, 7:HW], bufB[:, 4:8, 7:HW], bufB[:, 4:8, 3:HW - 4], ALU.add, ["bufB", "bufA"], ["bufA2"])
            tt("pool", bufB[:, 6:8, 15:HW], bufA[:, 6:8, 15:HW], bufA[:, 6:8, 7:HW - 8], ALU.add, ["bufA2", "bufB"], ["bufB2"])
            srcs = [(bufA, ["bufA"]), (bufB, ["bufB"]), (bufA, ["bufA2"]), (bufB, ["bufB2"])]
            if CUT <= 2:
                continue
            for gi in range(4):
                win = 2 ** (gi + 1)
                src, sk = srcs[gi]
                cs = slice(2 * gi, 2 * gi + 2)
                stt(pooled[:, cs, :], src[:, cs, 16:HW], 1.0 / win, hp[:, cs, 16:HW], ALU.mult, ALU.subtract,
                    sk + hpk, [("pooled", gi)])
                if first_seg and g == 0:
                    tt("dve", tmp16[:, cs, :], src[:, cs, 16:32], c_invc[:, cs, :], ALU.mult, sk + ["c_invc"], ["tmp16"])
                    tt("dve", pooled[:, cs, 0:16], tmp16[:, cs, :], hp[:, cs, 16:32], ALU.subtract, ["tmp16"] + hpk,
                       [("pooled", gi)])
            if CUT <= 3:
                continue
            for gi in range(4):
                for oc in range(2):
                    m = 2 * gi + oc
                    bk, bkk = nb()
                    for ic in range(2):
                        mm(bk[:], pwb[:, gi, ic, oc * 128:(oc + 1) * 128], pooled[:, 2 * gi + ic, :], ic == 0, ic == 1,
                           ["pwb", ("pooled", gi)], [bkk])
                    if CUT == 10:
                        stt(xT[:, m, gs], bk[:], pscale[:, m:m + 1], xT[:, m, gs], ALU.mult, ALU.add, [bkk, "pscale", ("x", g, m)], [("x", g, m)])
                    elif CUT == 11:
                        stt(xT[:, m, gs], bk[:], pscale[:, m:m + 1], xT[:, m, gs], ALU.mult, ALU.add, [bkk, "normw", ("x", g, m)], [("x", g, m)])
                    elif CUT == 6:
                        stt(xT[:, m, gs], bk[:], 0.5, xT[:, m, gs], ALU.mult, ALU.add, [bkk, ("x", g, m)], [("x", g, m)])
                    elif CUT == 7:
                        stt(xT[:, m, gs], bk[:], normw[:, m:m + 1], xT[:, m, gs], ALU.mult, ALU.add, [bkk, "normw", ("x", g, m)], [("x", g, m)])
                    elif CUT == 8:
                        act(tmpo, bk[:], AF.Copy, [bkk], ["tmpo"])
                        stt(xT[:, m, gs], tmpo, pscale[:, m:m + 1], xT[:, m, gs], ALU.mult, ALU.add, ["tmpo", "pscale", ("x", g, m)], [("x", g, m)])
                    elif CUT == 4:
                        tt("dve", xT[:, m, gs], xT[:, m, gs], bk[:], ALU.add, [bkk, ("x", g, m)], [("x", g, m)])
                    elif CUT == 5:
                        pass
                    else:
                        act(tmpo, bk[:], AF.Copy, [bkk, "pscale"], ["tmpo"], scale=pscale[:, m:m + 1])
                        tt("dve", xT[:, m, gs], xT[:, m, gs], tmpo, ALU.add, ["tmpo", ("x", g, m)], [("x", g, m)])
        P.fence()

    def final_out(seg):
        phase_reset()
        ob = [a32(KC * GT).rearrange("p (c t) -> p c t", c=KC) for _ in range(2)]
        sq = a16(KC * GT).rearrange("p (c t) -> p c t", c=KC)
        rstd = a32(GT)
        P.fence()
        for g in range(NG):
            b = g % 2
            norm_group(g, 6, ob[b], [("ob", b, c) for c in range(KC)], sq, rstd, psb[0], "ps0")
            t0 = seg * SEG + g * GT
            dma("sp", outT_d[:, t0:t0 + GT].rearrange("(c p) t -> p c t", p=128), ob[b],
                [("ob", b, c) for c in range(KC)], [], "out%d" % b)
        P.fence()

    def pool_halo_only(g):
        phase_reset()
        HW = 528
        hp = a32(KC * HW).rearrange("p (c t) -> p c t", c=KC)
        sq = a16(KC * GT).rearrange("p (c t) -> p c t", c=KC)
        rstd = a32(GT)
        P.fence()
        norm_group(g, 4, hp[:, :, 16:HW], [("hp", c) for c in range(KC)], sq, rstd, psb[0], "ps0")
        cp("pool", phalo, hp[:, :, GT:HW], [("hp", c) for c in range(KC)], ["phalo"])
        P.fence()

    allx = [k for g in range(NG) for k in xk(g)]
    inc = lambda nm: only is None or nm in only
    for ps in range(npre):
        last = ps == npre - 1
        for g_ in range(NG):
            t0_ = ps * SEG + g_ * GT
            dma("sp", xT[:, :, g_ * GT:(g_ + 1) * GT], xp_d[:, t0_:t0_ + GT].rearrange("(c p) t -> p c t", p=128),
                [], xk(g_), "xload%d" % g_)
        ffn(0, 0)
        if not last:
            mixer_ab(full_groups=(), state_groups=(0, 1, 2, 3))
        else:
            mixer_ab(full_groups=(NG - 1,), state_groups=tuple(range(NG - 1)))
            ffn(1, 2, groups=[NG - 1])
            ffn(2, 3, groups=[NG - 1])
            pool_halo_only(NG - 1)
    for seg in range(nseg):
        for g_ in range(NG):
            t0_ = seg * SEG + g_ * GT
            dma("sp", xT[:, :, g_ * GT:(g_ + 1) * GT], xT_d[:, t0_:t0_ + GT].rearrange("(c p) t -> p c t", p=128),
                [], xk(g_), "xload%d" % g_)
        if inc("ffn0"):
            ffn(0, 0)
        if inc("mix0"):
            mixer_ab()
        if inc("ffn1"):
            ffn(1, 2)
        if inc("ffn2"):
            ffn(2, 3)
        if inc("pool"):
            mixer_pool(seg == 0)
        if inc("ffn3"):
            ffn(3, 5)
        final_out(seg)
    P.fence()
    P.add("sp", None)
    P.emit(nc)
    return nc, dbg_outs


def host_prep(inp):
    f = lambda a: np.ascontiguousarray(np.asarray(a, dtype=np.float32))
    w_in = [inp["ffn1_w_in"][0], inp["ffn2_w_in"][0], inp["ffn1_w_in"][1], inp["ffn2_w_in"][1]]
    w_out = [inp["ffn1_w_out"][0], inp["ffn2_w_out"][0], inp["ffn1_w_out"][1], inp["ffn2_w_out"][1]]
    w1h = np.empty((4, NSL, 128, 8, 512), np.float32)
    for i, w in enumerate(w_in):
        w = np.asarray(w, np.float32)
        gate = w[:, :DFF].reshape(8, 128, NSL, 256)
        up = w[:, DFF:].reshape(8, 128, NSL, 256)
        w1h[i, :, :, :, :256] = gate.transpose(2, 1, 0, 3)
        w1h[i, :, :, :, 256:] = up.transpose(2, 1, 0, 3)
    w1h = w1h.reshape(4, NSL, 128, 4096)
    w2h = np.stack([np.asarray(w, np.float32) for w in w_out])
    wi = np.asarray(inp["ab_w_in"][0], np.float32)
    wo = np.asarray(inp["ab_w_out"][0], np.float32)
    bases = [0, 512, 1024, 1536, 2056, 2568]
    wmix = np.empty((8, 128, 8, 512), np.float32)
    for u, b in enumerate(bases):
        wmix[u] = wi[:, b:b + 512].reshape(8, 128, 512).transpose(1, 0, 2)
    for u in range(2):
        wmix[6 + u] = wo[:, u * 512:(u + 1) * 512].reshape(8, 128, 512).transpose(1, 0, 2)
    wmix = wmix.reshape(8, 128, 4096)
    wab = wi[:, 2048:2056].reshape(8, 128, 8).transpose(1, 0, 2).reshape(128, 64)
    nl = [inp["ffn_norm1"][0], inp["mix_norm"][0], inp["ffn_norm2"][0],
          inp["ffn_norm1"][1], inp["mix_norm"][1], inp["ffn_norm2"][1], inp["final_norm"]]
    norms = np.concatenate([np.asarray(v, np.float32).reshape(8, 128).T for v in nl], axis=1)
    convw = np.asarray(inp["dn_conv_w"][0], np.float32).reshape(4, 12, 128).transpose(2, 1, 0).reshape(128, 48)
    small = np.empty((128, 32), np.float32)
    small[:, 0:16] = np.tile(np.asarray(inp["dn_dt_bias"][0], np.float32), 4)[None, :]
    small[:, 16:32] = np.tile(np.asarray(inp["dn_a_log"][0], np.float32), 4)[None, :]
    dnn = np.asarray(inp["dn_out_norm"][0], np.float32).reshape(128, 1)
    sgnb = np.broadcast_to(np.asarray(inp["sg_norm"][0], np.float32).reshape(1, 512), (128, 512))
    sgwT = np.asarray(inp["sg_w"][0], np.float32).transpose(2, 0, 1).reshape(128, 512)
    sgb = np.broadcast_to(np.asarray(inp["sg_b"][0], np.float32).reshape(1, 512), (128, 512))
    pw = np.asarray(inp["pool_w"][0], np.float32).reshape(4, 2, 128, 256).transpose(2, 0, 1, 3).reshape(128, 2048)
    pscale = np.asarray(inp["pool_scale"][0], np.float32).reshape(8, 128).T
    idx = np.arange(128)
    same = (idx[:, None] // 64) == (idx[None, :] // 64)
    ident = np.eye(128, dtype=np.float32)
    triu = ((idx[:, None] <= idx[None, :]) & same).astype(np.float32)
    trils = ((idx[None, :] < idx[:, None]) & same).astype(np.float32)
    triu128 = (idx[:, None] <= idx[None, :]).astype(np.float32)
    invc = np.empty((128, 8, 16), np.float32)
    pos = np.arange(1, 17, dtype=np.float32)
    for c in range(8):
        win = 2 ** (c // 2 + 1)
        invc[:, c, :] = (1.0 / np.minimum(pos, win))[None, :]
    consts = np.concatenate([ident, triu, trils, triu128, invc.reshape(128, 128)], axis=1)
    return dict(w1h=f(w1h), w2h=f(w2h), wmix=f(wmix), wab=f(wab), norms=f(norms), convw=f(convw), small32=f(small),
                dnn=f(dnn), sgnb=f(sgnb), sgwT=f(sgwT), sgb=f(sgb), pw=f(pw), pscale=f(pscale), consts=f(consts))


_CACHE = {}
NPRE = 2
NOWN = 2


def kernel(**inputs):
    x = np.asarray(inputs["x"], np.float32)
    B, T, _ = x.shape
    half_t = T // 2
    assert half_t == NOWN * SEG
    shared = host_prep(inputs)
    key = (NPRE, NOWN)
    if key not in _CACHE:
        _CACHE[key] = build(NOWN, npre=NPRE)[0]
    nc = _CACHE[key]
    consts_a = shared["consts"]
    consts_b = consts_a.copy()
    invc_b = np.empty((128, 8, 16), np.float32)
    for c in range(8):
        invc_b[:, c, :] = 1.0 / (2 ** (c // 2 + 1))
    consts_b[:, 512:640] = invc_b.reshape(128, 128)
    in_maps = []
    for b in range(B):
        for h in range(2):
            m = dict(shared)
            m["xT"] = np.ascontiguousarray(x[b, h * half_t:(h + 1) * half_t].T)
            if h == 0:
                m["xp"] = np.zeros((D, NPRE * SEG), np.float32)
                m["consts"] = consts_a
            else:
                m["xp"] = np.ascontiguousarray(x[b, 0:half_t].T)
                m["consts"] = consts_b
            in_maps.append(m)
    res = run_bass_kernel_spmd(nc, in_maps, core_ids=list(range(2 * B)))
    out = np.empty((B, T, D), np.float32)
    for b in range(B):
        for h in range(2):
            out[b, h * half_t:(h + 1) * half_t] = res.results[2 * b + h]["outT"].T
    return out
```

```python
import numpy as np
import concourse.bass as bass
import concourse.mybir as mybir
from concourse.bass_utils import run_bass_kernel_spmd
from contextlib import ExitStack

F32 = mybir.dt.float32
BF16 = mybir.dt.bfloat16
AF = mybir.ActivationFunctionType
ALU = mybir.AluOpType

import os as _os0
SAME_ENG_SYNC = _os0.environ.get("SES", "1") == "1"
EPOCH = 12000


class Prog:
    CENGS = ("pe", "act", "dve", "pool", "sp")

    def __init__(self):
        self.ops = []
        self.lastw = {}
        self.readers = {}
        self.dma_cnt = {}
        self.pending = {}
        self.read_hook = None

    def fence(self, dma=True):
        last = {}
        for i, op in enumerate(self.ops):
            if op["fn"] is None:
                continue
            if op["dsem"] is not None and (not dma or str(op["dsem"]).startswith("xload")):
                continue
            k = ("d", op["dsem"]) if op["dsem"] is not None else ("e", op["eng"])
            last[k] = i
        self.pending = {e: set(last.values()) for e in self.CENGS}

    def add(self, eng, fn, r=(), w=(), dsem=None):
        i = len(self.ops)
        deps = set()
        if eng != "pe" and self.read_hook is not None:
            for k in r:
                if isinstance(k, str) and k.startswith("ps"):
                    self.read_hook(k)
        if eng != "pe":
            w = list(w) + [k for k in r if isinstance(k, str) and k.startswith("ps") and k not in w]
        if eng in self.pending:
            deps |= self.pending.pop(eng)
        for k in r:
            d = self.lastw.get(k)
            if d is not None:
                deps.add(d)
        for k in w:
            d = self.lastw.get(k)
            if d is not None:
                deps.add(d)
            for d in self.readers.get(k, ()):
                deps.add(d)
        for k in r:
            self.readers.setdefault(k, []).append(i)
        for k in w:
            self.lastw[k] = i
            self.readers[k] = []
        op = dict(eng=eng, fn=fn, deps=deps, dsem=dsem, signal=False, val=None, key=None)
        if dsem is not None:
            self.dma_cnt[dsem] = self.dma_cnt.get(dsem, 0) + 1
            op["val"] = 16 * self.dma_cnt[dsem]
            op["key"] = ("d", dsem)
        self.ops.append(op)
        return i

    def finalize(self):
        ops = self.ops
        for op in ops:
            keep = set()
            for d in op["deps"]:
                dop = ops[d]
                if dop["dsem"] is None:
                    if dop["eng"] == op["eng"] and op["dsem"] is None:
                        if op["eng"] == "pe" or not SAME_ENG_SYNC:
                            continue
                    dop["signal"] = True
                keep.add(d)
            op["deps"] = keep
        cnt = {e: 0 for e in self.CENGS}
        self.ekeys = set()
        for op in ops:
            if op["dsem"] is None and op["signal"]:
                c = cnt[op["eng"]]
                cnt[op["eng"]] += 1
                op["key"] = ("e", op["eng"], c // EPOCH)
                op["val"] = c % EPOCH + 1
                self.ekeys.add(op["key"])
        wm = {e: {} for e in self.CENGS}
        for op in ops:
            need = {}
            for d in op["deps"]:
                dop = ops[d]
                need[dop["key"]] = max(need.get(dop["key"], 0), dop["val"])
            waits = []
            for key, v in need.items():
                if wm[op["eng"]].get(key, 0) >= v:
                    continue
                wm[op["eng"]][key] = v
                waits.append((key, v))
            op["waits"] = waits

    def emit(self, nc):
        self.finalize()
        with ExitStack() as es:
            sems = {}
            for k in sorted(self.ekeys):
                sems[k] = es.enter_context(nc.semaphore("s_%s_%d" % (k[1], k[2])))
            for k in self.dma_cnt:
                sems[("d", k)] = es.enter_context(nc.semaphore("d_%s" % k))
            block = es.enter_context(nc.Block())

            def run(ename):
                def f(eng):
                    for op in self.ops:
                        if op["eng"] != ename:
                            continue
                        for key, v in op["waits"]:
                            eng.wait_ge(sems[key], v)
                        if op["fn"] is None:
                            continue
                        ins = op["fn"](eng)
                        if op["dsem"] is not None:
                            ins.then_inc(sems[op["key"]], 16)
                        elif op["signal"]:
                            ins.then_inc(sems[op["key"]], 1)
                return f

            block.tensor(run("pe"))
            block.scalar(run("act"))
            block.vector(run("dve"))
            block.gpsimd(run("pool"))
            block.sync(run("sp"))


D = 1024
KC = 8
DFF = 2816
NSL = 11
SEG = 2048
GT = 512
NG = SEG // GT
EPS = 1e-6
ARENA_WORDS = 53200


def build(nseg, dbg_names=(), only=None, npre=0):
    nc = bass.Bass("TRN2", target_bir_lowering=False)
    NTOK = nseg * SEG

    def din(name, shape):
        return nc.dram_tensor(name, list(shape), F32, kind="ExternalInput").ap()

    xT_d = din("xT", [D, NTOK])
    xp_d = din("xp", [D, max(npre, 1) * SEG])
    w1h_d = din("w1h", [4, NSL, 128, 4096])
    w2h_d = din("w2h", [4, DFF, D])
    wmix_d = din("wmix", [8, 128, 4096])
    wab_d = din("wab", [128, 64])
    norms_d = din("norms", [128, 56])
    convw_d = din("convw", [128, 48])
    small_d = din("small32", [128, 32])
    dnn_d = din("dnn", [128, 1])
    sgnb_d = din("sgnb", [128, 512])
    sgwT_d = din("sgwT", [128, 512])
    sgb_d = din("sgb", [128, 512])
    pw_d = din("pw", [128, 2048])
    pscale_d = din("pscale", [128, 8])
    consts_d = din("consts", [128, 4 * 128 + 128])
    outT_d = nc.dram_tensor("outT", [D, NTOK], F32, kind="ExternalOutput").ap()

    big = nc.alloc_sbuf_tensor("arena", [128, ARENA_WORDS], F32)
    off = [0]

    def a32(n):
        n = (n + 7) // 8 * 8
        v = big[:, off[0]:off[0] + n]
        off[0] += n
        assert off[0] <= ARENA_WORDS, ("arena overflow", off[0])
        return v

    def a16(n):
        w = (n // 2 + 7) // 8 * 8
        v = big[:, off[0]:off[0] + w].bitcast(BF16)[:, 0:n]
        off[0] += w
        assert off[0] <= ARENA_WORDS, ("arena overflow", off[0])
        return v

    psb = [nc.alloc_psum_tensor("ps%d" % i, [128, 512], F32) for i in range(8)]
    P = Prog()
    dbg_outs = {}

    def mm(out, lhsT, rhs, start, stop, r, w):
        P.add("pe", lambda e: e.matmul(out, lhsT=lhsT, rhs=rhs, start=start, stop=stop), r=r, w=w)

    TRP = _os0.environ.get("TRP", "0") == "1"

    def mmT(out, in_, ident, r, w):
        P.add("pe", lambda e: e.transpose(out, in_, ident), r=r, w=w)

    F32R = mybir.dt.float32r
    USE_R = _os0.environ.get("FP32R", "0") == "1"

    def mmr(out, lhsT, rhs, start, stop, r, w):
        if USE_R:
            lhsT = lhsT.bitcast(F32R)
            rhs = rhs.bitcast(F32R)
        P.add("pe", lambda e: e.matmul(out, lhsT=lhsT, rhs=rhs, start=start, stop=stop), r=r, w=w)

    def act(out, in_, func, r, w, bias=None, scale=1.0):
        if bias is None:
            P.add("act", lambda e: e.activation(out=out, in_=in_, func=func, scale=scale), r=r, w=w)
        else:
            P.add("act", lambda e: e.activation(out=out, in_=in_, func=func, bias=bias, scale=scale), r=r, w=w)

    def tt(eng, out, in0, in1, op, r, w):
        P.add(eng, lambda e: e.tensor_tensor(out=out, in0=in0, in1=in1, op=op), r=r, w=w)

    def ts(eng, out, in0, s1, op0, r, w, s2=None, op1=None):
        if op1 is None:
            P.add(eng, lambda e: e.tensor_scalar(out=out, in0=in0, scalar1=s1, scalar2=None, op0=op0), r=r, w=w)
        else:
            P.add(eng, lambda e: e.tensor_scalar(out=out, in0=in0, scalar1=s1, scalar2=s2, op0=op0, op1=op1), r=r, w=w)

    def stt(out, in0, scalar, in1, op0, op1, r, w):
        P.add("dve", lambda e: e.scalar_tensor_tensor(out=out, in0=in0, scalar=scalar, in1=in1, op0=op0, op1=op1), r=r, w=w)

    def cp(eng, out, in_, r, w):
        if eng == "act":
            P.add("act", lambda e: e.copy(out=out, in_=in_), r=r, w=w)
        else:
            P.add(eng, lambda e: e.tensor_copy(out=out, in_=in_), r=r, w=w)

    def recip(out, in_, r, w):
        P.add("dve", lambda e: e.reciprocal(out=out, in_=in_), r=r, w=w)

    def dma(q, out, in_, r, w, dsem):
        P.add(q, lambda e: e.dma_start(out=out, in_=in_), r=r, w=w, dsem=dsem)

    def memset(eng, ap, val, w):
        P.add(eng, lambda e: e.memset(ap, val), w=w)

    dbg_cnt = [0]

    def dbg(name, ap, r):
        if name not in dbg_names or name in dbg_outs:
            return
        shp = list(ap.shape)
        fl = 1
        for s in shp[1:]:
            fl *= s
        t = nc.dram_tensor("dbg_" + name, [shp[0], fl], ap.dtype, kind="ExternalOutput").ap()
        dbg_outs[name] = t
        view = t
        if len(shp) == 3:
            view = t.rearrange("p (a b) -> p a b", a=shp[1])
        elif len(shp) == 4:
            view = t.rearrange("p (a b c) -> p a b c", a=shp[1], b=shp[2])
        dma("sp", view, ap, r, [], "dbg%d" % dbg_cnt[0])
        dbg_cnt[0] += 1
        P.add("sp", None, r=[], w=[])
        P.ops[-1]["deps"] = {len(P.ops) - 2}

    rr = [0]

    hold = {}

    def nb(n=1):
        for _ in range(8):
            i = rr[0] % 8
            rr[0] += 1
            if hold.get("ps%d" % i, 0) == 0:
                hold["ps%d" % i] = n
                return psb[i], "ps%d" % i
        raise RuntimeError("no free PSUM bank")

    def _rh(k):
        if hold.get(k, 0) > 0:
            hold[k] -= 1

    P.read_hook = _rh

    xT = a32(KC * SEG).rearrange("p (c t) -> p c t", c=KC)

    def xk(g):
        return [("x", g, m) for m in range(KC)]

    c_ident = a32(128)
    c_triu = a32(128)
    c_trils = a32(128)
    c_triu128 = a32(128)
    c_invc = a32(128).rearrange("p (c t) -> p c t", c=8)
    ones32 = a32(128)
    onesb = a16(128)
    identb = a16(128)
    epst = a32(8)
    normw = a32(56)
    convw = a32(48).rearrange("p (c j) -> p c j", j=4)
    small = a32(32)
    nealog = a32(16)
    dnn = a32(8)
    sgnb = a32(512)
    sgb = a32(512).rearrange("p (g t) -> p g t", g=4)
    sgwTb = a16(512).rearrange("p (g t) -> p g t", g=4)
    pwb = a16(2048).rearrange("p (g i o) -> p g i o", g=4, i=2)
    pscale = a32(8)
    wab = a16(64).rearrange("p (k n) -> p k n", k=8)
    S32 = a32(512).rearrange("p (h v) -> p h v", h=4)
    Sb = a16(512).rearrange("p (h v) -> p h v", h=4)
    chalo = a32(48).rearrange("p (c j) -> p c j", j=4)
    phalo = a32(128).rearrange("p (c t) -> p c t", c=8)
    base_off = off[0]

    cst = a32(5 * 128)
    dma("sp", cst, consts_d, [], ["cst"], "setup1")
    dma("sp", normw, norms_d, [], ["normw"], "setup2")
    dma("sp", convw.rearrange("p c j -> p (c j)"), convw_d, [], ["convw"], "setup3")
    dma("sp", small, small_d, [], ["small"], "setup4")
    dma("sp", dnn[:, 0:1], dnn_d, [], ["dnn"], "setup5")
    dma("sp", sgnb, sgnb_d, [], ["sgnb"], "setup6")
    dma("sp", sgb.rearrange("p g t -> p (g t)"), sgb_d, [], ["sgb"], "setup7")
    dma("sp", pscale, pscale_d, [], ["pscale"], "setup8")
    dma("pool", wab.rearrange("p k n -> p (k n)"), wab_d, [], ["wab"], "setup9")
    dma("pool", pwb.rearrange("p g i o -> p (g i o)"), pw_d, [], ["pwb"], "setup10")
    sgw_stage = a32(512)
    dma("sp", sgw_stage, sgwT_d, [], ["sgwst"], "setup11")
    cp("dve", c_ident, cst[:, 0:128], ["cst"], ["c_ident"])
    cp("dve", c_triu, cst[:, 128:256], ["cst"], ["c_triu"])
    cp("dve", c_trils, cst[:, 256:384], ["cst"], ["c_trils"])
    cp("dve", c_triu128, cst[:, 384:512], ["cst"], ["c_triu128"])
    cp("dve", c_invc.rearrange("p c t -> p (c t)"), cst[:, 512:640], ["cst"], ["c_invc"])
    cp("dve", identb, cst[:, 0:128], ["cst"], ["identb"])
    memset("dve", ones32, 1.0, ["ones32"])
    memset("dve", onesb, 1.0, ["onesb"])
    memset("dve", epst, EPS, ["epst"])
    memset("dve", S32.rearrange("p h v -> p (h v)"), 0.0, ["S32"])
    memset("dve", Sb.rearrange("p h v -> p (h v)"), 0.0, ["Sb"])
    memset("dve", chalo.rearrange("p c j -> p (c j)"), 0.0, ["chalo"])
    memset("dve", phalo.rearrange("p c t -> p (c t)"), 0.0, ["phalo"])
    act(nealog, small[:, 16:32], AF.Exp, ["small"], ["nealog"])
    ts("dve", nealog, nealog, -1.0, ALU.mult, ["nealog"], ["nealog"])
    tt("dve", sgwTb, sgw_stage.rearrange("p (g t) -> p g t", g=4),
       c_triu128.unsqueeze(1).broadcast_to([128, 4, 128]), ALU.mult, ["sgwst", "c_triu128"], ["sgwTb"])
    P.fence()
    off[0] = base_off
    phase_base = base_off

    def phase_reset():
        off[0] = phase_base

    def norm_group(g, nidx, out_ap, out_keys, sq, rstd, bank, bankk):
        gs = slice(g * GT, (g + 1) * GT)
        act(sq, xT[:, :, gs], AF.Square, xk(g), ["sq"])
        for c in range(KC):
            mm(bank[:], onesb, sq[:, c, :], c == 0, c == KC - 1, ["sq", "onesb"], [bankk])
        act(rstd, bank[:], AF.Ln, [bankk, "epst"], ["rstd"], bias=epst[:, 0:1], scale=1.0 / D)
        act(rstd, rstd, AF.Exp, ["rstd"], ["rstd"], scale=-0.5)
        for c in range(KC):
            stt(out_ap[:, c, :], xT[:, c, gs], normw[:, nidx * 8 + c:nidx * 8 + c + 1], rstd, ALU.mult, ALU.mult,
                [("x", g, c), "normw", "rstd"], [out_keys[c]])

    def ffn(fidx, nidx, groups=None):
        groups = list(range(NG)) if groups is None else list(groups)
        phase_reset()
        xn = a16(KC * SEG).rearrange("p (c t) -> p c t", c=KC)
        sq = a16(KC * GT).rearrange("p (c t) -> p c t", c=KC)
        rstd = a32(GT)
        w1 = [a16(4096).rearrange("p (k n) -> p k n", k=8) for _ in range(2)]
        w2 = [a16(2048).rearrange("p (j n) -> p j n", j=2) for _ in range(2)]
        sg = [a32(GT) for _ in range(2)]
        hT = [a16(2 * GT).rearrange("p (j t) -> p j t", j=2) for _ in range(2)]
        P.fence()

        def load(s):
            sl = s % 2
            dma("pool", w1[sl].rearrange("p k n -> p (k n)"), w1h_d[fidx, s], [], ["w1_%d" % sl], "w1_%d" % sl)
            dma("pool", w2[sl], w2h_d[fidx, s * 256:(s + 1) * 256, :].rearrange("(j p) n -> p j n", p=128),
                [], ["w2_%d" % sl], "w2_%d" % sl)

        load(0)
        load(1)
        for g in groups:
            norm_group(g, nidx, xn[:, :, g * GT:(g + 1) * GT], [("xn", g, c) for c in range(KC)], sq, rstd, psb[4], "ps4")
        its = [(s, g) for s in range(NSL) for g in groups]

        def p1(i, j):
            s, g = its[i]
            sl = s % 2
            hb = i % 2
            gs = slice(g * GT, (g + 1) * GT)
            for which in range(2):
                bi = 2 * j + which
                for k in range(KC):
                    mm(psb[bi][:], w1[sl][:, k, which * 256 + j * 128: which * 256 + (j + 1) * 128],
                       xn[:, k, gs], k == 0, k == KC - 1, ["w1_%d" % sl, ("xn", g, k)], ["ps%d" % bi])
            act(sg[j], psb[2 * j][:], AF.Silu, ["ps%d" % (2 * j)], ["sg%d" % j])
            tt("dve", hT[hb][:, j, :], sg[j], psb[2 * j + 1][:], ALU.mult, ["sg%d" % j, "ps%d" % (2 * j + 1)],
               ["hT%d_%d" % (hb, j)])

        def p2(i, half):
            s, g = its[i]
            sl = s % 2
            hb = i % 2
            gs = slice(g * GT, (g + 1) * GT)
            for m in range(4 * half, 4 * half + 4):
                bi = 4 + (m % 4)
                for j in range(2):
                    mm(psb[bi][:], w2[sl][:, j, m * 128:(m + 1) * 128], hT[hb][:, j, :], j == 0, j == 1,
                       ["w2_%d" % sl, "hT%d_%d" % (hb, j)], ["ps%d" % bi])
                stt(xT[:, m, gs], psb[bi][:], 0.5, xT[:, m, gs], ALU.mult, ALU.add, ["ps%d" % bi, ("x", g, m)], [("x", g, m)])

        def scale_w2(s):
            sl = s % 2
            ts("pool", w2[sl].rearrange("p j n -> p (j n)"), w2[sl].rearrange("p j n -> p (j n)"), 0.5, ALU.mult,
               ["w2_%d" % sl], ["w2_%d" % sl])

        p1(0, 0)
        p1(0, 1)
        for i in range(len(its)):
            s = its[i][0]
            nxt_new = i + 1 < len(its) and its[i + 1][0] != s
            if _os0.environ.get("FSPLIT", "1") == "1":
                if i + 1 < len(its):
                    p1(i + 1, 0)
                p2(i, 0)
                if i + 1 < len(its):
                    p1(i + 1, 1)
                p2(i, 1)
            else:
                if i + 1 < len(its):
                    p1(i + 1, 0)
                    p1(i + 1, 1)
                p2(i, 0)
                p2(i, 1)
            if (i + 1 == len(its) or its[i + 1][0] != s) and s + 2 < NSL:
                load(s + 2)
        P.fence()

    def interleave(*gens):
        gens = [g_ for g_ in gens if g_ is not None]
        while gens:
            for g_ in list(gens):
                try:
                    next(g_)
                except StopIteration:
                    gens.remove(g_)

    def run(gen):
        for _ in gen:
            pass

    def pipeline(items, sets, width):
        free = list(sets)
        active = []
        idx = 0
        while idx < len(items) or active:
            while idx < len(items) and len(active) < width:
                fac, needs, on_start = items[idx]
                if needs and not free:
                    break
                if on_start is not None:
                    on_start()
                st = free.pop(0) if needs else None
                g_ = fac(st)
                idx += 1
                try:
                    next(g_)
                    active.append((g_, st))
                except StopIteration:
                    if st is not None:
                        free.append(st)
            for ent in list(active):
                try:
                    next(ent[0])
                except StopIteration:
                    active.remove(ent)
                    if ent[1] is not None:
                        free.append(ent[1])

    def mixer_ab(full_groups=(0, 1, 2, 3), state_groups=()):
        phase_reset()
        h4 = lambda ap: ap.rearrange("p (h c) -> p h c", h=4)
        wsl = [a16(4096).rearrange("p (k n) -> p k n", k=8) for _ in range(2)]
        qnT = h4(a16(4 * GT))
        kT = h4(a16(4 * GT))
        qdT = h4(a16(4 * GT))
        vT = h4(a16(4 * GT))
        zs = h4(a16(4 * GT))
        gsu = h4(a16(4 * GT))
        svt = a16(4 * GT).rearrange("p (a n) -> p a n", a=4)
        abx = a32(32)
        beta = a32(16)
        gT = a32(16)
        gam = a32(16)
        bg = a32(16)
        kts = a32(16)
        egl = a32(32).rearrange("p (a h e) -> p a h e", a=4, h=4)
        AT4 = [h4(a16(512)) for _ in range(4)]
        ktail = [h4(a16(512)) for _ in range(4)]
        u_sb = [h4(a32(512)) for _ in range(4)]
        wT = [h4(a16(512)) for _ in range(4)]
        vnew2 = [h4(a16(512)) for _ in range(2)]

        def prep_set(tag):
            T_ = [h4(a32(512)) for _ in range(7)]
            d_ = dict(tag=tag, gtri=T_[0], Gb=T_[1], EU=T_[2], EL=T_[3], L4=T_[4], U4=T_[5], X4=T_[6],
                      PP1=T_[0], PT1=T_[1], PP0=T_[2], PT0=T_[3])
            d_["kn"] = dict(gtri="T0", Gb="T1", EU="T2", EL="T3", L4="T4", U4="T5", X4="T6",
                            PP1="T0", PT1="T1", PP0="T2", PT0="T3", TT4="TT4", KbG="KbG", Vb="Vb")
            for nm in ("TT4", "KbG", "Vb"):
                d_[nm] = h4(a16(512))
            return d_

        setA = prep_set("A")
        R0 = off[0]
        hn = a16(KC * GT).rearrange("p (c t) -> p c t", c=KC)
        sq = a16(KC * GT).rearrange("p (c t) -> p c t", c=KC)
        rstd = a32(GT)
        SA = []
        for i_ in range(4):
            pre_ = a32(520)
            cv_ = a32(GT)
            SA.append(dict(i=i_, pre=pre_, sqb=pre_.bitcast(BF16)[:, 0:GT], cv=cv_, rs=cv_, tmpf=cv_, sv_=a32(GT), ss4=a32(8)))
        RA_end = off[0]
        off[0] = R0
        sB0 = off[0]
        setB = prep_set("B")
        setC = prep_set("C")
        sB1 = off[0]
        o_sb = h4(a32(4 * GT))
        RB_end = off[0]
        off[0] = sB0
        SC = [dict(i=i_, sqb=a16(GT), rs=a32(GT), tmpf=a32(GT)) for i_ in range(4)]
        assert off[0] <= sB1
        off[0] = max(RB_end, RA_end)
        P.fence()
        for hf in range(2):
            memset("pool", vnew2[hf].rearrange("p h c -> p (h c)"), 0.0, ["vnew%d" % hf])

        ucnt = [0]

        def load_unit(u):
            sl = ucnt[0] % 2
            ucnt[0] += 1
            dma("pool", wsl[sl].rearrange("p k n -> p (k n)"), wmix_d[u], [], ["wsl%d" % sl], "wsl%d" % sl)
            return sl

        bc4 = lambda ap16, t4: ap16[:, t4 * 4:(t4 + 1) * 4].unsqueeze(2).broadcast_to([128, 4, 128])
        m_u = c_triu.unsqueeze(1).broadcast_to([128, 4, 128])
        m_ls = c_trils.unsqueeze(1).broadcast_to([128, 4, 128])
        i4 = c_ident.unsqueeze(1).broadcast_to([128, 4, 128])
        v4 = lambda bank: bank[:].rearrange("p (h c) -> p h c", h=4)

        def gates_chain():
            bk, bkk = nb(2)
            for t4 in range(4):
                for k in range(KC):
                    mm(bk[:, t4 * 8:(t4 + 1) * 8], hn[:, k, t4 * 128:(t4 + 1) * 128], wab[:, k, :], k == 0, k == KC - 1,
                       [("hn", k), "wab"], [bkk])
            yield
            abv = bk[:, 0:32].rearrange("p (a n) -> p a n", a=4)
            act(beta.rearrange("p (a h) -> p a h", a=4), abv[:, :, 0:4], AF.Sigmoid, [bkk], ["beta"])
            xx = abx[:, 0:16]
            ax = abx[:, 16:32]
            tt("dve", xx.rearrange("p (a h) -> p a h", a=4), abv[:, :, 4:8], small[:, 0:16].rearrange("p (a h) -> p a h", a=4),
               ALU.add, [bkk, "small"], ["abx"])
            yield
            stt(ax, xx, -1.0, xx, ALU.mult, ALU.max, ["abx"], ["abx2"])
            yield
            act(ax, ax, AF.Exp, ["abx2"], ["abx2"], scale=-1.0)
            yield
            act(ax, ax, AF.Ln, ["abx2", "ones32"], ["abx2"], bias=ones32[:, 0:1])
            yield
            stt(gT, xx, 0.0, ax, ALU.max, ALU.add, ["abx", "abx2"], ["gT"])
            yield
            tt("dve", gT, gT, nealog, ALU.mult, ["gT", "nealog"], ["gT"])
            yield
            bk, bkk = nb()
            for t4 in range(4):
                mm(bk[:, t4 * 4:(t4 + 1) * 4], c_triu, gT[:, t4 * 4:(t4 + 1) * 4], True, True, ["c_triu", "gT"], [bkk])
            yield
            cp("dve", gam, bk[:, 0:16], [bkk], ["gam"])
            yield
            act(bg, gam, AF.Exp, ["gam"], ["bg"])
            yield
            tt("dve", bg, bg, beta, ALU.mult, ["bg", "beta"], ["bg"])

        def chain_qkv(u, h, sl, T_):
            i_ = T_["i"]
            pre, cv, sv_, sqb, rs = T_["pre"], T_["cv"], T_["sv_"], T_["sqb"], T_["rs"]
            kp, kcv, ksv = ["%s%d" % (n_, i_) for n_ in ("pre", "cv", "sv_")]
            kph, ksq, krs = kp, kp, kcv
            ch = u * 4 + h
            bk, bkk = nb()
            for k in range(KC):
                mm(bk[:], wsl[sl][:, k, h * 128:(h + 1) * 128], hn[:, k, :], k == 0, k == KC - 1,
                   ["wsl%d" % sl, ("hn", k)], [bkk])
            yield
            cp("act", pre[:, 3:515], bk[:], [bkk], [kp])
            cp("act", pre[:, 0:3], chalo[:, ch, 0:3], ["chalo%d" % ch], [kph])
            yield
            ts("dve", cv, pre[:, 3:515], convw[:, ch, 3:4], ALU.mult, [kp, "convw"], [kcv])
            yield
            for j in range(3):
                stt(cv, pre[:, j:j + 512], convw[:, ch, j:j + 1], cv, ALU.mult, ALU.add, [kp, kph, "convw", kcv], [kcv])
                yield
            cp("pool", chalo[:, ch, 0:3], pre[:, 512:515], [kp], ["chalo%d" % ch])
            if u == 2:
                act(vT[:, h, :], cv, AF.Silu, [kcv], [("vT", h)])
                return
            act(sv_, cv, AF.Silu, [kcv], [ksv])
            yield
            tt("dve", sqb, sv_, sv_, ALU.mult, [ksv], [ksq])
            yield
            b2, b2k = nb()
            mm(b2[:], onesb, sqb, True, True, ["onesb", ksq], [b2k])
            yield
            act(rs, b2[:], AF.Ln, [b2k, "epst"], [krs], bias=epst[:, 0:1])
            yield
            act(rs, rs, AF.Exp, [krs], [krs], scale=-0.5)
            yield
            if u == 0:
                stt(qnT[:, h, :], sv_, float(128 ** -0.5), rs, ALU.mult, ALU.mult, [ksv, krs], [("qnT", h)])
            else:
                tt("dve", kT[:, h, :], sv_, rs, ALU.mult, [ksv, krs], [("kT", h)])

        def chain_zsu(u, h, sl):
            bk, bkk = nb()
            for k in range(KC):
                mm(bk[:], wsl[sl][:, k, h * 128:(h + 1) * 128], hn[:, k, :], k == 0, k == KC - 1,
                   ["wsl%d" % sl, ("hn", k)], [bkk])
            yield
            if u == 3:
                act(zs[:, h, :], bk[:], AF.Silu, [bkk], [("zs", h)])
            else:
                act(gsu[:, h, :], bk[:], AF.Gelu, [bkk], [("gsu", h)])

        def chain_sv(t4, sl, T_):
            i_ = T_["i"]
            sv_, tmpf, ss4 = T_["sv_"], T_["tmpf"], T_["ss4"]
            ksv, ktm, kss = ["%s%d" % (n_, i_) for n_ in ("sv_", "cv", "ss4")]
            bk, bkk = nb()
            for k in range(KC):
                mm(bk[:], hn[:, k, t4 * 128:(t4 + 1) * 128], wsl[sl][:, k, :], k == 0, k == KC - 1,
                   [("hn", k), "wsl%d" % sl], [bkk])
            yield
            act(sv_, bk[:], AF.Gelu, [bkk], [ksv])
            yield
            tt("dve", tmpf, sv_, sv_, ALU.mult, [ksv], [ktm])
            yield
            P.add("dve", lambda e: e.tensor_reduce(out=ss4[:, 0:4], in_=tmpf.rearrange("p (a c) -> p a c", a=4),
                                                  axis=mybir.AxisListType.X, op=ALU.add), r=[ktm], w=[kss])
            yield
            act(ss4[:, 0:4], ss4[:, 0:4], AF.Sqrt, [kss, "epst"], [kss], bias=epst[:, 0:1], scale=1.0 / 128)
            yield
            recip(ss4[:, 0:4], ss4[:, 0:4], [kss], [kss])
            yield
            tt("dve", tmpf.rearrange("p (a c) -> p a c", a=4), sv_.rearrange("p (a c) -> p a c", a=4),
               ss4[:, 0:4].unsqueeze(2).broadcast_to([128, 4, 128]), ALU.mult, [ksv, kss], [ktm])
            yield
            tt("pool", svt[:, t4, :], tmpf, sgnb, ALU.mult, [ktm, "sgnb"], [("svt", t4)])

        def prep(t4, S_, full):
            tg = S_["tag"]
            K_ = lambda nm: S_["kn"][nm] + tg
            ob = t4
            ts_ = slice(t4 * 128, (t4 + 1) * 128)
            gtri, Gb, EU, EL, L4, U4, X4 = S_["gtri"], S_["Gb"], S_["EU"], S_["EL"], S_["L4"], S_["U4"], S_["X4"]
            TT4, KbG, Vb = S_["TT4"], S_["KbG"], S_["Vb"]
            tt("dve", gtri, m_u, bc4(gT, t4), ALU.mult, ["c_triu", "gT"], [K_("gtri")])
            yield
            bB, bBk = nb(3)
            mm(bB[:], ones32, gtri.rearrange("p h c -> p (h c)"), True, True, ["ones32", K_("gtri")], [bBk])
            yield
            B3 = v4(bB)
            tt("dve", EU, B3, bc4(gam, t4), ALU.subtract, [bBk, "gam"], [K_("EU")])
            yield
            tt("dve", EL, bc4(gam, t4), B3, ALU.subtract, [bBk, "gam"], [K_("EL")])
            yield
            act(Gb, B3, AF.Exp, [bBk], [K_("Gb")])
            yield
            act(EU, EU, AF.Exp, [K_("EU")], [K_("EU")])
            yield
            act(EL, EL, AF.Exp, [K_("EL")], [K_("EL")])
            yield
            stt(EU, EU, 1.0, m_u, ALU.min, ALU.mult, [K_("EU"), "c_triu"], [K_("EU")])
            yield
            stt(EL, EL, 1.0, m_ls, ALU.min, ALU.mult, [K_("EL"), "c_trils"], [K_("EL")])
            tt("pool", EL, EL, bc4(beta, t4), ALU.mult, [K_("EL"), "beta"], [K_("EL")])
            cp("pool", egl[:, t4, :, :], Gb[:, :, 63:128:64], [K_("Gb")], [("egl", t4)])
            yield
            if full:
                tt("pool", qdT[:, :, ts_], qnT[:, :, ts_], Gb, ALU.mult, [("qnT", h) for h in range(4)] + [K_("Gb")],
                   [("qdT", t4)])
            cp("pool", kts[0:64, t4 * 4:(t4 + 1) * 4], EU[0:64, :, 63], [K_("EU")], [("kts_a", t4)])
            cp("pool", kts[64:128, t4 * 4:(t4 + 1) * 4], EU[64:128, :, 127], [K_("EU")], [("kts_b", t4)])
            yield
            bK, bKk = nb(2)
            for h in range(4):
                mm(bK[:, h * 128:(h + 1) * 128], kT[:, h, ts_], identb, True, True, [("kT", h), "identb"], [bKk])
            yield
            K3 = v4(bK)
            tt("dve", ktail[ob], K3, bc4(kts, t4), ALU.mult, [bKk, ("kts_a", t4), ("kts_b", t4)], [("ktail", ob)])
            yield
            tt("dve", KbG, K3, bc4(bg, t4), ALU.mult, [bKk, "bg"], [K_("KbG")])
            yield
            bV, bVk = nb()
            for h in range(4):
                mm(bV[:, h * 128:(h + 1) * 128], vT[:, h, ts_], identb, True, True, [("vT", h), "identb"], [bVk])
            yield
            tt("dve", Vb, v4(bV), bc4(beta, t4), ALU.mult, [bVk, "beta"], [K_("Vb")])
            yield
            bKK, bKKk = nb()
            for h in range(4):
                mm(bKK[:, h * 128:(h + 1) * 128], kT[:, h, ts_], kT[:, h, ts_], True, True, [("kT", h)], [bKKk])
            if full:
                bKQ, bKQk = nb()
                for h in range(4):
                    mm(bKQ[:, h * 128:(h + 1) * 128], kT[:, h, ts_], qnT[:, h, ts_], True, True, [("kT", h), ("qnT", h)], [bKQk])
            yield
            tt("dve", L4, v4(bKK), EL, ALU.mult, [bKKk, K_("EL")], [K_("L4")])
            yield
            if full:
                tt("dve", AT4[ob], v4(bKQ), EU, ALU.mult, [bKQk, K_("EU")], [("AT4", ob)])
            yield
            bU, bUk = nb()
            for h in range(4):
                if TRP:
                    mmT(bU[:, h * 128:(h + 1) * 128], L4[:, h, :], c_ident, [K_("L4"), "c_ident"], [bUk])
                else:
                    mmr(bU[:, h * 128:(h + 1) * 128], L4[:, h, :], c_ident, True, True, [K_("L4"), "c_ident"], [bUk])
            yield
            cp("act", U4, v4(bU), [bUk], [K_("U4")])
            yield
            tt("dve", X4, i4, U4, ALU.subtract, ["c_ident", K_("U4")], [K_("X4")])
            yield
            PPb = [S_["PP0"], S_["PP1"]]
            PTb = [S_["PT0"], S_["PT1"]]
            K_ = lambda nm: S_["kn"].get(nm, nm) + tg
            cur = {0: (U4, L4, K_("U4"), K_("L4"))}

            def sqr(i):
                Pc, Ptc, Pck, Ptck = cur[i - 1]
                pn, ptn = PPb[i % 2], PTb[i % 2]
                pnk, ptnk = K_("PP%d" % (i % 2)), K_("PT%d" % (i % 2))
                if TRP:
                    b2, b2k = nb()
                    for h in range(4):
                        mmr(b2[:, h * 128:(h + 1) * 128], Pc[:, h, :], Ptc[:, h, :], True, True, [Pck, Ptck], [b2k])
                    cp("act", ptn, v4(b2), [b2k], [ptnk])
                    if i < 5:
                        b1, b1k = nb()
                        for h in range(4):
                            mmT(b1[:, h * 128:(h + 1) * 128], ptn[:, h, :], c_ident, [ptnk, "c_ident"], [b1k])
                        cp("act", pn, v4(b1), [b1k], [pnk])
                else:
                    if i < 5:
                        b1, b1k = nb()
                        for h in range(4):
                            mmr(b1[:, h * 128:(h + 1) * 128], Ptc[:, h, :], Pc[:, h, :], True, True, [Pck, Ptck], [b1k])
                    b2, b2k = nb()
                    for h in range(4):
                        mmr(b2[:, h * 128:(h + 1) * 128], Pc[:, h, :], Ptc[:, h, :], True, True, [Pck, Ptck], [b2k])
                    if i < 5:
                        cp("act", pn, v4(b1), [b1k], [pnk])
                    cp("act", ptn, v4(b2), [b2k], [ptnk])
                cur[i] = (pn, ptn, pnk, ptnk)

            def xpm(i):
                ptn, ptnk = cur[i][1], cur[i][3]
                b3, b3k = nb()
                for h in range(4):
                    mmr(b3[:, h * 128:(h + 1) * 128], ptn[:, h, :], X4[:, h, :], True, True, [ptnk, K_("X4")], [b3k])
                return b3, b3k

            def xadd(i, b3, b3k):
                if i < 5:
                    tt("dve", X4, X4, v4(b3), ALU.add, [K_("X4"), b3k], [K_("X4")])
                else:
                    tt("dve", TT4, X4, v4(b3), ALU.add, [K_("X4"), b3k], [K_("TT4")])

            sqr(1)
            yield
            for i in range(1, 6):
                if i + 1 <= 5:
                    sqr(i + 1)
                    yield
                b3, b3k = xpm(i)
                yield
                xadd(i, b3, b3k)
                yield
            bu, buk = nb()
            for h in range(4):
                mm(bu[:, h * 128:(h + 1) * 128], TT4[:, h, :], Vb[:, h, :], True, True, [K_("TT4"), K_("Vb")], [buk])
            bw, bwk = nb()
            for h in range(4):
                mm(bw[:, h * 128:(h + 1) * 128], KbG[:, h, :], TT4[:, h, :], True, True, [K_("KbG"), K_("TT4")], [bwk])
            yield
            cp("act", u_sb[ob], v4(bu), [buk], [("u_sb", ob)])
            cp("act", wT[ob], v4(bw), [bwk], [("wT", ob)])

        def scan(t4, full):
            ob = t4
            for half in range(2):
                r0 = half * 64
                rs_ = slice(r0, r0 + 64)
                n = t4 * 2 + half
                vnew = vnew2[half]
                vk = "vnew%d" % half
                bpw, bpwk = nb()
                for h in range(4):
                    mm(bpw[:, h * 128:(h + 1) * 128], wT[ob][:, h, :], Sb[:, h, :], True, True, [("wT", ob), "Sb"], [bpwk])
                tt("dve", vnew[rs_], u_sb[ob][rs_], bpw[rs_, :].rearrange("p (h c) -> p h c", h=4), ALU.subtract,
                   [("u_sb", ob), bpwk], [vk])
                if full:
                    bo, bok = nb()
                    for h in range(4):
                        mm(bo[:, h * 64:(h + 1) * 64], Sb[:, h, :], qdT[:, h, n * 64:(n + 1) * 64], True, False,
                           ["Sb", ("qdT", t4)], [bok])
                        mm(bo[:, h * 64:(h + 1) * 64], vnew[:, h, :], AT4[ob][:, h, r0:r0 + 64], False, True,
                           [vk, ("AT4", ob)], [bok])
                    cp("act", o_sb[:, :, n * 64:(n + 1) * 64], bo[:, 0:256].rearrange("p (h c) -> p h c", h=4), [bok],
                       [("o_sb", n)])
                bs, bsk = nb()
                for h in range(4):
                    mm(bs[:, h * 128:(h + 1) * 128], ktail[ob][:, h, :], vnew[:, h, :], True, True, [("ktail", ob), vk], [bsk])
                tt("pool", S32, S32, egl[:, t4, :, half:half + 1].broadcast_to([128, 4, 128]), ALU.mult,
                   ["S32", ("egl", t4)], ["S32"])
                tt("dve", S32, S32, v4(bs), ALU.add, ["S32", bsk], ["S32"])
                cp("act", Sb, S32, ["S32"], ["Sb"])
                yield

        osk = [("o_sb", n) for n in range(8)]

        def chain_onorm(h, T_):
            i_ = T_["i"]
            sqb, rs, tmpf = T_["sqb"], T_["rs"], T_["tmpf"]
            ksq, krs, ktm = ["%s%d" % (n_, i_) for n_ in ("csqb", "crs", "ctmpf")]
            tt("dve", sqb, o_sb[:, h, :], o_sb[:, h, :], ALU.mult, osk, [ksq])
            yield
            b2, b2k = nb()
            mm(b2[:], onesb, sqb, True, True, ["onesb", ksq], [b2k])
            yield
            act(rs, b2[:], AF.Ln, [b2k, "epst"], [krs], bias=epst[:, 0:1], scale=1.0 / 128)
            yield
            act(rs, rs, AF.Exp, [krs], [krs], scale=-0.5)
            yield
            stt(tmpf, o_sb[:, h, :], dnn[:, 0:1], rs, ALU.mult, ALU.mult, osk + ["dnn", krs], [ktm])
            yield
            tt("dve", zs[:, h, :], tmpf, zs[:, h, :], ALU.mult, [ktm, ("zs", h)], [("zs", h)])

        def chain_sgmix(gi, T_):
            i_ = T_["i"]
            tmpf = T_["tmpf"]
            ktm = "ctmpf%d" % i_
            bk, bkk = nb()
            for t4 in range(4):
                mm(bk[:, t4 * 128:(t4 + 1) * 128], svt[:, t4, gi * 128:(gi + 1) * 128], sgwTb[:, gi, :], True, True,
                   [("svt", t4), "sgwTb"], [bkk])
            yield
            tt("dve", tmpf.rearrange("p (a t) -> p a t", a=4), bk[:].rearrange("p (a t) -> p a t", a=4),
               sgb[:, gi, :].unsqueeze(1).broadcast_to([128, 4, 128]), ALU.add, [bkk, "sgb"], [ktm])
            yield
            tt("dve", gsu[:, gi, :], tmpf, gsu[:, gi, :], ALU.mult, [ktm, ("gsu", gi)], [("gsu", gi)])

        groups = sorted(set(full_groups) | set(state_groups))
        MIXCUT = int(_os0.environ.get("MIXCUT", "0"))

        def seqg(*gens):
            for g_ in gens:
                yield from g_

        for g in groups:
            full = g in full_groups
            gs = slice(g * GT, (g + 1) * GT)
            ulist = [0, 1, 2, 3, 4, 5, 6, 7] if full else [1, 2]
            norm_group(g, 1, hn, [("hn", c) for c in range(KC)], sq, rstd, psb[0], "ps0")
            slots = {}
            slots[ulist[0]] = load_unit(ulist[0])
            items = [(lambda st: gates_chain(), False, None)]
            for ui, u in enumerate(ulist):
                if u >= 6:
                    break

                def on_start(ui=ui):
                    if ui + 1 < len(ulist):
                        slots[ulist[ui + 1]] = load_unit(ulist[ui + 1])

                for h in range(4):
                    osf = on_start if h == 0 else None
                    if u <= 2:
                        items.append((lambda st, u=u, h=h: chain_qkv(u, h, slots[u], st), True, osf))
                    elif u <= 4:
                        items.append((lambda st, u=u, h=h: chain_zsu(u, h, slots[u]), False, osf))
                    else:
                        items.append((lambda st, u=u, h=h: chain_sv(h, slots[u], st), True, osf))
            pipeline(items, SA, int(_os0.environ.get("PW", "5")))
            P.fence(dma=False)
            if MIXCUT == 1:
                break
            if _os0.environ.get("SEQB", "0") == "1":
                for t4_ in range(4):
                    run(prep(t4_, [setA, setB, setC, setA][t4_], full))
                    run(scan(t4_, full))
            else:
                interleave(prep(0, setA, full), prep(1, setB, full), prep(2, setC, full))
                interleave(seqg(scan(0, full), scan(1, full), scan(2, full)), prep(3, setA, full))
                run(scan(3, full))
            P.fence(dma=False)
            if MIXCUT == 2:
                break
            if full:
                interleave(*[chain_onorm(h, SC[h]) for h in range(4)])
                interleave(*[chain_sgmix(gi, SC[gi]) for gi in range(4)])
                for uo in range(2):
                    u = 6 + uo
                    if uo == 0:
                        slots[7] = load_unit(7)
                    sl = slots[u]
                    for mq in range(4):
                        m = uo * 4 + mq
                        bk, bkk = nb()
                        for k in range(8):
                            rhs = zs[:, k, :] if k < 4 else gsu[:, k - 4, :]
                            rk = ("zs", k) if k < 4 else ("gsu", k - 4)
                            mm(bk[:], wsl[sl][:, k, mq * 128:(mq + 1) * 128], rhs, k == 0, k == 7, ["wsl%d" % sl, rk], [bkk])
                        tt("dve", xT[:, m, gs], xT[:, m, gs], bk[:], ALU.add, [("x", g, m), bkk], [("x", g, m)])
            P.fence(dma=False)
        P.fence()

    def mixer_pool(first_seg):
        phase_reset()
        HW = 528
        hp = a32(KC * HW).rearrange("p (c t) -> p c t", c=KC)
        bufA = a32(KC * HW).rearrange("p (c t) -> p c t", c=KC)
        bufB = a32(KC * HW).rearrange("p (c t) -> p c t", c=KC)
        pooled = a16(KC * GT).rearrange("p (c t) -> p c t", c=KC)
        sq = a16(KC * GT).rearrange("p (c t) -> p c t", c=KC)
        rstd = a32(GT)
        tmp16 = a32(KC * 16).rearrange("p (c t) -> p c t", c=KC)
        tmpo = a32(GT)
        P.fence()
        for g in range(NG):
            gs = slice(g * GT, (g + 1) * GT)
            cp("pool", hp[:, :, 0:16], phalo, ["phalo"], ["hp_h"])
            norm_group(g, 4, hp[:, :, 16:HW], [("hp", c) for c in range(KC)], sq, rstd, psb[0], "ps0")
            hpk = [("hp", c) for c in range(KC)] + ["hp_h"]
            cp("pool", phalo, hp[:, :, GT:HW], hpk, ["phalo"])
            import os
            CUT = int(os.environ.get("POOLCUT", "9"))
            if CUT <= 1:
                continue
            tt("dve", bufA[:, :, 1:HW], hp[:, :, 1:HW], hp[:, :, 0:HW - 1], ALU.add, hpk, ["bufA", "bufA2"])
            tt("dve", bufB[:, 2:8, 3:HW], bufA[:, 2:8, 3:HW], bufA[:, 2:8, 1:HW - 2], ALU.add, ["bufA"], ["bufB", "bufB2"])
            tt("dve", bufA[:, 4:8, 7:HW], bufB[:, 4:8, 7:HW], bufB[:, 4:8, 3:HW - 4], ALU.add, ["bufB", "bufA"], ["bufA2"])
            tt("dve", bufB[:, 6:8, 15:HW], bufA[:, 6:8, 15:HW], bufA[:, 6:8, 7:HW - 8], ALU.add, ["bufA2", "bufB"], ["bufB2"])
            srcs = [(bufA, ["bufA"]), (bufB, ["bufB"]), (bufA, ["bufA2"]), (bufB, ["bufB2"])]
            if CUT <= 2:
                continue
            for gi in range(4):
                win = 2 ** (gi + 1)
                src, sk = srcs[gi]
                cs = slice(2 * gi, 2 * gi + 2)
                stt(pooled[:, cs, :], src[:, cs, 16:HW], 1.0 / win, hp[:, cs, 16:HW], ALU.mult, ALU.subtract,
                    sk + hpk, [("pooled", gi)])
                if first_seg and g == 0:
                    tt("dve", tmp16[:, cs, :], src[:, cs, 16:32], c_invc[:, cs, :], ALU.mult, sk + ["c_invc"], ["tmp16"])
                    tt("dve", pooled[:, cs, 0:16], tmp16[:, cs, :], hp[:, cs, 16:32], ALU.subtract, ["tmp16"] + hpk,
                       [("pooled", gi)])
            if CUT <= 3:
                continue
            for gi in range(4):
                for oc in range(2):
                    m = 2 * gi + oc
                    bk, bkk = nb()
                    for ic in range(2):
                        mm(bk[:], pwb[:, gi, ic, oc * 128:(oc + 1) * 128], pooled[:, 2 * gi + ic, :], ic == 0, ic == 1,
                           ["pwb", ("pooled", gi)], [bkk])
                    if CUT == 10:
                        stt(xT[:, m, gs], bk[:], pscale[:, m:m + 1], xT[:, m, gs], ALU.mult, ALU.add, [bkk, "pscale", ("x", g, m)], [("x", g, m)])
                    elif CUT == 11:
                        stt(xT[:, m, gs], bk[:], pscale[:, m:m + 1], xT[:, m, gs], ALU.mult, ALU.add, [bkk, "normw", ("x", g, m)], [("x", g, m)])
                    elif CUT == 6:
                        stt(xT[:, m, gs], bk[:], 0.5, xT[:, m, gs], ALU.mult, ALU.add, [bkk, ("x", g, m)], [("x", g, m)])
                    elif CUT == 7:
                        stt(xT[:, m, gs], bk[:], normw[:, m:m + 1], xT[:, m, gs], ALU.mult, ALU.add, [bkk, "normw", ("x", g, m)], [("x", g, m)])
                    elif CUT == 8:
                        act(tmpo, bk[:], AF.Copy, [bkk], ["tmpo"])
                        stt(xT[:, m, gs], tmpo, pscale[:, m:m + 1], xT[:, m, gs], ALU.mult, ALU.add, ["tmpo", "pscale", ("x", g, m)], [("x", g, m)])
                    elif CUT == 4:
                        tt("dve", xT[:, m, gs], xT[:, m, gs], bk[:], ALU.add, [bkk, ("x", g, m)], [("x", g, m)])
                    elif CUT == 5:
                        pass
                    else:
                        act(tmpo, bk[:], AF.Copy, [bkk, "pscale"], ["tmpo"], scale=pscale[:, m:m + 1])
                        tt("dve", xT[:, m, gs], xT[:, m, gs], tmpo, ALU.add, ["tmpo", ("x", g, m)], [("x", g, m)])
        P.fence()

    def final_out(seg):
        phase_reset()
        ob = [a32(KC * GT).rearrange("p (c t) -> p c t", c=KC) for _ in range(2)]
        sq = a16(KC * GT).rearrange("p (c t) -> p c t", c=KC)
        rstd = a32(GT)
        P.fence()
        for g in range(NG):
            b = g % 2
            norm_group(g, 6, ob[b], [("ob", b, c) for c in range(KC)], sq, rstd, psb[0], "ps0")
            t0 = seg * SEG + g * GT
            dma("sp", outT_d[:, t0:t0 + GT].rearrange("(c p) t -> p c t", p=128), ob[b],
                [("ob", b, c) for c in range(KC)], [], "out%d" % b)
        P.fence()

    def pool_halo_only(g):
        phase_reset()
        HW = 528
        hp = a32(KC * HW).rearrange("p (c t) -> p c t", c=KC)
        sq = a16(KC * GT).rearrange("p (c t) -> p c t", c=KC)
        rstd = a32(GT)
        P.fence()
        norm_group(g, 4, hp[:, :, 16:HW], [("hp", c) for c in range(KC)], sq, rstd, psb[0], "ps0")
        cp("pool", phalo, hp[:, :, GT:HW], [("hp", c) for c in range(KC)], ["phalo"])
        P.fence()

    allx = [k for g in range(NG) for k in xk(g)]
    inc = lambda nm: only is None or nm in only
    for ps in range(npre):
        last = ps == npre - 1
        for g_ in range(NG):
            t0_ = ps * SEG + g_ * GT
            dma("sp", xT[:, :, g_ * GT:(g_ + 1) * GT], xp_d[:, t0_:t0_ + GT].rearrange("(c p) t -> p c t", p=128),
                [], xk(g_), "xload%d" % g_)
        ffn(0, 0)
        if not last:
            mixer_ab(full_groups=(), state_groups=(0, 1, 2, 3))
        else:
            mixer_ab(full_groups=(NG - 1,), state_groups=tuple(range(NG - 1)))
            ffn(1, 2, groups=[NG - 1])
            ffn(2, 3, groups=[NG - 1])
            pool_halo_only(NG - 1)
    for seg in range(nseg):
        for g_ in range(NG):
            t0_ = seg * SEG + g_ * GT
            dma("sp", xT[:, :, g_ * GT:(g_ + 1) * GT], xT_d[:, t0_:t0_ + GT].rearrange("(c p) t -> p c t", p=128),
                [], xk(g_), "xload%d" % g_)
        if inc("ffn0"):
            ffn(0, 0)
        if inc("mix0"):
            mixer_ab()
        if inc("ffn1"):
            ffn(1, 2)
        if inc("ffn2"):
            ffn(2, 3)
        if inc("pool"):
            mixer_pool(seg == 0)
        if inc("ffn3"):
            ffn(3, 5)
        final_out(seg)
    P.fence()
    P.add("sp", None)
    P.emit(nc)
    return nc, dbg_outs


def host_prep(inp):
    f = lambda a: np.ascontiguousarray(np.asarray(a, dtype=np.float32))
    w_in = [inp["ffn1_w_in"][0], inp["ffn2_w_in"][0], inp["ffn1_w_in"][1], inp["ffn2_w_in"][1]]
    w_out = [inp["ffn1_w_out"][0], inp["ffn2_w_out"][0], inp["ffn1_w_out"][1], inp["ffn2_w_out"][1]]
    w1h = np.empty((4, NSL, 128, 8, 512), np.float32)
    for i, w in enumerate(w_in):
        w = np.asarray(w, np.float32)
        gate = w[:, :DFF].reshape(8, 128, NSL, 256)
        up = w[:, DFF:].reshape(8, 128, NSL, 256)
        w1h[i, :, :, :, :256] = gate.transpose(2, 1, 0, 3)
        w1h[i, :, :, :, 256:] = up.transpose(2, 1, 0, 3)
    w1h = w1h.reshape(4, NSL, 128, 4096)
    w2h = np.stack([np.asarray(w, np.float32) for w in w_out])
    wi = np.asarray(inp["ab_w_in"][0], np.float32)
    wo = np.asarray(inp["ab_w_out"][0], np.float32)
    bases = [0, 512, 1024, 1536, 2056, 2568]
    wmix = np.empty((8, 128, 8, 512), np.float32)
    for u, b in enumerate(bases):
        wmix[u] = wi[:, b:b + 512].reshape(8, 128, 512).transpose(1, 0, 2)
    for u in range(2):
        wmix[6 + u] = wo[:, u * 512:(u + 1) * 512].reshape(8, 128, 512).transpose(1, 0, 2)
    wmix = wmix.reshape(8, 128, 4096)
    wab = wi[:, 2048:2056].reshape(8, 128, 8).transpose(1, 0, 2).reshape(128, 64)
    nl = [inp["ffn_norm1"][0], inp["mix_norm"][0], inp["ffn_norm2"][0],
          inp["ffn_norm1"][1], inp["mix_norm"][1], inp["ffn_norm2"][1], inp["final_norm"]]
    norms = np.concatenate([np.asarray(v, np.float32).reshape(8, 128).T for v in nl], axis=1)
    convw = np.asarray(inp["dn_conv_w"][0], np.float32).reshape(4, 12, 128).transpose(2, 1, 0).reshape(128, 48)
    small = np.empty((128, 32), np.float32)
    small[:, 0:16] = np.tile(np.asarray(inp["dn_dt_bias"][0], np.float32), 4)[None, :]
    small[:, 16:32] = np.tile(np.asarray(inp["dn_a_log"][0], np.float32), 4)[None, :]
    dnn = np.asarray(inp["dn_out_norm"][0], np.float32).reshape(128, 1)
    sgnb = np.broadcast_to(np.asarray(inp["sg_norm"][0], np.float32).reshape(1, 512), (128, 512))
    sgwT = np.asarray(inp["sg_w"][0], np.float32).transpose(2, 0, 1).reshape(128, 512)
    sgb = np.broadcast_to(np.asarray(inp["sg_b"][0], np.float32).reshape(1, 512), (128, 512))
    pw = np.asarray(inp["pool_w"][0], np.float32).reshape(4, 2, 128, 256).transpose(2, 0, 1, 3).reshape(128, 2048)
    pscale = np.asarray(inp["pool_scale"][0], np.float32).reshape(8, 128).T
    idx = np.arange(128)
    same = (idx[:, None] // 64) == (idx[None, :] // 64)
    ident = np.eye(128, dtype=np.float32)
    triu = ((idx[:, None] <= idx[None, :]) & same).astype(np.float32)
    trils = ((idx[None, :] < idx[:, None]) & same).astype(np.float32)
    triu128 = (idx[:, None] <= idx[None, :]).astype(np.float32)
    invc = np.empty((128, 8, 16), np.float32)
    pos = np.arange(1, 17, dtype=np.float32)
    for c in range(8):
        win = 2 ** (c // 2 + 1)
        invc[:, c, :] = (1.0 / np.minimum(pos, win))[None, :]
    consts = np.concatenate([ident, triu, trils, triu128, invc.reshape(128, 128)], axis=1)
    return dict(w1h=f(w1h), w2h=f(w2h), wmix=f(wmix), wab=f(wab), norms=f(norms), convw=f(convw), small32=f(small),
                dnn=f(dnn), sgnb=f(sgnb), sgwT=f(sgwT), sgb=f(sgb), pw=f(pw), pscale=f(pscale), consts=f(consts))


_CACHE = {}
NPRE = 2
NOWN = 2


def kernel(**inputs):
    x = np.asarray(inputs["x"], np.float32)
    B, T, _ = x.shape
    half_t = T // 2
    assert half_t == NOWN * SEG
    shared = host_prep(inputs)
    key = (NPRE, NOWN)
    if key not in _CACHE:
        _CACHE[key] = build(NOWN, npre=NPRE)[0]
    nc = _CACHE[key]
    consts_a = shared["consts"]
    consts_b = consts_a.copy()
    invc_b = np.empty((128, 8, 16), np.float32)
    for c in range(8):
        invc_b[:, c, :] = 1.0 / (2 ** (c // 2 + 1))
    consts_b[:, 512:640] = invc_b.reshape(128, 128)
    in_maps = []
    for b in range(B):
        for h in range(2):
            m = dict(shared)
            m["xT"] = np.ascontiguousarray(x[b, h * half_t:(h + 1) * half_t].T)
            if h == 0:
                m["xp"] = np.zeros((D, NPRE * SEG), np.float32)
                m["consts"] = consts_a
            else:
                m["xp"] = np.ascontiguousarray(x[b, 0:half_t].T)
                m["consts"] = consts_b
            in_maps.append(m)
    res = run_bass_kernel_spmd(nc, in_maps, core_ids=list(range(2 * B)))
    out = np.empty((B, T, D), np.float32)
    for b in range(B):
        for h in range(2):
            out[b, h * half_t:(h + 1) * half_t] = res.results[2 * b + h]["outT"].T
    return out
```

```python
import numpy as np
import concourse.bass as bass
import concourse.mybir as mybir
from concourse.bass_utils import run_bass_kernel_spmd
from contextlib import ExitStack

F32 = mybir.dt.float32
BF16 = mybir.dt.bfloat16
AF = mybir.ActivationFunctionType
ALU = mybir.AluOpType

import os as _os0
SAME_ENG_SYNC = _os0.environ.get("SES", "1") == "1"
EPOCH = 12000


class Prog:
    CENGS = ("pe", "act", "dve", "pool", "sp")

    def __init__(self):
        self.ops = []
        self.lastw = {}
        self.readers = {}
        self.dma_cnt = {}
        self.pending = {}
        self.read_hook = None

    def fence(self, dma=True):
        last = {}
        for i, op in enumerate(self.ops):
            if op["fn"] is None:
                continue
            if op["dsem"] is not None and (not dma or str(op["dsem"]).startswith("xload")):
                continue
            k = ("d", op["dsem"]) if op["dsem"] is not None else ("e", op["eng"])
            last[k] = i
        self.pending = {e: set(last.values()) for e in self.CENGS}

    def add(self, eng, fn, r=(), w=(), dsem=None):
        i = len(self.ops)
        deps = set()
        if eng != "pe" and self.read_hook is not None:
            for k in r:
                if isinstance(k, str) and k.startswith("ps"):
                    self.read_hook(k)
        if eng != "pe":
            w = list(w) + [k for k in r if isinstance(k, str) and k.startswith("ps") and k not in w]
        if eng in self.pending:
            deps |= self.pending.pop(eng)
        for k in r:
            d = self.lastw.get(k)
            if d is not None:
                deps.add(d)
        for k in w:
            d = self.lastw.get(k)
            if d is not None:
                deps.add(d)
            for d in self.readers.get(k, ()):
                deps.add(d)
        for k in r:
            self.readers.setdefault(k, []).append(i)
        for k in w:
            self.lastw[k] = i
            self.readers[k] = []
        op = dict(eng=eng, fn=fn, deps=deps, dsem=dsem, signal=False, val=None, key=None)
        if dsem is not None:
            self.dma_cnt[dsem] = self.dma_cnt.get(dsem, 0) + 1
            op["val"] = 16 * self.dma_cnt[dsem]
            op["key"] = ("d", dsem)
        self.ops.append(op)
        return i

    def finalize(self):
        ops = self.ops
        for op in ops:
            keep = set()
            for d in op["deps"]:
                dop = ops[d]
                if dop["dsem"] is None:
                    if dop["eng"] == op["eng"] and op["dsem"] is None:
                        if op["eng"] == "pe" or not SAME_ENG_SYNC:
                            continue
                    dop["signal"] = True
                keep.add(d)
            op["deps"] = keep
        cnt = {e: 0 for e in self.CENGS}
        self.ekeys = set()
        for op in ops:
            if op["dsem"] is None and op["signal"]:
                c = cnt[op["eng"]]
                cnt[op["eng"]] += 1
                op["key"] = ("e", op["eng"], c // EPOCH)
                op["val"] = c % EPOCH + 1
                self.ekeys.add(op["key"])
        wm = {e: {} for e in self.CENGS}
        for op in ops:
            need = {}
            for d in op["deps"]:
                dop = ops[d]
                need[dop["key"]] = max(need.get(dop["key"], 0), dop["val"])
            waits = []
            for key, v in need.items():
                if wm[op["eng"]].get(key, 0) >= v:
                    continue
                wm[op["eng"]][key] = v
                waits.append((key, v))
            op["waits"] = waits

    def emit(self, nc):
        self.finalize()
        with ExitStack() as es:
            sems = {}
            for k in sorted(self.ekeys):
                sems[k] = es.enter_context(nc.semaphore("s_%s_%d" % (k[1], k[2])))
            for k in self.dma_cnt:
                sems[("d", k)] = es.enter_context(nc.semaphore("d_%s" % k))
            block = es.enter_context(nc.Block())

            def run(ename):
                def f(eng):
                    for op in self.ops:
                        if op["eng"] != ename:
                            continue
                        for key, v in op["waits"]:
                            eng.wait_ge(sems[key], v)
                        if op["fn"] is None:
                            continue
                        ins = op["fn"](eng)
                        if op["dsem"] is not None:
                            ins.then_inc(sems[op["key"]], 16)
                        elif op["signal"]:
                            ins.then_inc(sems[op["key"]], 1)
                return f

            block.tensor(run("pe"))
            block.scalar(run("act"))
            block.vector(run("dve"))
            block.gpsimd(run("pool"))
            block.sync(run("sp"))


D = 1024
KC = 8
DFF = 2816
NSL = 11
SEG = 2048
GT = 512
NG = SEG // GT
EPS = 1e-6
ARENA_WORDS = 53200


def build(nseg, dbg_names=(), only=None, npre=0):
    nc = bass.Bass("TRN2", target_bir_lowering=False)
    NTOK = nseg * SEG

    def din(name, shape):
        return nc.dram_tensor(name, list(shape), F32, kind="ExternalInput").ap()

    xT_d = din("xT", [D, NTOK])
    xp_d = din("xp", [D, max(npre, 1) * SEG])
    w1h_d = din("w1h", [4, NSL, 128, 4096])
    w2h_d = din("w2h", [4, DFF, D])
    wmix_d = din("wmix", [8, 128, 4096])
    wab_d = din("wab", [128, 64])
    norms_d = din("norms", [128, 56])
    convw_d = din("convw", [128, 48])
    small_d = din("small32", [128, 32])
    dnn_d = din("dnn", [128, 1])
    sgnb_d = din("sgnb", [128, 512])
    sgwT_d = din("sgwT", [128, 512])
    sgb_d = din("sgb", [128, 512])
    pw_d = din("pw", [128, 2048])
    pscale_d = din("pscale", [128, 8])
    consts_d = din("consts", [128, 4 * 128 + 128])
    outT_d = nc.dram_tensor("outT", [D, NTOK], F32, kind="ExternalOutput").ap()

    big = nc.alloc_sbuf_tensor("arena", [128, ARENA_WORDS], F32)
    off = [0]

    def a32(n):
        n = (n + 7) // 8 * 8
        v = big[:, off[0]:off[0] + n]
        off[0] += n
        assert off[0] <= ARENA_WORDS, ("arena overflow", off[0])
        return v

    def a16(n):
        w = (n // 2 + 7) // 8 * 8
        v = big[:, off[0]:off[0] + w].bitcast(BF16)[:, 0:n]
        off[0] += w
        assert off[0] <= ARENA_WORDS, ("arena overflow", off[0])
        return v

    psb = [nc.alloc_psum_tensor("ps%d" % i, [128, 512], F32) for i in range(8)]
    P = Prog()
    dbg_outs = {}

    def mm(out, lhsT, rhs, start, stop, r, w):
        P.add("pe", lambda e: e.matmul(out, lhsT=lhsT, rhs=rhs, start=start, stop=stop), r=r, w=w)

    TRP = _os0.environ.get("TRP", "0") == "1"

    def mmT(out, in_, ident, r, w):
        P.add("pe", lambda e: e.transpose(out, in_, ident), r=r, w=w)

    F32R = mybir.dt.float32r
    USE_R = _os0.environ.get("FP32R", "0") == "1"

    def mmr(out, lhsT, rhs, start, stop, r, w):
        if USE_R:
            lhsT = lhsT.bitcast(F32R)
            rhs = rhs.bitcast(F32R)
        P.add("pe", lambda e: e.matmul(out, lhsT=lhsT, rhs=rhs, start=start, stop=stop), r=r, w=w)

    def act(out, in_, func, r, w, bias=None, scale=1.0):
        if bias is None:
            P.add("act", lambda e: e.activation(out=out, in_=in_, func=func, scale=scale), r=r, w=w)
        else:
            P.add("act", lambda e: e.activation(out=out, in_=in_, func=func, bias=bias, scale=scale), r=r, w=w)

    def tt(eng, out, in0, in1, op, r, w):
        P.add(eng, lambda e: e.tensor_tensor(out=out, in0=in0, in1=in1, op=op), r=r, w=w)

    def ts(eng, out, in0, s1, op0, r, w, s2=None, op1=None):
        if op1 is None:
            P.add(eng, lambda e: e.tensor_scalar(out=out, in0=in0, scalar1=s1, scalar2=None, op0=op0), r=r, w=w)
        else:
            P.add(eng, lambda e: e.tensor_scalar(out=out, in0=in0, scalar1=s1, scalar2=s2, op0=op0, op1=op1), r=r, w=w)

    def stt(out, in0, scalar, in1, op0, op1, r, w):
        P.add("dve", lambda e: e.scalar_tensor_tensor(out=out, in0=in0, scalar=scalar, in1=in1, op0=op0, op1=op1), r=r, w=w)

    def cp(eng, out, in_, r, w):
        if eng == "act":
            P.add("act", lambda e: e.copy(out=out, in_=in_), r=r, w=w)
        else:
            P.add(eng, lambda e: e.tensor_copy(out=out, in_=in_), r=r, w=w)

    def recip(out, in_, r, w):
        P.add("dve", lambda e: e.reciprocal(out=out, in_=in_), r=r, w=w)

    def dma(q, out, in_, r, w, dsem):
        P.add(q, lambda e: e.dma_start(out=out, in_=in_), r=r, w=w, dsem=dsem)

    def memset(eng, ap, val, w):
        P.add(eng, lambda e: e.memset(ap, val), w=w)

    dbg_cnt = [0]

    def dbg(name, ap, r):
        if name not in dbg_names or name in dbg_outs:
            return
        shp = list(ap.shape)
        fl = 1
        for s in shp[1:]:
            fl *= s
        t = nc.dram_tensor("dbg_" + name, [shp[0], fl], ap.dtype, kind="ExternalOutput").ap()
        dbg_outs[name] = t
        view = t
        if len(shp) == 3:
            view = t.rearrange("p (a b) -> p a b", a=shp[1])
        elif len(shp) == 4:
            view = t.rearrange("p (a b c) -> p a b c", a=shp[1], b=shp[2])
        dma("sp", view, ap, r, [], "dbg%d" % dbg_cnt[0])
        dbg_cnt[0] += 1
        P.add("sp", None, r=[], w=[])
        P.ops[-1]["deps"] = {len(P.ops) - 2}

    rr = [0]

    hold = {}

    def nb(n=1):
        for _ in range(8):
            i = rr[0] % 8
            rr[0] += 1
            if hold.get("ps%d" % i, 0) == 0:
                hold["ps%d" % i] = n
                return psb[i], "ps%d" % i
        raise RuntimeError("no free PSUM bank")

    def _rh(k):
        if hold.get(k, 0) > 0:
            hold[k] -= 1

    P.read_hook = _rh

    xT = a32(KC * SEG).rearrange("p (c t) -> p c t", c=KC)

    def xk(g):
        return [("x", g, m) for m in range(KC)]

    c_ident = a32(128)
    c_triu = a32(128)
    c_trils = a32(128)
    c_triu128 = a32(128)
    c_invc = a32(128).rearrange("p (c t) -> p c t", c=8)
    ones32 = a32(128)
    onesb = a16(128)
    identb = a16(128)
    epst = a32(8)
    normw = a32(56)
    convw = a32(48).rearrange("p (c j) -> p c j", j=4)
    small = a32(32)
    nealog = a32(16)
    dnn = a32(8)
    sgnb = a32(512)
    sgb = a32(512).rearrange("p (g t) -> p g t", g=4)
    sgwTb = a16(512).rearrange("p (g t) -> p g t", g=4)
    pwb = a16(2048).rearrange("p (g i o) -> p g i o", g=4, i=2)
    pscale = a32(8)
    wab = a16(64).rearrange("p (k n) -> p k n", k=8)
    S32 = a32(512).rearrange("p (h v) -> p h v", h=4)
    Sb = a16(512).rearrange("p (h v) -> p h v", h=4)
    chalo = a32(48).rearrange("p (c j) -> p c j", j=4)
    phalo = a32(128).rearrange("p (c t) -> p c t", c=8)
    base_off = off[0]

    cst = a32(5 * 128)
    dma("sp", cst, consts_d, [], ["cst"], "setup1")
    dma("sp", normw, norms_d, [], ["normw"], "setup2")
    dma("sp", convw.rearrange("p c j -> p (c j)"), convw_d, [], ["convw"], "setup3")
    dma("sp", small, small_d, [], ["small"], "setup4")
    dma("sp", dnn[:, 0:1], dnn_d, [], ["dnn"], "setup5")
    dma("sp", sgnb, sgnb_d, [], ["sgnb"], "setup6")
    dma("sp", sgb.rearrange("p g t -> p (g t)"), sgb_d, [], ["sgb"], "setup7")
    dma("sp", pscale, pscale_d, [], ["pscale"], "setup8")
    dma("pool", wab.rearrange("p k n -> p (k n)"), wab_d, [], ["wab"], "setup9")
    dma("pool", pwb.rearrange("p g i o -> p (g i o)"), pw_d, [], ["pwb"], "setup10")
    sgw_stage = a32(512)
    dma("sp", sgw_stage, sgwT_d, [], ["sgwst"], "setup11")
    cp("dve", c_ident, cst[:, 0:128], ["cst"], ["c_ident"])
    cp("dve", c_triu, cst[:, 128:256], ["cst"], ["c_triu"])
    cp("dve", c_trils, cst[:, 256:384], ["cst"], ["c_trils"])
    cp("dve", c_triu128, cst[:, 384:512], ["cst"], ["c_triu128"])
    cp("dve", c_invc.rearrange("p c t -> p (c t)"), cst[:, 512:640], ["cst"], ["c_invc"])
    cp("dve", identb, cst[:, 0:128], ["cst"], ["identb"])
    memset("dve", ones32, 1.0, ["ones32"])
    memset("dve", onesb, 1.0, ["onesb"])
    memset("dve", epst, EPS, ["epst"])
    memset("dve", S32.rearrange("p h v -> p (h v)"), 0.0, ["S32"])
    memset("dve", Sb.rearrange("p h v -> p (h v)"), 0.0, ["Sb"])
    memset("dve", chalo.rearrange("p c j -> p (c j)"), 0.0, ["chalo"])
    memset("dve", phalo.rearrange("p c t -> p (c t)"), 0.0, ["phalo"])
    act(nealog, small[:, 16:32], AF.Exp, ["small"], ["nealog"])
    ts("dve", nealog, nealog, -1.0, ALU.mult, ["nealog"], ["nealog"])
    tt("dve", sgwTb, sgw_stage.rearrange("p (g t) -> p g t", g=4),
       c_triu128.unsqueeze(1).broadcast_to([128, 4, 128]), ALU.mult, ["sgwst", "c_triu128"], ["sgwTb"])
    P.fence()
    off[0] = base_off
    phase_base = base_off

    def phase_reset():
        off[0] = phase_base

    def norm_group(g, nidx, out_ap, out_keys, sq, rstd, bank, bankk):
        gs = slice(g * GT, (g + 1) * GT)
        act(sq, xT[:, :, gs], AF.Square, xk(g), ["sq"])
        for c in range(KC):
            mm(bank[:], onesb, sq[:, c, :], c == 0, c == KC - 1, ["sq", "onesb"], [bankk])
        act(rstd, bank[:], AF.Ln, [bankk, "epst"], ["rstd"], bias=epst[:, 0:1], scale=1.0 / D)
        act(rstd, rstd, AF.Exp, ["rstd"], ["rstd"], scale=-0.5)
        for c in range(KC):
            stt(out_ap[:, c, :], xT[:, c, gs], normw[:, nidx * 8 + c:nidx * 8 + c + 1], rstd, ALU.mult, ALU.mult,
                [("x", g, c), "normw", "rstd"], [out_keys[c]])

    def ffn(fidx, nidx, groups=None):
        groups = list(range(NG)) if groups is None else list(groups)
        phase_reset()
        xn = a16(KC * SEG).rearrange("p (c t) -> p c t", c=KC)
        sq = a16(KC * GT).rearrange("p (c t) -> p c t", c=KC)
        rstd = a32(GT)
        w1 = [a16(4096).rearrange("p (k n) -> p k n", k=8) for _ in range(2)]
        w2 = [a16(2048).rearrange("p (j n) -> p j n", j=2) for _ in range(2)]
        sg = [a32(GT) for _ in range(2)]
        hT = [a16(2 * GT).rearrange("p (j t) -> p j t", j=2) for _ in range(2)]
        P.fence()

        def load(s):
            sl = s % 2
            dma("pool", w1[sl].rearrange("p k n -> p (k n)"), w1h_d[fidx, s], [], ["w1_%d" % sl], "w1_%d" % sl)
            dma("pool", w2[sl], w2h_d[fidx, s * 256:(s + 1) * 256, :].rearrange("(j p) n -> p j n", p=128),
                [], ["w2_%d" % sl], "w2_%d" % sl)

        load(0)
        load(1)
        for g in groups:
            norm_group(g, nidx, xn[:, :, g * GT:(g + 1) * GT], [("xn", g, c) for c in range(KC)], sq, rstd, psb[4], "ps4")
        its = [(s, g) for s in range(NSL) for g in groups]

        def p1(i, j):
            s, g = its[i]
            sl = s % 2
            hb = i % 2
            gs = slice(g * GT, (g + 1) * GT)
            for which in range(2):
                bi = 2 * j + which
                for k in range(KC):
                    mm(psb[bi][:], w1[sl][:, k, which * 256 + j * 128: which * 256 + (j + 1) * 128],
                       xn[:, k, gs], k == 0, k == KC - 1, ["w1_%d" % sl, ("xn", g, k)], ["ps%d" % bi])
            act(sg[j], psb[2 * j][:], AF.Silu, ["ps%d" % (2 * j)], ["sg%d" % j])
            tt("dve", hT[hb][:, j, :], sg[j], psb[2 * j + 1][:], ALU.mult, ["sg%d" % j, "ps%d" % (2 * j + 1)],
               ["hT%d_%d" % (hb, j)])

        def p2(i, half):
            s, g = its[i]
            sl = s % 2
            hb = i % 2
            gs = slice(g * GT, (g + 1) * GT)
            for m in range(4 * half, 4 * half + 4):
                bi = 4 + (m % 4)
                for j in range(2):
                    mm(psb[bi][:], w2[sl][:, j, m * 128:(m + 1) * 128], hT[hb][:, j, :], j == 0, j == 1,
                       ["w2_%d" % sl, "hT%d_%d" % (hb, j)], ["ps%d" % bi])
                stt(xT[:, m, gs], psb[bi][:], 0.5, xT[:, m, gs], ALU.mult, ALU.add, ["ps%d" % bi, ("x", g, m)], [("x", g, m)])

        def scale_w2(s):
            sl = s % 2
            ts("pool", w2[sl].rearrange("p j n -> p (j n)"), w2[sl].rearrange("p j n -> p (j n)"), 0.5, ALU.mult,
               ["w2_%d" % sl], ["w2_%d" % sl])

        p1(0, 0)
        p1(0, 1)
        for i in range(len(its)):
            s = its[i][0]
            nxt_new = i + 1 < len(its) and its[i + 1][0] != s
            if _os0.environ.get("FSPLIT", "1") == "1":
                if i + 1 < len(its):
                    p1(i + 1, 0)
                p2(i, 0)
                if i + 1 < len(its):
                    p1(i + 1, 1)
                p2(i, 1)
            else:
                if i + 1 < len(its):
                    p1(i + 1, 0)
                    p1(i + 1, 1)
                p2(i, 0)
                p2(i, 1)
            if (i + 1 == len(its) or its[i + 1][0] != s) and s + 2 < NSL:
                load(s + 2)
        P.fence()

    def interleave(*gens):
        gens = [g_ for g_ in gens if g_ is not None]
        while gens:
            for g_ in list(gens):
                try:
                    next(g_)
                except StopIteration:
                    gens.remove(g_)

    def run(gen):
        for _ in gen:
            pass

    def pipeline(items, sets, width):
        free = list(sets)
        active = []
        idx = 0
        while idx < len(items) or active:
            while idx < len(items) and len(active) < width:
                fac, needs, on_start = items[idx]
                if needs and not free:
                    break
                if on_start is not None:
                    on_start()
                st = free.pop(0) if needs else None
                g_ = fac(st)
                idx += 1
                try:
                    next(g_)
                    active.append((g_, st))
                except StopIteration:
                    if st is not None:
                        free.append(st)
            for ent in list(active):
                try:
                    next(ent[0])
                except StopIteration:
                    active.remove(ent)
                    if ent[1] is not None:
                        free.append(ent[1])

    def mixer_ab(full_groups=(0, 1, 2, 3), state_groups=()):
        phase_reset()
        h4 = lambda ap: ap.rearrange("p (h c) -> p h c", h=4)
        wsl = [a16(4096).rearrange("p (k n) -> p k n", k=8) for _ in range(2)]
        qnT = h4(a16(4 * GT))
        kT = h4(a16(4 * GT))
        qdT = h4(a16(4 * GT))
        vT = h4(a16(4 * GT))
        zs = h4(a16(4 * GT))
        gsu = h4(a16(4 * GT))
        svt = a16(4 * GT).rearrange("p (a n) -> p a n", a=4)
        abx = a32(32)
        beta = a32(16)
        gT = a32(16)
        gam = a32(16)
        bg = a32(16)
        kts = a32(16)
        egl = a32(32).rearrange("p (a h e) -> p a h e", a=4, h=4)
        AT4 = [h4(a16(512)) for _ in range(4)]
        ktail = [h4(a16(512)) for _ in range(4)]
        u_sb = [h4(a32(512)) for _ in range(4)]
        wT = [h4(a16(512)) for _ in range(4)]
        vnew2 = [h4(a16(512)) for _ in range(2)]

        def prep_set(tag):
            T_ = [h4(a32(512)) for _ in range(7)]
            d_ = dict(tag=tag, gtri=T_[0], Gb=T_[1], EU=T_[2], EL=T_[3], L4=T_[4], U4=T_[5], X4=T_[6],
                      PP1=T_[0], PT1=T_[1], PP0=T_[2], PT0=T_[3])
            d_["kn"] = dict(gtri="T0", Gb="T1", EU="T2", EL="T3", L4="T4", U4="T5", X4="T6",
                            PP1="T0", PT1="T1", PP0="T2", PT0="T3", TT4="TT4", KbG="KbG", Vb="Vb")
            for nm in ("TT4", "KbG", "Vb"):
                d_[nm] = h4(a16(512))
            return d_

        setA = prep_set("A")
        R0 = off[0]
        hn = a16(KC * GT).rearrange("p (c t) -> p c t", c=KC)
        sq = a16(KC * GT).rearrange("p (c t) -> p c t", c=KC)
        rstd = a32(GT)
        SA = []
        for i_ in range(4):
            pre_ = a32(520)
            cv_ = a32(GT)
            SA.append(dict(i=i_, pre=pre_, sqb=pre_.bitcast(BF16)[:, 0:GT], cv=cv_, rs=cv_, tmpf=cv_, sv_=a32(GT), ss4=a32(8)))
        RA_end = off[0]
        off[0] = R0
        sB0 = off[0]
        setB = prep_set("B")
        setC = prep_set("C")
        sB1 = off[0]
        o_sb = h4(a32(4 * GT))
        RB_end = off[0]
        off[0] = sB0
        SC = [dict(i=i_, sqb=a16(GT), rs=a32(GT), tmpf=a32(GT)) for i_ in range(4)]
        assert off[0] <= sB1
        off[0] = max(RB_end, RA_end)
        P.fence()
        for hf in range(2):
            memset("pool", vnew2[hf].rearrange("p h c -> p (h c)"), 0.0, ["vnew%d" % hf])

        ucnt = [0]

        def load_unit(u):
            sl = ucnt[0] % 2
            ucnt[0] += 1
            dma("pool", wsl[sl].rearrange("p k n -> p (k n)"), wmix_d[u], [], ["wsl%d" % sl], "wsl%d" % sl)
            return sl

        bc4 = lambda ap16, t4: ap16[:, t4 * 4:(t4 + 1) * 4].unsqueeze(2).broadcast_to([128, 4, 128])
        m_u = c_triu.unsqueeze(1).broadcast_to([128, 4, 128])
        m_ls = c_trils.unsqueeze(1).broadcast_to([128, 4, 128])
        i4 = c_ident.unsqueeze(1).broadcast_to([128, 4, 128])
        v4 = lambda bank: bank[:].rearrange("p (h c) -> p h c", h=4)

        def gates_chain():
            bk, bkk = nb(2)
            for t4 in range(4):
                for k in range(KC):
                    mm(bk[:, t4 * 8:(t4 + 1) * 8], hn[:, k, t4 * 128:(t4 + 1) * 128], wab[:, k, :], k == 0, k == KC - 1,
                       [("hn", k), "wab"], [bkk])
            yield
            abv = bk[:, 0:32].rearrange("p (a n) -> p a n", a=4)
            act(beta.rearrange("p (a h) -> p a h", a=4), abv[:, :, 0:4], AF.Sigmoid, [bkk], ["beta"])
            xx = abx[:, 0:16]
            ax = abx[:, 16:32]
            tt("dve", xx.rearrange("p (a h) -> p a h", a=4), abv[:, :, 4:8], small[:, 0:16].rearrange("p (a h) -> p a h", a=4),
               ALU.add, [bkk, "small"], ["abx"])
            yield
            stt(ax, xx, -1.0, xx, ALU.mult, ALU.max, ["abx"], ["abx2"])
            yield
            act(ax, ax, AF.Exp, ["abx2"], ["abx2"], scale=-1.0)
            yield
            act(ax, ax, AF.Ln, ["abx2", "ones32"], ["abx2"], bias=ones32[:, 0:1])
            yield
            stt(gT, xx, 0.0, ax, ALU.max, ALU.add, ["abx", "abx2"], ["gT"])
            yield
            tt("dve", gT, gT, nealog, ALU.mult, ["gT", "nealog"], ["gT"])
            yield
            bk, bkk = nb()
            for t4 in range(4):
                mm(bk[:, t4 * 4:(t4 + 1) * 4], c_triu, gT[:, t4 * 4:(t4 + 1) * 4], True, True, ["c_triu", "gT"], [bkk])
            yield
            cp("dve", gam, bk[:, 0:16], [bkk], ["gam"])
            yield
            act(bg, gam, AF.Exp, ["gam"], ["bg"])
            yield
            tt("dve", bg, bg, beta, ALU.mult, ["bg", "beta"], ["bg"])

        def chain_qkv(u, h, sl, T_):
            i_ = T_["i"]
            pre, cv, sv_, sqb, rs = T_["pre"], T_["cv"], T_["sv_"], T_["sqb"], T_["rs"]
            kp, kcv, ksv = ["%s%d" % (n_, i_) for n_ in ("pre", "cv", "sv_")]
            kph, ksq, krs = kp, kp, kcv
            ch = u * 4 + h
            bk, bkk = nb()
            for k in range(KC):
                mm(bk[:], wsl[sl][:, k, h * 128:(h + 1) * 128], hn[:, k, :], k == 0, k == KC - 1,
                   ["wsl%d" % sl, ("hn", k)], [bkk])
            yield
            cp("act", pre[:, 3:515], bk[:], [bkk], [kp])
            cp("act", pre[:, 0:3], chalo[:, ch, 0:3], ["chalo%d" % ch], [kph])
            yield
            ts("dve", cv, pre[:, 3:515], convw[:, ch, 3:4], ALU.mult, [kp, "convw"], [kcv])
            yield
            for j in range(3):
                stt(cv, pre[:, j:j + 512], convw[:, ch, j:j + 1], cv, ALU.mult, ALU.add, [kp, kph, "convw", kcv], [kcv])
                yield
            cp("pool", chalo[:, ch, 0:3], pre[:, 512:515], [kp], ["chalo%d" % ch])
            if u == 2:
                act(vT[:, h, :], cv, AF.Silu, [kcv], [("vT", h)])
                return
            act(sv_, cv, AF.Silu, [kcv], [ksv])
            yield
            tt("dve", sqb, sv_, sv_, ALU.mult, [ksv], [ksq])
            yield
            b2, b2k = nb()
            mm(b2[:], onesb, sqb, True, True, ["onesb", ksq], [b2k])
            yield
            act(rs, b2[:], AF.Ln, [b2k, "epst"], [krs], bias=epst[:, 0:1])
            yield
            act(rs, rs, AF.Exp, [krs], [krs], scale=-0.5)
            yield
            if u == 0:
                stt(qnT[:, h, :], sv_, float(128 ** -0.5), rs, ALU.mult, ALU.mult, [ksv, krs], [("qnT", h)])
            else:
                tt("dve", kT[:, h, :], sv_, rs, ALU.mult, [ksv, krs], [("kT", h)])

        def chain_zsu(u, h, sl):
            bk, bkk = nb()
            for k in range(KC):
                mm(bk[:], wsl[sl][:, k, h * 128:(h + 1) * 128], hn[:, k, :], k == 0, k == KC - 1,
                   ["wsl%d" % sl, ("hn", k)], [bkk])
            yield
            if u == 3:
                act(zs[:, h, :], bk[:], AF.Silu, [bkk], [("zs", h)])
            else:
                act(gsu[:, h, :], bk[:], AF.Gelu, [bkk], [("gsu", h)])

        def chain_sv(t4, sl, T_):
            i_ = T_["i"]
            sv_, tmpf, ss4 = T_["sv_"], T_["tmpf"], T_["ss4"]
            ksv, ktm, kss = ["%s%d" % (n_, i_) for n_ in ("sv_", "cv", "ss4")]
            bk, bkk = nb()
            for k in range(KC):
                mm(bk[:], hn[:, k, t4 * 128:(t4 + 1) * 128], wsl[sl][:, k, :], k == 0, k == KC - 1,
                   [("hn", k), "wsl%d" % sl], [bkk])
            yield
            act(sv_, bk[:], AF.Gelu, [bkk], [ksv])
            yield
            tt("dve", tmpf, sv_, sv_, ALU.mult, [ksv], [ktm])
            yield
            P.add("dve", lambda e: e.tensor_reduce(out=ss4[:, 0:4], in_=tmpf.rearrange("p (a c) -> p a c", a=4),
                                                  axis=mybir.AxisListType.X, op=ALU.add), r=[ktm], w=[kss])
            yield
            act(ss4[:, 0:4], ss4[:, 0:4], AF.Sqrt, [kss, "epst"], [kss], bias=epst[:, 0:1], scale=1.0 / 128)
            yield
            recip(ss4[:, 0:4], ss4[:, 0:4], [kss], [kss])
            yield
            tt("dve", tmpf.rearrange("p (a c) -> p a c", a=4), sv_.rearrange("p (a c) -> p a c", a=4),
               ss4[:, 0:4].unsqueeze(2).broadcast_to([128, 4, 128]), ALU.mult, [ksv, kss], [ktm])
            yield
            tt("pool", svt[:, t4, :], tmpf, sgnb, ALU.mult, [ktm, "sgnb"], [("svt", t4)])

        def prep(t4, S_, full):
            tg = S_["tag"]
            K_ = lambda nm: S_["kn"][nm] + tg
            ob = t4
            ts_ = slice(t4 * 128, (t4 + 1) * 128)
            gtri, Gb, EU, EL, L4, U4, X4 = S_["gtri"], S_["Gb"], S_["EU"], S_["EL"], S_["L4"], S_["U4"], S_["X4"]
            TT4, KbG, Vb = S_["TT4"], S_["KbG"], S_["Vb"]
            tt("dve", gtri, m_u, bc4(gT, t4), ALU.mult, ["c_triu", "gT"], [K_("gtri")])
            yield
            bB, bBk = nb(3)
            mm(bB[:], ones32, gtri.rearrange("p h c -> p (h c)"), True, True, ["ones32", K_("gtri")], [bBk])
            yield
            B3 = v4(bB)
            tt("dve", EU, B3, bc4(gam, t4), ALU.subtract, [bBk, "gam"], [K_("EU")])
            yield
            tt("dve", EL, bc4(gam, t4), B3, ALU.subtract, [bBk, "gam"], [K_("EL")])
            yield
            act(Gb, B3, AF.Exp, [bBk], [K_("Gb")])
            yield
            act(EU, EU, AF.Exp, [K_("EU")], [K_("EU")])
            yield
            act(EL, EL, AF.Exp, [K_("EL")], [K_("EL")])
            yield
            stt(EU, EU, 1.0, m_u, ALU.min, ALU.mult, [K_("EU"), "c_triu"], [K_("EU")])
            yield
            stt(EL, EL, 1.0, m_ls, ALU.min, ALU.mult, [K_("EL"), "c_trils"], [K_("EL")])
            tt("pool", EL, EL, bc4(beta, t4), ALU.mult, [K_("EL"), "beta"], [K_("EL")])
            cp("pool", egl[:, t4, :, :], Gb[:, :, 63:128:64], [K_("Gb")], [("egl", t4)])
            yield
            if full:
                tt("pool", qdT[:, :, ts_], qnT[:, :, ts_], Gb, ALU.mult, [("qnT", h) for h in range(4)] + [K_("Gb")],
                   [("qdT", t4)])
            cp("pool", kts[0:64, t4 * 4:(t4 + 1) * 4], EU[0:64, :, 63], [K_("EU")], [("kts_a", t4)])
            cp("pool", kts[64:128, t4 * 4:(t4 + 1) * 4], EU[64:128, :, 127], [K_("EU")], [("kts_b", t4)])
            yield
            bK, bKk = nb(2)
            for h in range(4):
                mm(bK[:, h * 128:(h + 1) * 128], kT[:, h, ts_], identb, True, True, [("kT", h), "identb"], [bKk])
            yield
            K3 = v4(bK)
            tt("dve", ktail[ob], K3, bc4(kts, t4), ALU.mult, [bKk, ("kts_a", t4), ("kts_b", t4)], [("ktail", ob)])
            yield
            tt("dve", KbG, K3, bc4(bg, t4), ALU.mult, [bKk, "bg"], [K_("KbG")])
            yield
            bV, bVk = nb()
            for h in range(4):
                mm(bV[:, h * 128:(h + 1) * 128], vT[:, h, ts_], identb, True, True, [("vT", h), "identb"], [bVk])
            yield
            tt("dve", Vb, v4(bV), bc4(beta, t4), ALU.mult, [bVk, "beta"], [K_("Vb")])
            yield
            bKK, bKKk = nb()
            for h in range(4):
                mm(bKK[:, h * 128:(h + 1) * 128], kT[:, h, ts_], kT[:, h, ts_], True, True, [("kT", h)], [bKKk])
            if full:
                bKQ, bKQk = nb()
                for h in range(4):
                    mm(bKQ[:, h * 128:(h + 1) * 128], kT[:, h, ts_], qnT[:, h, ts_], True, True, [("kT", h), ("qnT", h)], [bKQk])
            yield
            tt("dve", L4, v4(bKK), EL, ALU.mult, [bKKk, K_("EL")], [K_("L4")])
            yield
            if full:
                tt("dve", AT4[ob], v4(bKQ), EU, ALU.mult, [bKQk, K_("EU")], [("AT4", ob)])
            yield
            bU, bUk = nb()
            for h in range(4):
                if TRP:
                    mmT(bU[:, h * 128:(h + 1) * 128], L4[:, h, :], c_ident, [K_("L4"), "c_ident"], [bUk])
                else:
                    mmr(bU[:, h * 128:(h + 1) * 128], L4[:, h, :], c_ident, True, True, [K_("L4"), "c_ident"], [bUk])
            yield
            cp("act", U4, v4(bU), [bUk], [K_("U4")])
            yield
            tt("dve", X4, i4, U4, ALU.subtract, ["c_ident", K_("U4")], [K_("X4")])
            yield
            K_ = lambda nm: S_["kn"].get(nm, nm) + tg
            hb4 = lambda ap32: ap32.rearrange("p h c -> p (h c)").bitcast(BF16).rearrange("p (a h c) -> p a h c", a=2, h=4)
            T0b = hb4(S_["PP1"])
            T2b = hb4(S_["PP0"])
            Pb = [T2b[:, 0], T0b[:, 0]]
            Ptb = [T2b[:, 1], T0b[:, 1]]
            Pbk = [K_("PP0"), K_("PP1")]
            PT1f = S_["PT1"]
            Xb = hb4(S_["PT0"])[:, 0]
            kX, kXb, kPT1 = K_("X4"), K_("PT0"), K_("PT1")

            def sqr(i):
                if i == 1:
                    b1, b1k = nb()
                    for h in range(4):
                        mmr(b1[:, h * 128:(h + 1) * 128], L4[:, h, :], U4[:, h, :], True, True, [K_("U4"), K_("L4")], [b1k])
                    b2, b2k = nb(2)
                    for h in range(4):
                        mmr(b2[:, h * 128:(h + 1) * 128], U4[:, h, :], L4[:, h, :], True, True, [K_("U4"), K_("L4")], [b2k])
                    cp("act", Pb[1], v4(b1), [b1k], [Pbk[1]])
                    cp("act", PT1f, v4(b2), [b2k], [kPT1])
                    cp("dve", Ptb[1], v4(b2), [b2k], [Pbk[1]])
                    return
                src = (i - 1) % 2
                dst = i % 2
                if i < 5:
                    b1, b1k = nb()
                    for h in range(4):
                        mm(b1[:, h * 128:(h + 1) * 128], Ptb[src][:, h, :], Pb[src][:, h, :], True, True, [Pbk[src]], [b1k])
                b2, b2k = nb()
                for h in range(4):
                    mm(b2[:, h * 128:(h + 1) * 128], Pb[src][:, h, :], Ptb[src][:, h, :], True, True, [Pbk[src]], [b2k])
                if i < 5:
                    cp("act", Pb[dst], v4(b1), [b1k], [Pbk[dst]])
                cp("act", Ptb[dst], v4(b2), [b2k], [Pbk[dst]])

            def xpm(i):
                b3, b3k = nb()
                for h in range(4):
                    if i == 1:
                        mmr(b3[:, h * 128:(h + 1) * 128], PT1f[:, h, :], X4[:, h, :], True, True, [kPT1, kX], [b3k])
                    else:
                        mm(b3[:, h * 128:(h + 1) * 128], Ptb[i % 2][:, h, :], Xb[:, h, :], True, True, [Pbk[i % 2], kXb], [b3k])
                return b3, b3k

            def xadd(i, b3, b3k):
                if i < 5:
                    tt("dve", X4, X4, v4(b3), ALU.add, [kX, b3k], [kX])
                    cp("act", Xb, X4, [kX], [kXb])
                else:
                    tt("dve", TT4, X4, v4(b3), ALU.add, [kX, b3k], [K_("TT4")])

            sqr(1)
            yield
            for i in range(1, 6):
                if i + 1 <= 5:
                    sqr(i + 1)
                    yield
                b3, b3k = xpm(i)
                yield
                xadd(i, b3, b3k)
                yield
            bu, buk = nb()
            for h in range(4):
                mm(bu[:, h * 128:(h + 1) * 128], TT4[:, h, :], Vb[:, h, :], True, True, [K_("TT4"), K_("Vb")], [buk])
            bw, bwk = nb()
            for h in range(4):
                mm(bw[:, h * 128:(h + 1) * 128], KbG[:, h, :], TT4[:, h, :], True, True, [K_("KbG"), K_("TT4")], [bwk])
            yield
            cp("act", u_sb[ob], v4(bu), [buk], [("u_sb", ob)])
            cp("act", wT[ob], v4(bw), [bwk], [("wT", ob)])

        def scan(t4, full):
            ob = t4
            for half in range(2):
                r0 = half * 64
                rs_ = slice(r0, r0 + 64)
                n = t4 * 2 + half
                vnew = vnew2[half]
                vk = "vnew%d" % half
                bpw, bpwk = nb()
                for h in range(4):
                    mm(bpw[:, h * 128:(h + 1) * 128], wT[ob][:, h, :], Sb[:, h, :], True, True, [("wT", ob), "Sb"], [bpwk])
                tt("dve", vnew[rs_], u_sb[ob][rs_], bpw[rs_, :].rearrange("p (h c) -> p h c", h=4), ALU.subtract,
                   [("u_sb", ob), bpwk], [vk])
                if full:
                    bo, bok = nb()
                    for h in range(4):
                        mm(bo[:, h * 64:(h + 1) * 64], Sb[:, h, :], qdT[:, h, n * 64:(n + 1) * 64], True, False,
                           ["Sb", ("qdT", t4)], [bok])
                        mm(bo[:, h * 64:(h + 1) * 64], vnew[:, h, :], AT4[ob][:, h, r0:r0 + 64], False, True,
                           [vk, ("AT4", ob)], [bok])
                    cp("act", o_sb[:, :, n * 64:(n + 1) * 64], bo[:, 0:256].rearrange("p (h c) -> p h c", h=4), [bok],
                       [("o_sb", n)])
                bs, bsk = nb()
                for h in range(4):
                    mm(bs[:, h * 128:(h + 1) * 128], ktail[ob][:, h, :], vnew[:, h, :], True, True, [("ktail", ob), vk], [bsk])
                tt("pool", S32, S32, egl[:, t4, :, half:half + 1].broadcast_to([128, 4, 128]), ALU.mult,
                   ["S32", ("egl", t4)], ["S32"])
                tt("dve", S32, S32, v4(bs), ALU.add, ["S32", bsk], ["S32"])
                cp("act", Sb, S32, ["S32"], ["Sb"])
                yield

        osk = [("o_sb", n) for n in range(8)]

        def chain_onorm(h, T_):
            i_ = T_["i"]
            sqb, rs, tmpf = T_["sqb"], T_["rs"], T_["tmpf"]
            ksq, krs, ktm = ["%s%d" % (n_, i_) for n_ in ("csqb", "crs", "ctmpf")]
            tt("dve", sqb, o_sb[:, h, :], o_sb[:, h, :], ALU.mult, osk, [ksq])
            yield
            b2, b2k = nb()
            mm(b2[:], onesb, sqb, True, True, ["onesb", ksq], [b2k])
            yield
            act(rs, b2[:], AF.Ln, [b2k, "epst"], [krs], bias=epst[:, 0:1], scale=1.0 / 128)
            yield
            act(rs, rs, AF.Exp, [krs], [krs], scale=-0.5)
            yield
            stt(tmpf, o_sb[:, h, :], dnn[:, 0:1], rs, ALU.mult, ALU.mult, osk + ["dnn", krs], [ktm])
            yield
            tt("dve", zs[:, h, :], tmpf, zs[:, h, :], ALU.mult, [ktm, ("zs", h)], [("zs", h)])

        def chain_sgmix(gi, T_):
            i_ = T_["i"]
            tmpf = T_["tmpf"]
            ktm = "ctmpf%d" % i_
            bk, bkk = nb()
            for t4 in range(4):
                mm(bk[:, t4 * 128:(t4 + 1) * 128], svt[:, t4, gi * 128:(gi + 1) * 128], sgwTb[:, gi, :], True, True,
                   [("svt", t4), "sgwTb"], [bkk])
            yield
            tt("dve", tmpf.rearrange("p (a t) -> p a t", a=4), bk[:].rearrange("p (a t) -> p a t", a=4),
               sgb[:, gi, :].unsqueeze(1).broadcast_to([128, 4, 128]), ALU.add, [bkk, "sgb"], [ktm])
            yield
            tt("dve", gsu[:, gi, :], tmpf, gsu[:, gi, :], ALU.mult, [ktm, ("gsu", gi)], [("gsu", gi)])

        groups = sorted(set(full_groups) | set(state_groups))
        MIXCUT = int(_os0.environ.get("MIXCUT", "0"))

        def seqg(*gens):
            for g_ in gens:
                yield from g_

        for g in groups:
            full = g in full_groups
            gs = slice(g * GT, (g + 1) * GT)
            ulist = [0, 1, 2, 3, 4, 5, 6, 7] if full else [1, 2]
            norm_group(g, 1, hn, [("hn", c) for c in range(KC)], sq, rstd, psb[0], "ps0")
            slots = {}
            slots[ulist[0]] = load_unit(ulist[0])
            items = [(lambda st: gates_chain(), False, None)]
            for ui, u in enumerate(ulist):
                if u >= 6:
                    break

                def on_start(ui=ui):
                    if ui + 1 < len(ulist):
                        slots[ulist[ui + 1]] = load_unit(ulist[ui + 1])

                for h in range(4):
                    osf = on_start if h == 0 else None
                    if u <= 2:
                        items.append((lambda st, u=u, h=h: chain_qkv(u, h, slots[u], st), True, osf))
                    elif u <= 4:
                        items.append((lambda st, u=u, h=h: chain_zsu(u, h, slots[u]), False, osf))
                    else:
                        items.append((lambda st, u=u, h=h: chain_sv(h, slots[u], st), True, osf))
            pipeline(items, SA, int(_os0.environ.get("PW", "5")))
            P.fence(dma=False)
            if MIXCUT == 1:
                break
            if _os0.environ.get("SEQB", "0") == "1":
                for t4_ in range(4):
                    run(prep(t4_, [setA, setB, setC, setA][t4_], full))
                    run(scan(t4_, full))
            else:
                interleave(prep(0, setA, full), prep(1, setB, full), prep(2, setC, full))
                interleave(seqg(scan(0, full), scan(1, full), scan(2, full)), prep(3, setA, full))
                run(scan(3, full))
            P.fence(dma=False)
            if MIXCUT == 2:
                break
            if full:
                interleave(*[chain_onorm(h, SC[h]) for h in range(4)])
                interleave(*[chain_sgmix(gi, SC[gi]) for gi in range(4)])
                for uo in range(2):
                    u = 6 + uo
                    if uo == 0:
                        slots[7] = load_unit(7)
                    sl = slots[u]
                    for mq in range(4):
                        m = uo * 4 + mq
                        bk, bkk = nb()
                        for k in range(8):
                            rhs = zs[:, k, :] if k < 4 else gsu[:, k - 4, :]
                            rk = ("zs", k) if k < 4 else ("gsu", k - 4)
                            mm(bk[:], wsl[sl][:, k, mq * 128:(mq + 1) * 128], rhs, k == 0, k == 7, ["wsl%d" % sl, rk], [bkk])
                        tt("dve", xT[:, m, gs], xT[:, m, gs], bk[:], ALU.add, [("x", g, m), bkk], [("x", g, m)])
            P.fence(dma=False)
        P.fence()

    def mixer_pool(first_seg):
        phase_reset()
        HW = 528
        hp = a32(KC * HW).rearrange("p (c t) -> p c t", c=KC)
        bufA = a32(KC * HW).rearrange("p (c t) -> p c t", c=KC)
        bufB = a32(KC * HW).rearrange("p (c t) -> p c t", c=KC)
        pooled = a16(KC * GT).rearrange("p (c t) -> p c t", c=KC)
        sq = a16(KC * GT).rearrange("p (c t) -> p c t", c=KC)
        rstd = a32(GT)
        tmp16 = a32(KC * 16).rearrange("p (c t) -> p c t", c=KC)
        tmpo = a32(GT)
        P.fence()
        for g in range(NG):
            gs = slice(g * GT, (g + 1) * GT)
            cp("pool", hp[:, :, 0:16], phalo, ["phalo"], ["hp_h"])
            norm_group(g, 4, hp[:, :, 16:HW], [("hp", c) for c in range(KC)], sq, rstd, psb[0], "ps0")
            hpk = [("hp", c) for c in range(KC)] + ["hp_h"]
            cp("pool", phalo, hp[:, :, GT:HW], hpk, ["phalo"])
            import os
            CUT = int(os.environ.get("POOLCUT", "9"))
            if CUT <= 1:
                continue
            tt("dve", bufA[:, :, 1:HW], hp[:, :, 1:HW], hp[:, :, 0:HW - 1], ALU.add, hpk, ["bufA", "bufA2"])
            tt("dve", bufB[:, 2:8, 3:HW], bufA[:, 2:8, 3:HW], bufA[:, 2:8, 1:HW - 2], ALU.add, ["bufA"], ["bufB", "bufB2"])
            tt("dve", bufA[:, 4:8, 7:HW], bufB[:, 4:8, 7:HW], bufB[:, 4:8, 3:HW - 4], ALU.add, ["bufB", "bufA"], ["bufA2"])
            tt("dve", bufB[:, 6:8, 15:HW], bufA[:, 6:8, 15:HW], bufA[:, 6:8, 7:HW - 8], ALU.add, ["bufA2", "bufB"], ["bufB2"])
            srcs = [(bufA, ["bufA"]), (bufB, ["bufB"]), (bufA, ["bufA2"]), (bufB, ["bufB2"])]
            if CUT <= 2:
                continue
            for gi in range(4):
                win = 2 ** (gi + 1)
                src, sk = srcs[gi]
                cs = slice(2 * gi, 2 * gi + 2)
                stt(pooled[:, cs, :], src[:, cs, 16:HW], 1.0 / win, hp[:, cs, 16:HW], ALU.mult, ALU.subtract,
                    sk + hpk, [("pooled", gi)])
                if first_seg and g == 0:
                    tt("dve", tmp16[:, cs, :], src[:, cs, 16:32], c_invc[:, cs, :], ALU.mult, sk + ["c_invc"], ["tmp16"])
                    tt("dve", pooled[:, cs, 0:16], tmp16[:, cs, :], hp[:, cs, 16:32], ALU.subtract, ["tmp16"] + hpk,
                       [("pooled", gi)])
            if CUT <= 3:
                continue
            for gi in range(4):
                for oc in range(2):
                    m = 2 * gi + oc
                    bk, bkk = nb()
                    for ic in range(2):
                        mm(bk[:], pwb[:, gi, ic, oc * 128:(oc + 1) * 128], pooled[:, 2 * gi + ic, :], ic == 0, ic == 1,
                           ["pwb", ("pooled", gi)], [bkk])
                    if CUT == 10:
                        stt(xT[:, m, gs], bk[:], pscale[:, m:m + 1], xT[:, m, gs], ALU.mult, ALU.add, [bkk, "pscale", ("x", g, m)], [("x", g, m)])
                    elif CUT == 11:
                        stt(xT[:, m, gs], bk[:], pscale[:, m:m + 1], xT[:, m, gs], ALU.mult, ALU.add, [bkk, "normw", ("x", g, m)], [("x", g, m)])
                    elif CUT == 6:
                        stt(xT[:, m, gs], bk[:], 0.5, xT[:, m, gs], ALU.mult, ALU.add, [bkk, ("x", g, m)], [("x", g, m)])
                    elif CUT == 7:
                        stt(xT[:, m, gs], bk[:], normw[:, m:m + 1], xT[:, m, gs], ALU.mult, ALU.add, [bkk, "normw", ("x", g, m)], [("x", g, m)])
                    elif CUT == 8:
                        act(tmpo, bk[:], AF.Copy, [bkk], ["tmpo"])
                        stt(xT[:, m, gs], tmpo, pscale[:, m:m + 1], xT[:, m, gs], ALU.mult, ALU.add, ["tmpo", "pscale", ("x", g, m)], [("x", g, m)])
                    elif CUT == 4:
                        tt("dve", xT[:, m, gs], xT[:, m, gs], bk[:], ALU.add, [bkk, ("x", g, m)], [("x", g, m)])
                    elif CUT == 5:
                        pass
                    else:
                        act(tmpo, bk[:], AF.Copy, [bkk, "pscale"], ["tmpo"], scale=pscale[:, m:m + 1])
                        tt("dve", xT[:, m, gs], xT[:, m, gs], tmpo, ALU.add, ["tmpo", ("x", g, m)], [("x", g, m)])
        P.fence()

    def final_out(seg):
        phase_reset()
        ob = [a32(KC * GT).rearrange("p (c t) -> p c t", c=KC) for _ in range(2)]
        sq = a16(KC * GT).rearrange("p (c t) -> p c t", c=KC)
        rstd = a32(GT)
        P.fence()
        for g in range(NG):
            b = g % 2
            norm_group(g, 6, ob[b], [("ob", b, c) for c in range(KC)], sq, rstd, psb[0], "ps0")
            t0 = seg * SEG + g * GT
            dma("sp", outT_d[:, t0:t0 + GT].rearrange("(c p) t -> p c t", p=128), ob[b],
                [("ob", b, c) for c in range(KC)], [], "out%d" % b)
        P.fence()

    def pool_halo_only(g):
        phase_reset()
        HW = 528
        hp = a32(KC * HW).rearrange("p (c t) -> p c t", c=KC)
        sq = a16(KC * GT).rearrange("p (c t) -> p c t", c=KC)
        rstd = a32(GT)
        P.fence()
        norm_group(g, 4, hp[:, :, 16:HW], [("hp", c) for c in range(KC)], sq, rstd, psb[0], "ps0")
        cp("pool", phalo, hp[:, :, GT:HW], [("hp", c) for c in range(KC)], ["phalo"])
        P.fence()

    allx = [k for g in range(NG) for k in xk(g)]
    inc = lambda nm: only is None or nm in only
    for ps in range(npre):
        last = ps == npre - 1
        for g_ in range(NG):
            t0_ = ps * SEG + g_ * GT
            dma("sp", xT[:, :, g_ * GT:(g_ + 1) * GT], xp_d[:, t0_:t0_ + GT].rearrange("(c p) t -> p c t", p=128),
                [], xk(g_), "xload%d" % g_)
        ffn(0, 0)
        if not last:
            mixer_ab(full_groups=(), state_groups=(0, 1, 2, 3))
        else:
            mixer_ab(full_groups=(NG - 1,), state_groups=tuple(range(NG - 1)))
            ffn(1, 2, groups=[NG - 1])
            ffn(2, 3, groups=[NG - 1])
            pool_halo_only(NG - 1)
    for seg in range(nseg):
        for g_ in range(NG):
            t0_ = seg * SEG + g_ * GT
            dma("sp", xT[:, :, g_ * GT:(g_ + 1) * GT], xT_d[:, t0_:t0_ + GT].rearrange("(c p) t -> p c t", p=128),
                [], xk(g_), "xload%d" % g_)
        if inc("ffn0"):
            ffn(0, 0)
        if inc("mix0"):
            mixer_ab()
        if inc("ffn1"):
            ffn(1, 2)
        if inc("ffn2"):
            ffn(2, 3)
        if inc("pool"):
            mixer_pool(seg == 0)
        if inc("ffn3"):
            ffn(3, 5)
        final_out(seg)
    P.fence()
    P.add("sp", None)
    P.emit(nc)
    return nc, dbg_outs


def host_prep(inp):
    f = lambda a: np.ascontiguousarray(np.asarray(a, dtype=np.float32))
    w_in = [inp["ffn1_w_in"][0], inp["ffn2_w_in"][0], inp["ffn1_w_in"][1], inp["ffn2_w_in"][1]]
    w_out = [inp["ffn1_w_out"][0], inp["ffn2_w_out"][0], inp["ffn1_w_out"][1], inp["ffn2_w_out"][1]]
    w1h = np.empty((4, NSL, 128, 8, 512), np.float32)
    for i, w in enumerate(w_in):
        w = np.asarray(w, np.float32)
        gate = w[:, :DFF].reshape(8, 128, NSL, 256)
        up = w[:, DFF:].reshape(8, 128, NSL, 256)
        w1h[i, :, :, :, :256] = gate.transpose(2, 1, 0, 3)
        w1h[i, :, :, :, 256:] = up.transpose(2, 1, 0, 3)
    w1h = w1h.reshape(4, NSL, 128, 4096)
    w2h = np.stack([np.asarray(w, np.float32) for w in w_out])
    wi = np.asarray(inp["ab_w_in"][0], np.float32)
    wo = np.asarray(inp["ab_w_out"][0], np.float32)
    bases = [0, 512, 1024, 1536, 2056, 2568]
    wmix = np.empty((8, 128, 8, 512), np.float32)
    for u, b in enumerate(bases):
        wmix[u] = wi[:, b:b + 512].reshape(8, 128, 512).transpose(1, 0, 2)
    for u in range(2):
        wmix[6 + u] = wo[:, u * 512:(u + 1) * 512].reshape(8, 128, 512).transpose(1, 0, 2)
    wmix = wmix.reshape(8, 128, 4096)
    wab = wi[:, 2048:2056].reshape(8, 128, 8).transpose(1, 0, 2).reshape(128, 64)
    nl = [inp["ffn_norm1"][0], inp["mix_norm"][0], inp["ffn_norm2"][0],
          inp["ffn_norm1"][1], inp["mix_norm"][1], inp["ffn_norm2"][1], inp["final_norm"]]
    norms = np.concatenate([np.asarray(v, np.float32).reshape(8, 128).T for v in nl], axis=1)
    convw = np.asarray(inp["dn_conv_w"][0], np.float32).reshape(4, 12, 128).transpose(2, 1, 0).reshape(128, 48)
    small = np.empty((128, 32), np.float32)
    small[:, 0:16] = np.tile(np.asarray(inp["dn_dt_bias"][0], np.float32), 4)[None, :]
    small[:, 16:32] = np.tile(np.asarray(inp["dn_a_log"][0], np.float32), 4)[None, :]
    dnn = np.asarray(inp["dn_out_norm"][0], np.float32).reshape(128, 1)
    sgnb = np.broadcast_to(np.asarray(inp["sg_norm"][0], np.float32).reshape(1, 512), (128, 512))
    sgwT = np.asarray(inp["sg_w"][0], np.float32).transpose(2, 0, 1).reshape(128, 512)
    sgb = np.broadcast_to(np.asarray(inp["sg_b"][0], np.float32).reshape(1, 512), (128, 512))
    pw = np.asarray(inp["pool_w"][0], np.float32).reshape(4, 2, 128, 256).transpose(2, 0, 1, 3).reshape(128, 2048)
    pscale = np.asarray(inp["pool_scale"][0], np.float32).reshape(8, 128).T
    idx = np.arange(128)
    same = (idx[:, None] // 64) == (idx[None, :] // 64)
    ident = np.eye(128, dtype=np.float32)
    triu = ((idx[:, None] <= idx[None, :]) & same).astype(np.float32)
    trils = ((idx[None, :] < idx[:, None]) & same).astype(np.float32)
    triu128 = (idx[:, None] <= idx[None, :]).astype(np.float32)
    invc = np.empty((128, 8, 16), np.float32)
    pos = np.arange(1, 17, dtype=np.float32)
    for c in range(8):
        win = 2 ** (c // 2 + 1)
        invc[:, c, :] = (1.0 / np.minimum(pos, win))[None, :]
    consts = np.concatenate([ident, triu, trils, triu128, invc.reshape(128, 128)], axis=1)
    return dict(w1h=f(w1h), w2h=f(w2h), wmix=f(wmix), wab=f(wab), norms=f(norms), convw=f(convw), small32=f(small),
                dnn=f(dnn), sgnb=f(sgnb), sgwT=f(sgwT), sgb=f(sgb), pw=f(pw), pscale=f(pscale), consts=f(consts))


_CACHE = {}
NPRE = 2
NOWN = 2


def kernel(**inputs):
    x = np.asarray(inputs["x"], np.float32)
    B, T, _ = x.shape
    half_t = T // 2
    assert half_t == NOWN * SEG
    shared = host_prep(inputs)
    key = (NPRE, NOWN)
    if key not in _CACHE:
        _CACHE[key] = build(NOWN, npre=NPRE)[0]
    nc = _CACHE[key]
    consts_a = shared["consts"]
    consts_b = consts_a.copy()
    invc_b = np.empty((128, 8, 16), np.float32)
    for c in range(8):
        invc_b[:, c, :] = 1.0 / (2 ** (c // 2 + 1))
    consts_b[:, 512:640] = invc_b.reshape(128, 128)
    in_maps = []
    for b in range(B):
        for h in range(2):
            m = dict(shared)
            m["xT"] = np.ascontiguousarray(x[b, h * half_t:(h + 1) * half_t].T)
            if h == 0:
                m["xp"] = np.zeros((D, NPRE * SEG), np.float32)
                m["consts"] = consts_a
            else:
                m["xp"] = np.ascontiguousarray(x[b, 0:half_t].T)
                m["consts"] = consts_b
            in_maps.append(m)
    res = run_bass_kernel_spmd(nc, in_maps, core_ids=list(range(2 * B)))
    out = np.empty((B, T, D), np.float32)
    for b in range(B):
        for h in range(2):
            out[b, h * half_t:(h + 1) * half_t] = res.results[2 * b + h]["outT"].T
    return out
```
